# Optimizing a Trainium2 kernel written in Bass

```python
import jax, jax.numpy as jnp
from jax import lax
import numpy as np

D_MODEL = 2048
BATCH = 2
SEQ = 8192
DEPTH = 1

CHUNK = 64
GDN_HEADS = 16
GDN_DK = 128
GDN_DV = 128
GDN_CONV = 4
QK_W = GDN_HEADS * GDN_DK
V_W = GDN_HEADS * GDN_DV
QKV_W = 2 * QK_W + V_W
POOL_WINDOWS = (2, 4, 8, 16)
POOL_WIDTH = D_MODEL // 2
POOL_GROUP = POOL_WIDTH // 4
XA_HEADS = 4
XA_HEAD_DIM = D_MODEL // XA_HEADS
MEM_LEN = 256
D_FF = 5504
FFN_CONV = 3
EPS = 1e-6
IN_SIZES = (QKV_W, V_W, GDN_HEADS, GDN_HEADS, POOL_WIDTH, D_MODEL, D_MODEL)
D_IN = QKV_W + V_W + 2 * GDN_HEADS + POOL_WIDTH + 2 * D_MODEL

kernel_name = "hybrid_gdn_pool_xattn_convffn_block"


def rms_norm(x, w):
    xf = x.astype(jnp.float32)
    y = xf * lax.rsqrt(jnp.mean(xf * xf, axis=-1, keepdims=True) + EPS)
    return (y * w.astype(jnp.float32)).astype(x.dtype)


def l2_norm(x):
    return x * lax.rsqrt(jnp.sum(x * x, axis=-1, keepdims=True) + EPS)


def causal_dwconv(x, w):
    K = w.shape[0]
    S = x.shape[1]
    xp = jnp.pad(x, ((0, 0), (K - 1, 0), (0, 0)))
    y = xp[:, K - 1:K - 1 + S] * w[K - 1]
    for j in range(K - 1):
        y = y + xp[:, j:j + S] * w[j]
    return y


def gated_delta_rule(q, k, v, g, beta):
    B, S, H, DK = q.shape
    DV = v.shape[-1]
    N = S // CHUNK

    def blocks(t):
        return t.reshape(B, N, CHUNK, H, -1).transpose(0, 3, 1, 2, 4)

    q, k, v = blocks(q), blocks(k), blocks(v)
    g = blocks(g[..., None])[..., 0]
    beta = blocks(beta[..., None])[..., 0]
    G = jnp.cumsum(g, axis=-1)
    pos = jnp.arange(CHUNK)
    incl = pos[:, None] >= pos[None, :]
    strict = pos[:, None] > pos[None, :]
    gap = G[..., :, None] - G[..., None, :]
    decay = jnp.where(incl, jnp.exp(jnp.where(incl, gap, 0.0)), 0.0)
    kk = jnp.einsum('bhnid,bhnjd->bhnij', k, k)
    lower = jnp.where(strict, beta[..., :, None] * decay * kk, 0.0)
    system = lower + jnp.eye(CHUNK, dtype=lower.dtype)
    rhs = jnp.concatenate([beta[..., None] * v, (beta * jnp.exp(G))[..., None] * k], axis=-1)
    sol = lax.linalg.triangular_solve(system, rhs, left_side=True, lower=True,
                                      unit_diagonal=True)
    u_v, w_k = sol[..., :DV], sol[..., DV:]
    attn = decay * jnp.einsum('bhnid,bhnjd->bhnij', q, k)
    q_dec = q * jnp.exp(G)[..., None]
    k_dec = k * jnp.exp(G[..., -1:] - G)[..., None]
    chunk_decay = jnp.exp(G[..., -1])

    def step(state, xs):
        u_c, w_c, a_c, qd_c, kd_c, cd_c = xs
        u = u_c - jnp.einsum('bhck,bhkv->bhcv', w_c, state)
        o = jnp.einsum('bhck,bhkv->bhcv', qd_c, state) + jnp.einsum('bhij,bhjv->bhiv', a_c, u)
        state = cd_c[..., None, None] * state + jnp.einsum('bhck,bhcv->bhkv', kd_c, u)
        return state, o

    xs = tuple(jnp.moveaxis(t, 2, 0) for t in (u_v, w_k, attn, q_dec, k_dec, chunk_decay))
    state0 = jnp.zeros((B, H, DK, DV), jnp.float32)
    _, o = lax.scan(step, state0, xs)
    return o.transpose(1, 0, 3, 2, 4).reshape(B, S, H, DV)


def multi_scale_pool(p, pool_w, pool_scale):
    B, S, _ = p.shape
    pf = p.astype(jnp.float32)
    csum = jnp.pad(jnp.cumsum(pf, axis=1), ((0, 0), (1, 0), (0, 0)))
    xs = jnp.split(pf, len(POOL_WINDOWS), axis=-1)
    cs = jnp.split(csum, len(POOL_WINDOWS), axis=-1)
    t = jnp.arange(S)
    outs = []
    for gi, win in enumerate(POOL_WINDOWS):
        lo = jnp.maximum(t + 1 - win, 0)
        cnt = jnp.minimum(t + 1, win).astype(jnp.float32)
        mean = (cs[gi][:, 1:] - cs[gi][:, lo]) / cnt[None, :, None]
        outs.append(mean - xs[gi])
    y = jnp.stack(outs, axis=2)
    y = jnp.einsum('bsgc,gcd->bsgd', y, pool_w.astype(jnp.float32)).reshape(B, S, POOL_WIDTH)
    return (y * pool_scale.astype(jnp.float32)).astype(p.dtype)


def hybrid_mixer(h, w_in, conv_qkv, a_log, dt_bias, gdn_norm, pool_w, pool_scale,
                 w_branch_a, w_branch_b, w_mix_out):
    B, S, _ = h.shape
    splits = [int(i) for i in np.cumsum(IN_SIZES)[:-1]]
    qkv, z, b_raw, a_raw, p, gate_a, gate_b = jnp.split(h @ w_in, splits, axis=-1)
    qkv = jax.nn.silu(causal_dwconv(qkv, conv_qkv))
    q, k, v = jnp.split(qkv, [QK_W, 2 * QK_W], axis=-1)
    q = l2_norm(q.reshape(B, S, GDN_HEADS, GDN_DK).astype(jnp.float32)) * (GDN_DK ** -0.5)
    k = l2_norm(k.reshape(B, S, GDN_HEADS, GDN_DK).astype(jnp.float32))
    v = v.reshape(B, S, GDN_HEADS, GDN_DV).astype(jnp.float32)
    beta = jax.nn.sigmoid(b_raw.astype(jnp.float32))
    g = -jnp.exp(a_log.astype(jnp.float32)) * jax.nn.softplus(
        a_raw.astype(jnp.float32) + dt_bias.astype(jnp.float32))
    o = gated_delta_rule(q, k, v, g, beta)
    o = rms_norm(o, gdn_norm) * jax.nn.silu(z.reshape(B, S, GDN_HEADS, GDN_DV).astype(jnp.float32))
    y_a = o.reshape(B, S, V_W).astype(h.dtype) @ w_branch_a
    y_b = multi_scale_pool(p, pool_w, pool_scale) @ w_branch_b
    merged = jax.nn.sigmoid(gate_a) * y_a + jax.nn.sigmoid(gate_b) * y_b
    return merged @ w_mix_out


def memory_cross_attention(h, m, w_xq, w_xkv, w_xo):
    B, S, _ = h.shape
    M = m.shape[1]
    q = (h @ w_xq).reshape(B, S, XA_HEADS, XA_HEAD_DIM)
    k, v = jnp.split(m @ w_xkv, 2, axis=-1)
    k = k.reshape(B, M, XA_HEADS, XA_HEAD_DIM)
    v = v.reshape(B, M, XA_HEADS, XA_HEAD_DIM)
    s = jnp.einsum('bshd,bmhd->bhsm', q, k).astype(jnp.float32) * (XA_HEAD_DIM ** -0.5)
    pr = jax.nn.softmax(s, axis=-1).astype(v.dtype)
    o = jnp.einsum('bhsm,bmhd->bshd', pr, v).reshape(B, S, D_MODEL)
    return o @ w_xo


def conv_glu_ffn(h, w_up, ffn_conv_w, ffn_conv_b, w_down):
    u = causal_dwconv(h @ w_up, ffn_conv_w) + ffn_conv_b
    a, b = jnp.split(u, 2, axis=-1)
    return (jax.nn.silu(a) * b) @ w_down


def setup_inputs(seed: int = 0) -> dict:
    key = jax.random.key(seed)
    ks = iter(jax.random.split(key, 32))

    def dense(shape, fan_in):
        return jax.random.normal(next(ks), shape, jnp.float32) * (fan_in ** -0.5)

    def gain(shape):
        return 1.0 + 0.05 * jax.random.normal(next(ks), shape, jnp.float32)

    L = DEPTH
    x = jax.random.normal(next(ks), (BATCH, SEQ, D_MODEL), jnp.float32)
    mem = jax.random.normal(next(ks), (BATCH, MEM_LEN, D_MODEL), jnp.float32)
    a_log = jnp.log(jax.random.uniform(next(ks), (L, GDN_HEADS), jnp.float32, 1.0, 16.0))
    dt = jnp.exp(jax.random.uniform(next(ks), (L, GDN_HEADS), jnp.float32,
                                    np.log(1e-3), np.log(1e-1)))
    dt_bias = dt + jnp.log(-jnp.expm1(-dt))
    return {
        "x": x,
        "mem": mem,
        "mix_pre_norm": gain((L, D_MODEL)),
        "w_in": dense((L, D_MODEL, D_IN), D_MODEL),
        "conv_qkv": dense((L, GDN_CONV, QKV_W), GDN_CONV),
        "a_log": a_log,
        "dt_bias": dt_bias,
        "gdn_norm": gain((L, GDN_DV)),
        "pool_w": dense((L, len(POOL_WINDOWS), POOL_GROUP, POOL_GROUP), POOL_GROUP),
        "pool_scale": gain((L, POOL_WIDTH)),
        "w_branch_a": dense((L, V_W, D_MODEL), V_W),
        "w_branch_b": dense((L, POOL_WIDTH, D_MODEL), POOL_WIDTH),
        "w_mix_out": dense((L, D_MODEL, D_MODEL), D_MODEL),
        "mix_post_norm": gain((L, D_MODEL)),
        "xa_pre_norm": gain((L, D_MODEL)),
        "mem_norm": gain((L, D_MODEL)),
        "w_xq": dense((L, D_MODEL, D_MODEL), D_MODEL),
        "w_xkv": dense((L, D_MODEL, 2 * D_MODEL), D_MODEL),
        "w_xo": dense((L, D_MODEL, D_MODEL), D_MODEL),
        "xa_post_norm": gain((L, D_MODEL)),
        "ffn_pre_norm": gain((L, D_MODEL)),
        "w_up": dense((L, D_MODEL, 2 * D_FF), D_MODEL),
        "ffn_conv_w": dense((L, FFN_CONV, 2 * D_FF), FFN_CONV),
        "ffn_conv_b": 0.01 * jax.random.normal(next(ks), (L, 2 * D_FF), jnp.float32),
        "w_down": dense((L, D_FF, D_MODEL), D_FF),
        "ffn_post_norm": gain((L, D_MODEL)),
    }


def reference(x, mem, mix_pre_norm, w_in, conv_qkv, a_log, dt_bias, gdn_norm, pool_w,
              pool_scale, w_branch_a, w_branch_b, w_mix_out, mix_post_norm, xa_pre_norm,
              mem_norm, w_xq, w_xkv, w_xo, xa_post_norm, ffn_pre_norm, w_up, ffn_conv_w,
              ffn_conv_b, w_down, ffn_post_norm):
    for l in range(DEPTH):
        h = rms_norm(x, mix_pre_norm[l])
        y = hybrid_mixer(h, w_in[l], conv_qkv[l], a_log[l], dt_bias[l], gdn_norm[l], pool_w[l],
                         pool_scale[l], w_branch_a[l], w_branch_b[l], w_mix_out[l])
        x = x + rms_norm(y, mix_post_norm[l])
        h = rms_norm(x, xa_pre_norm[l])
        m = rms_norm(mem, mem_norm[l])
        y = memory_cross_attention(h, m, w_xq[l], w_xkv[l], w_xo[l])
        x = x + rms_norm(y, xa_post_norm[l])
        h = rms_norm(x, ffn_pre_norm[l])
        y = conv_glu_ffn(h, w_up[l], ffn_conv_w[l], ffn_conv_b[l], w_down[l])
        x = x + rms_norm(y, ffn_post_norm[l])
    return x
```

```python
import numpy as np
import concourse.bass as bass
import concourse.mybir as mybir
from concourse.bass_utils import run_bass_kernel_spmd
from contextlib import ExitStack

F32 = mybir.dt.float32
BF16 = mybir.dt.bfloat16
AF = mybir.ActivationFunctionType
ALU = mybir.AluOpType

ENGS = ("pe", "act", "dve", "pool", "sp")
NDSEM = 24


class Tl:
    __slots__ = ("ap", "name", "w", "r", "excl")

    def __init__(self, ap, name="", excl=False):
        self.ap = ap
        self.name = name
        self.w = {}
        self.r = {}
        self.excl = excl


def _add(depmap, dep):
    if dep[0] == "c":
        k = ("c", dep[1])
        if k not in depmap or depmap[k][2] < dep[2]:
            depmap[k] = dep
    else:
        depmap[dep] = dep


class Prog:
    def __init__(self, nc):
        self.nc = nc
        self.ops = {e: [] for e in ENGS}
        self.ndma = {e: 0 for e in ENGS}

    def op(self, eng, fn, reads=(), writes=(), dma=False):
        ops = self.ops[eng]
        idx = len(ops)
        deps = {}
        ex = [t for t in reads if t.excl]
        if ex:
            reads = [t for t in reads if not t.excl]
            writes = list(writes) + [t for t in ex if t not in writes]
        for t in reads:
            for d in t.w.values():
                _add(deps, d)
        for t in writes:
            for d in t.w.values():
                _add(deps, d)
            for d in t.r.values():
                _add(deps, d)
        if dma:
            n = self.ndma[eng]
            self.ndma[eng] += 1
            me = ("d", eng, n)
            if n >= NDSEM:
                _add(deps, ("d", eng, n - NDSEM))
        else:
            me = ("c", eng, idx)
        if eng == "pe":
            deps.pop(("c", "pe"), None)
        rec = dict(fn=fn, deps=list(deps.values()), me=me, needed=False)
        ops.append(rec)
        for t in writes:
            if t.r:
                t.w = {}
                t.r = {}
            _add(t.w, me)
        for t in reads:
            _add(t.r, me)
        return rec


def emit_program(nc, prog, es):
    for e in ENGS:
        for rec in prog.ops[e]:
            for d in rec["deps"]:
                if d[0] == "c":
                    prog.ops[d[1]][d[2]]["needed"] = True
    cum = {}
    for e in ENGS:
        c = 0
        arr = []
        for rec in prog.ops[e]:
            if rec["me"][0] == "c" and rec["needed"]:
                c += 1
            arr.append(c)
        cum[e] = arr
    csem = {e: es.enter_context(nc.semaphore("cs_" + e)) for e in ENGS}
    dsem = {}
    for e in ENGS:
        if prog.ndma[e] > 0:
            dsem[e] = [es.enter_context(nc.semaphore("ds_%s_%d" % (e, i)))
                       for i in range(min(NDSEM, prog.ndma[e]))]
    block = es.enter_context(nc.Block())

    def run(ename, engobj):
        waited = {}
        for rec in prog.ops[ename]:
            for d in rec["deps"]:
                if d[0] == "c":
                    sem = csem[d[1]]
                    val = cum[d[1]][d[2]]
                    key = ("c", d[1])
                else:
                    sem = dsem[d[1]][d[2] % NDSEM]
                    val = 16 * (d[2] // NDSEM + 1)
                    key = ("d", d[1], d[2] % NDSEM)
                if waited.get(key, 0) >= val:
                    continue
                waited[key] = val
                engobj.wait_ge(sem, val)
            ins = rec["fn"](engobj)
            me = rec["me"]
            if me[0] == "c":
                if rec["needed"]:
                    ins.then_inc(csem[ename], 1)
            else:
                ins.then_inc(dsem[ename][me[2] % NDSEM], 16)
        if ename in dsem:
            n = prog.ndma[ename]
            for i in range(min(NDSEM, n)):
                last = ((n - 1 - i) // NDSEM) * NDSEM + i
                engobj.wait_ge(dsem[ename][i], 16 * (last // NDSEM + 1))

    @block.tensor
    def _(eng):
        run("pe", eng)

    @block.scalar
    def _(eng):
        run("act", eng)

    @block.vector
    def _(eng):
        run("dve", eng)

    @block.gpsimd
    def _(eng):
        run("pool", eng)

    @block.sync
    def _(eng):
        run("sp", eng)


class FreeList:
    def __init__(self, tiles):
        self.free = list(tiles)
        self.total = len(tiles)
        self.low = len(tiles)

    def get(self):
        if not self.free:
            raise RuntimeError("freelist exhausted")
        t = self.free.pop(0)
        self.low = min(self.low, len(self.free))
        return t

    def put(self, t):
        self.free.append(t)


class Ring:
    def __init__(self, tiles):
        self.t = list(tiles)
        self.i = 0

    def next(self):
        t = self.t[self.i % len(self.t)]
        self.i += 1
        return t


class _Stop(Exception):
    pass


class Cfg:
    stop = 0

    def __init__(self, SEQ=8192, NT=256):
        self.SEQ = SEQ
        self.NT = NT
        self.OWN = SEQ // 4
        self.HALO = 32
        self.NTILES = SEQ // NT
        self.TH = (SEQ - self.OWN) // NT - 1
        self.NSUB = NT // 128


D = 2048
KC = 16
NH = 16
DFF = 5504
FC = 43
MEM = 256
HG = 4
EPS = 1e-6

P_GAIN = 0
P_CONVQ = 112
P_FCW = 304
P_FCB = 562
P_PSC = 648
P_GN = 656
P_ALOG = 657
P_DTB = 673
P_FLAG = 689
P_INVC = 690
NPRM = 754
C_ID = 0
C_UBD = 512
C_MSTR = 1024
C_ONES = 1536
C_OBD = 1664
C_BSEL = 1792
C_EPS = 1794
C_ONE = 1795
NCONST = 1800


def build(cfg, dbg=()):
    SEQ, NT, OWN, HALO, NTILES, TH, NSUB = cfg.SEQ, cfg.NT, cfg.OWN, cfg.HALO, cfg.NTILES, cfg.TH, cfg.NSUB
    nc = bass.Bass("TRN2", target_bir_lowering=False)
    nc.dge_precook = False

    def din(name, shape):
        return nc.dram_tensor(name, shape, F32, kind="ExternalInput").ap()

    xT = din("xT", [128, KC, SEQ])
    memT = din("memT", [128, KC, MEM])
    consts_d = din("consts", [128, NCONST])
    prm_d = din("prm", [128, NPRM])
    wba_d = din("wba", [128, KC, 32])
    poolw_d = din("poolw", [128, 4, 2, 256])
    w_in = din("w_in", [104, 128, KC, 128])
    w_a = din("w_a", [16, 128, KC, 128])
    w_b = din("w_b", [16, 128, 8, 128])
    w_mix = din("w_mix", [16, 128, KC, 128])
    w_xq = din("w_xq", [16, 128, KC, 128])
    w_xkv = din("w_xkv", [32, 128, KC, 128])
    w_xo = din("w_xo", [16, 128, KC, 128])
    w_up = din("w_up", [86, 128, KC, 128])
    w_down = din("w_down", [16, 128, FC, 128])
    outT = nc.dram_tensor("outT", [128, KC, OWN], F32, kind="ExternalOutput").ap()
    dbg_out = {}
    for name, shape in dbg:
        dbg_out[name] = nc.dram_tensor("dbg_" + name, list(shape), F32, kind="ExternalOutput").ap()

    es = ExitStack()
    with es:
        def sb(name, shape, dt=F32):
            return es.enter_context(nc.sbuf_tensor(name, list(shape), dt))

        def pst(name, shape, dt=F32):
            return es.enter_context(nc.psum_tensor(name, list(shape), dt))

        P = Prog(nc)

        def ckpt(i):
            if cfg.stop == i:
                raise _Stop()

        xa_t = sb("xa", [128, KC, NT]); XA = [Tl(xa_t[:, k, :]) for k in range(KC)]
        hb_t = sb("hb", [128, KC, NT], BF16); HB = [Tl(hb_t[:, k, :]) for k in range(KC)]
        cb_t = sb("cb", [128, KC, NT], BF16); CB = [Tl(cb_t[:, k, :]) for k in range(KC)]
        db_t = sb("db", [128, KC, NT]); DB = [Tl(db_t[:, k, :]) for k in range(KC)]
        act_t = sb("actb", [128, FC, NT], BF16); ACTB = [Tl(act_t[:, k, :]) for k in range(FC)]
        s_t = sb("state", [128, NH, 128]); S = [Tl(s_t[:, h, :]) for h in range(NH)]
        NW = 3
        w_ts = [sb("wt%d" % i, [128, 22, 128], BF16) for i in range(NW)]
        WR = Ring([Tl(t) for t in w_ts])
        kx_t = sb("kx", [128, KC, MEM], BF16); KX = Tl(kx_t)
        vx_t = sb("vx", [128, 2, D], BF16); VX = Tl(vx_t)
        cst = sb("cst", [128, NCONST]); CST = Tl(cst)
        prm = sb("prm_s", [128, NPRM]); PRM = Tl(prm)
        onesb_t = sb("onesb", [128, 128], BF16); ONESB = Tl(onesb_t)
        wba_t = sb("wba_s", [128, KC, 32], BF16); WBA = Tl(wba_t)
        poolw_t = sb("poolw_s", [128, 4, 2, 256], BF16); POOLW = Tl(poolw_t)
        nga_t = sb("nga", [128, NH]); NGA = Tl(nga_t)
        qtail_t = sb("qtail", [128, 48, 3]); QTAIL = [Tl(qtail_t[:, c, :]) for c in range(48)]
        ptail_t = sb("pbuf", [128, 8, 15 + NT]); PBUF = [Tl(ptail_t[:, c, :]) for c in range(8)]
        ftail_t = sb("ftail", [128, 86, 2]); FTAIL = [Tl(ftail_t[:, c, :]) for c in range(86)]
        bg_t = sb("bg", [128, 3, NSUB, NH]); BG = Tl(bg_t)
        kt_t = sb("kt", [128, HG, NT]); KT0 = [Tl(kt_t[:, i, :]) for i in range(HG)]
        qt_t = sb("qt", [128, HG, NT]); QT0 = [Tl(qt_t[:, i, :]) for i in range(HG)]
        ktb_t = sb("ktb", [128, HG, NT], BF16); KTB0 = [Tl(ktb_t[:, i, :]) for i in range(HG)]
        vtb_t = sb("vtb", [128, HG, NT], BF16); VTB0 = [Tl(vtb_t[:, i, :]) for i in range(HG)]
        qtb_t = sb("qtb", [128, HG, NT], BF16); QTB0 = [Tl(qtb_t[:, i, :]) for i in range(HG)]
        act32 = act_t.bitcast(F32)
        cpr = NT // 128

        def alias32(j):
            return Tl(act32[:, j * cpr:(j + 1) * cpr, :].rearrange("p a b -> p (a b)"))
        KT1 = [alias32(i) for i in range(HG)]
        QT1 = [alias32(HG + i) for i in range(HG)]
        r0 = 2 * HG * cpr
        KTB1 = [Tl(act_t[:, r0 + i, :]) for i in range(HG)]
        VTB1 = [Tl(act_t[:, r0 + HG + i, :]) for i in range(HG)]
        QTB1 = [Tl(act_t[:, r0 + 2 * HG + i, :]) for i in range(HG)]
        r1 = r0 + 3 * HG
        assert r1 + 8 <= FC
        KTS = [KT0, KT1]; QTS = [QT0, QT1]
        KTBS = [KTB0, KTB1]; VTBS = [VTB0, VTB1]; QTBS = [QTB0, QTB1]
        or_t = sb("or", [128, HG, NT]); OR = [Tl(or_t[:, i, :]) for i in range(HG)]
        NTMP = 7
        tmp_t = sb("gtmp", [128, NTMP, 512])
        TMP = FreeList([Tl(tmp_t[:, i, :]) for i in range(NTMP)])
        NTMPB = 17
        tmpb_t = sb("gtmpb", [128, NTMPB, 512], BF16)
        TMPB = FreeList([Tl(tmpb_t[:, i, :]) for i in range(NTMPB)])
        identb_t = sb("identb", [128, 128], BF16); IDENTB = Tl(identb_t)
        db16 = db_t.bitcast(BF16)
        for j_ in range(8):
            TMPB.put(Tl(db16[:, 2 * j_:2 * j_ + 2, :].rearrange("p a b -> p (a b)")[:, 0:512]))
        gs_t = sb("gs", [128, NSUB, 8, NH]); GS = Tl(gs_t)
        cq_t = sb("cq", [128, 2, 3 + NT]); CQ = Ring([Tl(cq_t[:, i, :]) for i in range(2)])
        acc_t = sb("acc", [128, 3, NT]); ACC = Ring([Tl(acc_t[:, i, :]) for i in range(3)])
        sqb_t = sb("sqb", [128, 3, NT], BF16); SQB = Ring([Tl(sqb_t[:, i, :]) for i in range(3)])
        rs_t = sb("rs", [128, 2, NT]); RS = Ring([Tl(rs_t[:, i, :]) for i in range(2)])
        f32s_t = sb("f32s", [128, 3, NT]); FS = Ring([Tl(f32s_t[:, i, :]) for i in range(3)])
        et_t = sb("et", [128, 4, NT], BF16); ET = Ring([Tl(et_t[:, i, :]) for i in range(4)])
        YPI = [Tl(act_t[:, r1 + i, :]) for i in range(8)]
        ypo_t = sb("ypo", [128, 8, NT], BF16); YPO = [Tl(ypo_t[:, i, :]) for i in range(8)]
        pw_t = sb("pw", [128, 2, 2, 15 + NT]); PW = [Tl(pw_t[:, i, :, :]) for i in range(2)]

        pbig = [pst("pbig%d" % i, [128, 512]) for i in range(2)]
        PB = Ring([Tl(pbig[i][:, :], excl=True) for i in range(2)])
        pss = pst("pss", [128, 512])
        PSS = Ring([Tl(pss[:, :], excl=True)])
        psm = [pst("psm%d" % i, [128, 512]) for i in range(5)]
        PSM = FreeList([Tl(psm[i][:, :], excl=True) for i in range(5)])

        def C(c0, n=128):
            return cst[:, c0:c0 + n]

        def pc(c):
            return prm[:, c:c + 1]

        def mm(out_tl, out_ap, a_tl, a_ap, b_tl, b_ap, start=True, stop=True):
            P.op("pe", lambda e: e.matmul(out_ap, a_ap, b_ap, start=start, stop=stop),
                 reads=[a_tl, b_tl], writes=[out_tl])

        def act(out_tl, out_ap, in_tl, in_ap, func, bias=None, scale=None, rd=()):
            kw = {}
            if bias is not None:
                kw["bias"] = bias
            if scale is not None:
                kw["scale"] = scale
            P.op("act", lambda e: e.activation(out=out_ap, in_=in_ap, func=func, **kw),
                 reads=[in_tl] + list(rd), writes=[out_tl])

        def ts(out_tl, out_ap, in_tl, in_ap, s1, s2, op0, op1=None, rd=(), eng="dve"):
            if op1 is None:
                P.op(eng, lambda e: e.tensor_scalar(out_ap, in_ap, s1, None, op0),
                     reads=[in_tl] + list(rd), writes=[out_tl])
            else:
                P.op(eng, lambda e: e.tensor_scalar(out_ap, in_ap, s1, s2, op0, op1),
                     reads=[in_tl] + list(rd), writes=[out_tl])

        def stt(out_tl, out_ap, in0_tl, in0_ap, scalar, in1_tl, in1_ap, op0, op1, rd=()):
            P.op("dve", lambda e: e.scalar_tensor_tensor(out_ap, in0_ap, scalar, in1_ap, op0, op1),
                 reads=[in0_tl, in1_tl] + list(rd), writes=[out_tl])

        def tt(out_tl, out_ap, a_tl, a_ap, b_tl, b_ap, op, eng="dve"):
            P.op(eng, lambda e: e.tensor_tensor(out_ap, a_ap, b_ap, op),
                 reads=[a_tl, b_tl], writes=[out_tl])

        def cp(out_tl, out_ap, in_tl, in_ap, eng="dve"):
            P.op(eng, lambda e: e.tensor_copy(out_ap, in_ap), reads=[in_tl], writes=[out_tl])

        def recip(out_tl, out_ap, in_tl, in_ap):
            P.op("dve", lambda e: e.reciprocal(out_ap, in_ap), reads=[in_tl], writes=[out_tl])

        def memset(tl, ap, val, eng="dve"):
            P.op(eng, lambda e: e.memset(ap, val), writes=[tl])

        def dma(eng, out_tl, out_ap, in_tl, in_ap):
            P.op(eng, lambda e: e.dma_start(out=out_ap, in_=in_ap),
                 reads=[in_tl] if in_tl is not None else [],
                 writes=[out_tl] if out_tl is not None else [], dma=True)

        def dbg_dump(name, tl, ap):
            if name in dbg_out:
                dma("sp", None, dbg_out[name], tl, ap)

        def load_w(dram_ap, nk):
            w = WR.next()
            dma("pool", w, w.ap[:, 0:nk, :], None, dram_ap)
            return w

        def proj(wd, in_list, c0, n, nk=KC, k0=0, ps=None, first=True, last=True):
            w = load_w(wd, nk)
            if ps is None:
                ps = PB.next()
            for k in range(nk):
                mm(ps, ps.ap[:, 0:n], w, w.ap[:, k, :], in_list[k0 + k], in_list[k0 + k].ap[:, c0:c0 + n],
                   start=(first and k == 0), stop=(last and k == nk - 1))
            return ps

        def rstd_from(ss, n, scale, out=None):
            r = RS.next() if out is None else out
            act(r, r.ap[:, 0:n], ss, ss.ap[:, 0:n], AF.Sqrt, bias=C(C_EPS, 1), scale=scale, rd=[CST])
            recip(r, r.ap[:, 0:n], r, r.ap[:, 0:n])
            return r

        def rmsnorm_to_bf(src, c0, n, gi, dst):
            ss = PSS.next()
            for k in range(KC):
                q = SQB.next()
                act(q, q.ap[:, 0:n], src[k], src[k].ap[:, c0:c0 + n], AF.Square)
                mm(ss, ss.ap[:, 0:n], ONESB, onesb_t[:, :], q, q.ap[:, 0:n], start=(k == 0), stop=(k == KC - 1))
            r = rstd_from(ss, n, 1.0 / D)
            for k in range(KC):
                stt(dst[k], dst[k].ap[:, c0:c0 + n], src[k], src[k].ap[:, c0:c0 + n], pc(P_GAIN + 16 * gi + k),
                    r, r.ap[:, 0:n], ALU.mult, ALU.mult, rd=[PRM])

        def postnorm_residual(ss, c0, n, gi):
            r = rstd_from(ss, n, 1.0 / D)
            for k in range(KC):
                f = FS.next()
                tt(f, f.ap[:, 0:n], DB[k], DB[k].ap[:, c0:c0 + n], r, r.ap[:, 0:n], ALU.mult)
                stt(XA[k], XA[k].ap[:, c0:c0 + n], f, f.ap[:, 0:n], pc(P_GAIN + 16 * gi + k),
                    XA[k], XA[k].ap[:, c0:c0 + n], ALU.mult, ALU.add, rd=[PRM])

        def y_chunk_out(ps, mo, c0, n, ss):
            act(DB[mo], DB[mo].ap[:, c0:c0 + n], ps, ps.ap[:, 0:n], AF.Copy)
            q = SQB.next()
            act(q, q.ap[:, 0:n], ps, ps.ap[:, 0:n], AF.Square)
            mm(ss, ss.ap[:, 0:n], ONESB, onesb_t[:, :], q, q.ap[:, 0:n], start=(mo == 0), stop=(mo == KC - 1))

        def body():
            pass

        dma("sp", CST, cst[:, :], None, consts_d)
        dma("sp", PRM, prm[:, :], None, prm_d)
        dma("pool", WBA, wba_t[:, :, :], None, wba_d)
        dma("pool", POOLW, poolw_t[:, :, :, :], None, poolw_d)
        cp(ONESB, onesb_t[:, :], CST, C(C_ONES))
        cp(IDENTB, identb_t[:, :], CST, C(C_ID))
        for h in range(NH):
            memset(S[h], S[h].ap, 0.0)
        for c in range(48):
            memset(QTAIL[c], QTAIL[c].ap, 0.0)
        for c in range(8):
            memset(PBUF[c], PBUF[c].ap, 0.0)
        for c in range(86):
            memset(FTAIL[c], FTAIL[c].ap, 0.0)
        act(NGA, nga_t[:, :], PRM, prm[:, P_ALOG:P_ALOG + 16], AF.Exp)
        ts(NGA, nga_t[:, :], NGA, nga_t[:, :], -1.0, None, ALU.mult)

        def xattn_kv():
            assert NT == MEM
            MXl = DB
            MTl = HB
            for k in range(KC):
                dma("sp", MXl[k], MXl[k].ap, None, memT[:, k, :])
            ckpt(21)
            rmsnorm_to_bf(MXl, 0, MEM, 3, MTl)
            ckpt(22)
            for mo in range(KC):
                ps = proj(w_xkv[mo], MTl, 0, MEM)
                act(KX, kx_t[:, mo, :], ps, ps.ap[:, 0:MEM], AF.Copy)
                ckpt(100 + mo)
            ckpt(24)
            for mo in range(KC):
                ps = proj(w_xkv[KC + mo], MTl, 0, MEM)
                f = FS.next()
                act(f, f.ap[:, 0:MEM], ps, ps.ap[:, 0:MEM], AF.Copy)
                for mt in range(2):
                    p2 = PSM.get()
                    mm(p2, p2.ap[:, 0:128], f, f.ap[:, mt * 128:(mt + 1) * 128], CST, C(C_ID))
                    cp(VX, vx_t[:, mt, mo * 128:(mo + 1) * 128], p2, p2.ap[:, 0:128])
                    PSM.put(p2)

        def conv4(ps, ch, n, c0, out_tl, out_ap_silu):
            cq = CQ.next()
            a = ACC.next()
            act(cq, cq.ap[:, 3:3 + n], ps, ps.ap[:, 0:n], AF.Copy)
            cp(cq, cq.ap[:, 0:3], QTAIL[ch], QTAIL[ch].ap)
            act(a, a.ap[:, 0:n], ps, ps.ap[:, 0:n], AF.Identity, scale=pc(P_CONVQ + 4 * ch + 3), rd=[PRM])
            for j in range(3):
                stt(a, a.ap[:, 0:n], cq, cq.ap[:, j:j + n], pc(P_CONVQ + 4 * ch + j), a, a.ap[:, 0:n],
                    ALU.mult, ALU.add, rd=[PRM])
            cp(QTAIL[ch], QTAIL[ch].ap, cq, cq.ap[:, n:n + 3])
            act(out_tl, out_ap_silu, a, a.ap[:, 0:n], AF.Silu)

        def l2norm_to(tl, ap, otl, oap, n, mul):
            q = SQB.next()
            tt(q, q.ap[:, 0:n], tl, ap, tl, ap, ALU.mult)
            ss = PSS.next()
            mm(ss, ss.ap[:, 0:n], ONESB, onesb_t[:, :], q, q.ap[:, 0:n])
            r = rstd_from(ss, n, 1.0)
            if mul == 1.0:
                tt(otl, oap, tl, ap, r, r.ap[:, 0:n], ALU.mult)
            else:
                stt(otl, oap, tl, ap, mul, r, r.ap[:, 0:n], ALU.mult, ALU.mult)

        def Q4(t, i):
            return t.ap[:, i * 128:(i + 1) * 128]

        def gs_stage(sub):
            g = bg_t[:, 2, sub, :]
            f = FS.next()
            for c in range(2):
                ts(f, f.ap[:, 16 * c:16 * c + 16], BG, g, C(C_BSEL + c, 1), None, ALU.mult, rd=[CST])
            bk = PSM.get()
            mm(bk, bk.ap[:, 0:16], CST, C(C_UBD), BG, g)
            mm(bk, bk.ap[:, 16:32], CST, C(C_OBD), BG, g)
            mm(bk, bk.ap[:, 32:64], CST, C(C_ONES), f, f.ap[:, 0:32])
            for r_ in range(4):
                cp(GS, gs_t[:, sub, r_, :], bk, bk.ap[:, 16 * r_:16 * r_ + 16])
            PSM.put(bk)
            f2 = FS.next()
            act(f2, f2.ap[:, 0:16], GS, gs_t[:, sub, 0, :], AF.Exp)
            tt(GS, gs_t[:, sub, 4, :], f2, f2.ap[:, 0:16], BG, bg_t[:, 0, sub, :], ALU.mult)
            tt(f2, f2.ap[:, 16:32], GS, gs_t[:, sub, 1, :], GS, gs_t[:, sub, 0, :], ALU.subtract)
            act(GS, gs_t[:, sub, 5, :], f2, f2.ap[:, 16:32], AF.Exp)
            for r_ in range(2):
                act(GS, gs_t[:, sub, 6 + r_, :], GS, gs_t[:, sub, 2 + r_, :], AF.Exp)

        def gdn_prep(hg, sub, full, st):
            cs = slice(sub * 128, sub * 128 + 128)
            heads = [hg * HG + i for i in range(HG)]
            KT = KTBS[hg % 2]; VT = VTBS[hg % 2]; QT = QTBS[hg % 2]

            def gcol(row, h):
                return gs_t[:, sub, row, h:h + 1]
            gd = TMP.get()
            for i, h in enumerate(heads):
                ts(gd, Q4(gd, i), CST, C(C_ID), gcol(0, h), -1.0, ALU.mult, ALU.mult, rd=[GS])
            bk = PSM.get()
            for i in range(HG):
                mm(bk, Q4(bk, i), CST, C(C_ONES), gd, Q4(gd, i))
            yield
            Dm = TMP.get()
            for i, h in enumerate(heads):
                ts(Dm, Q4(Dm, i), bk, Q4(bk, i), gcol(0, h), 0.0, ALU.add, ALU.min, rd=[GS])
            if full:
                DT = TMP.get(); eG = TMP.get()
                for i, h in enumerate(heads):
                    ts(DT, Q4(DT, i), bk, Q4(bk, i), -1.0, gcol(0, h), ALU.mult, ALU.subtract, rd=[GS])
                ts(DT, DT.ap, DT, DT.ap, 0.0, None, ALU.min)
                act(eG, eG.ap, bk, bk.ap, AF.Exp, scale=-1.0)
            PSM.put(bk)
            TMP.put(gd)
            yield
            act(Dm, Dm.ap, Dm, Dm.ap, AF.Exp)
            if full:
                act(DT, DT.ap, DT, DT.ap, AF.Exp)
            yield
            tt(Dm, Dm.ap, Dm, Dm.ap, CST, C(C_MSTR, 512), ALU.mult)
            if full:
                tt(DT, DT.ap, DT, DT.ap, CST, C(C_UBD, 512), ALU.mult)
            bk = PSM.get()
            for i in range(HG):
                mm(bk, Q4(bk, i), KT[i], KT[i].ap[:, cs], KT[i], KT[i].ap[:, cs])
            yield
            N0 = TMPB.get()
            for i, h in enumerate(heads):
                stt(N0, Q4(N0, i), bk, Q4(bk, i), bg_t[:, 1, sub, h:h + 1], Dm, Q4(Dm, i), ALU.mult, ALU.mult, rd=[BG])
            PSM.put(bk)
            TMP.put(Dm)
            yield
            bk = PSM.get()
            for i in range(HG):
                mm(bk, Q4(bk, i), N0, Q4(N0, i), IDENTB, identb_t[:, :])
            if full:
                bk2 = PSM.get()
                for i in range(HG):
                    mm(bk2, Q4(bk2, i), KT[i], KT[i].ap[:, cs], QT[i], QT[i].ap[:, cs])
            yield
            N0T = TMPB.get(); TT = TMPB.get()
            act(N0T, N0T.ap, bk, bk.ap, AF.Copy)
            tt(TT, TT.ap, bk, bk.ap, CST, C(C_ID, 512), ALU.add)
            PSM.put(bk)
            if full:
                AT = TMPB.get(); Qd = TMPB.get()
                tt(AT, AT.ap, bk2, bk2.ap, DT, DT.ap, ALU.mult)
                PSM.put(bk2)
                for i in range(HG):
                    tt(Qd, Q4(Qd, i), QT[i], QT[i].ap[:, cs], eG, Q4(eG, i), ALU.mult)
                TMP.put(DT); TMP.put(eG)
                st["AT"] = AT; st["Qd"] = Qd
            yield
            Pk, PTk = N0, N0T
            for k in range(5):
                b1 = PSM.get()
                for i in range(HG):
                    mm(b1, Q4(b1, i), PTk, Q4(PTk, i), Pk, Q4(Pk, i))
                if k < 4:
                    b2 = PSM.get()
                    for i in range(HG):
                        mm(b2, Q4(b2, i), Pk, Q4(Pk, i), PTk, Q4(PTk, i))
                yield
                Pn = TMPB.get()
                act(Pn, Pn.ap, b1, b1.ap, AF.Copy)
                PSM.put(b1)
                if k < 4:
                    PTn = TMPB.get()
                    cp(PTn, PTn.ap, b2, b2.ap)
                    PSM.put(b2)
                yield
                b3 = PSM.get()
                for i in range(HG):
                    mm(b3, Q4(b3, i), Pn, Q4(Pn, i), TT, Q4(TT, i))
                yield
                tt(TT, TT.ap, TT, TT.ap, b3, b3.ap, ALU.add)
                PSM.put(b3)
                TMPB.put(Pk); TMPB.put(PTk)
                Pk = Pn
                PTk = PTn if k < 4 else None
                yield
            TMPB.put(Pk)
            bK = PSM.get(); bV = PSM.get()
            for i in range(HG):
                mm(bK, Q4(bK, i), KT[i], KT[i].ap[:, cs], IDENTB, identb_t[:, :])
                mm(bV, Q4(bV, i), VT[i], VT[i].ap[:, cs], IDENTB, identb_t[:, :])
            yield
            Rv = TMPB.get(); Rk = TMPB.get(); Kd = TMPB.get()
            for i, h in enumerate(heads):
                ts(Rv, Q4(Rv, i), bV, Q4(bV, i), bg_t[:, 0, sub, h:h + 1], None, ALU.mult, rd=[BG])
                ts(Rk, Q4(Rk, i), bK, Q4(bK, i), gcol(4, h), None, ALU.mult, rd=[GS])
                act(Kd, Q4(Kd, i), bK, Q4(bK, i), AF.Identity, scale=gcol(5, h), rd=[GS])
            PSM.put(bK); PSM.put(bV)
            yield
            bU = PSM.get(); bW = PSM.get()
            for i in range(HG):
                mm(bU, Q4(bU, i), TT, Q4(TT, i), Rv, Q4(Rv, i))
                mm(bW, Q4(bW, i), Rk, Q4(Rk, i), TT, Q4(TT, i))
            yield
            TMPB.put(Rv); TMPB.put(Rk); TMPB.put(TT)
            Uv = TMP.get(); WkT = TMPB.get()
            act(Uv, Uv.ap, bU, bU.ap, AF.Copy)
            cp(WkT, WkT.ap, bW, bW.ap)
            PSM.put(bU); PSM.put(bW)
            st["Uv"] = Uv; st["WkT"] = WkT; st["Kd"] = Kd
            yield

        def gdn_state(hg, sub, full, st):
            heads = [hg * HG + i for i in range(HG)]
            Uv = st["Uv"]; WkT = st["WkT"]; Kd = st["Kd"]
            u = TMPB.get(); Sb = TMPB.get()
            Sg = [S[h] for h in heads]
            sg_ap = s_t[:, hg * HG:(hg + 1) * HG, :].rearrange("p a b -> p (a b)")
            P.op("act", lambda e: e.activation(out=Sb.ap, in_=sg_ap, func=AF.Copy), reads=Sg, writes=[Sb])
            yield
            for c in range(2):
                r = slice(64 * c, 64 * c + 64)
                bk = PSM.get()
                for i, h in enumerate(heads):
                    if c == 0:
                        mm(bk, bk.ap[0:64, i * 128:(i + 1) * 128], WkT, WkT.ap[:, i * 128:i * 128 + 64], Sb, Q4(Sb, i))
                    else:
                        mm(bk, Q4(bk, i), WkT, Q4(WkT, i), Sb, Q4(Sb, i))
                yield
                tt(u, u.ap[r, :], Uv, Uv.ap[r, :], bk, bk.ap[r, :], ALU.subtract)
                PSM.put(bk)
                yield
                bk = PSM.get()
                for i, h in enumerate(heads):
                    mm(bk, Q4(bk, i), Kd, Kd.ap[r, i * 128:(i + 1) * 128], u, u.ap[r, i * 128:(i + 1) * 128])
                if full:
                    bo = PSM.get()
                    Qd = st["Qd"]; AT = st["AT"]
                    for i, h in enumerate(heads):
                        oq = bo.ap[:, i * 128:i * 128 + 64]
                        mm(bo, oq, Sb, Q4(Sb, i), Qd, Qd.ap[:, i * 128 + 64 * c:i * 128 + 64 * c + 64], start=True, stop=False)
                        mm(bo, oq, u, u.ap[r, i * 128:(i + 1) * 128], AT, AT.ap[r, i * 128 + 64 * c:i * 128 + 64 * c + 64],
                           start=False, stop=True)
                yield
                for i, h in enumerate(heads):
                    stt(S[h], S[h].ap, S[h], S[h].ap, gs_t[:, sub, 6 + c, h:h + 1], bk, Q4(bk, i), ALU.mult, ALU.add, rd=[GS])
                PSM.put(bk)
                if full:
                    o0 = sub * 128 + 64 * c
                    for i in range(HG):
                        act(OR[i], OR[i].ap[:, o0:o0 + 64], bo, bo.ap[:, i * 128:i * 128 + 64], AF.Copy)
                    PSM.put(bo)
                yield
                if c == 0:
                    P.op("act", lambda e: e.activation(out=Sb.ap, in_=sg_ap, func=AF.Copy), reads=Sg, writes=[Sb])
                    yield
            TMPB.put(u); TMPB.put(Sb); TMP.put(Uv); TMPB.put(WkT); TMPB.put(Kd)
            if full:
                TMPB.put(st["AT"]); TMPB.put(st["Qd"])

        def interleave(gens):
            gens = list(gens)
            while gens:
                nxt = []
                for g in gens:
                    try:
                        next(g)
                        nxt.append(g)
                    except StopIteration:
                        pass
                gens = nxt

        def mixer_rest(T, c0, n):
            for mo in range(KC):
                ps1 = proj(w_in[72 + mo], HB, c0, n)
                sg = FS.next()
                act(sg, sg.ap[:, 0:n], ps1, ps1.ap[:, 0:n], AF.Sigmoid)
                ps2 = proj(w_a[mo], CB, c0, n)
                tt(DB[mo], DB[mo].ap[:, c0:c0 + n], ps2, ps2.ap[:, 0:n], sg, sg.ap[:, 0:n], ALU.mult)
            for pcix in range(8):
                ps = proj(w_in[64 + pcix], HB, c0, n)
                act(PBUF[pcix], PBUF[pcix].ap[:, 15:15 + n], ps, ps.ap[:, 0:n], AF.Copy)
            L = 15 + n
            for g in range(4):
                win = 2 << g
                src_tl = None
                src = ptail_t[:, 2 * g:2 * g + 2, :]
                srcs = [PBUF[2 * g], PBUF[2 * g + 1]]
                sh = 1
                wi = 0
                cur = src
                cur_tls = srcs
                for lvl in range(g + 1):
                    dst = PW[wi % 2]
                    wi += 1
                    P.op("dve", (lambda d=dst.ap, s=cur, sh=sh: lambda e: e.tensor_tensor(d[:, :, sh:L], s[:, :, sh:L], s[:, :, 0:L - sh], ALU.add))(),
                         reads=list(cur_tls), writes=[dst])
                    cur = dst.ap
                    cur_tls = [dst]
                    sh *= 2
                for i in range(2):
                    yp = YPI[2 * g + i]
                    stt(yp, yp.ap[:, 0:n], cur_tls[0], cur[:, i, 15:15 + n], 1.0 / win,
                        PBUF[2 * g + i], PBUF[2 * g + i].ap[:, 15:15 + n], ALU.mult, ALU.subtract)
                    if T == TH + 1:
                        f = FS.next()
                        tt(f, f.ap[:, 0:16], cur_tls[0], cur[:, i, 15:31], PRM, prm[:, P_INVC + 16 * g:P_INVC + 16 * g + 16], ALU.mult)
                        tt(yp, yp.ap[:, 0:16], f, f.ap[:, 0:16], PBUF[2 * g + i], PBUF[2 * g + i].ap[:, 15:31], ALU.subtract)
            for pcix in range(8):
                f = FS.next()
                cp(f, f.ap[:, 0:15], PBUF[pcix], PBUF[pcix].ap[:, n:n + 15])
                cp(PBUF[pcix], PBUF[pcix].ap[:, 0:15], f, f.ap[:, 0:15])
            for g in range(4):
                for mo2 in range(2):
                    ps = PB.next()
                    for ki in range(2):
                        mm(ps, ps.ap[:, 0:n], POOLW, poolw_t[:, g, ki, mo2 * 128:(mo2 + 1) * 128],
                           YPI[2 * g + ki], YPI[2 * g + ki].ap[:, 0:n], start=(ki == 0), stop=(ki == 1))
                    yo = YPO[2 * g + mo2]
                    act(yo, yo.ap[:, 0:n], ps, ps.ap[:, 0:n], AF.Identity, scale=pc(P_PSC + 2 * g + mo2), rd=[PRM])
            YPOc = [Tl(None)] * 0
            for mo in range(KC):
                ps1 = proj(w_in[88 + mo], HB, c0, n)
                sg = FS.next()
                act(sg, sg.ap[:, 0:n], ps1, ps1.ap[:, 0:n], AF.Sigmoid)
                w = load_w(w_b[mo], 8)
                ps2 = PB.next()
                for k in range(8):
                    mm(ps2, ps2.ap[:, 0:n], w, w.ap[:, k, :], YPO[k], YPO[k].ap[:, 0:n], start=(k == 0), stop=(k == 7))
                f = FS.next()
                tt(f, f.ap[:, 0:n], ps2, ps2.ap[:, 0:n], sg, sg.ap[:, 0:n], ALU.mult)
                tt(CB[mo], CB[mo].ap[:, c0:c0 + n], f, f.ap[:, 0:n], DB[mo], DB[mo].ap[:, c0:c0 + n], ALU.add)
            ss = PSS.next()
            for mo in range(KC):
                ps = proj(w_mix[mo], CB, c0, n)
                y_chunk_out(ps, mo, c0, n, ss)
            postnorm_residual(ss, c0, n, 1)

        def xattn(T, c0, n):
            rmsnorm_to_bf(XA, c0, n, 2, HB)
            for mo in range(KC):
                ps = proj(w_xq[mo], HB, c0, n)
                act(CB[mo], CB[mo].ap[:, c0:c0 + n], ps, ps.ap[:, 0:n], AF.Copy)
            scl = 512.0 ** -0.5
            for hx in range(4):
                ets = []
                for mt in range(2):
                    ps = PB.next()
                    for c in range(4):
                        kc = 4 * hx + c
                        mm(ps, ps.ap[:, 0:n], KX, kx_t[:, kc, mt * 128:(mt + 1) * 128], CB[kc], CB[kc].ap[:, c0:c0 + n],
                           start=(c == 0), stop=(c == 3))
                    e_ = ET.next()
                    act(e_, e_.ap[:, 0:n], ps, ps.ap[:, 0:n], AF.Exp, scale=scl)
                    ets.append(e_)
                den = PSS.next()
                for mt in range(2):
                    mm(den, den.ap[:, 0:n], ONESB, onesb_t[:, :], ets[mt], ets[mt].ap[:, 0:n], start=(mt == 0), stop=(mt == 1))
                rd_ = RS.next()
                recip(rd_, rd_.ap[:, 0:n], den, den.ap[:, 0:n])
                for c in range(4):
                    kc = 4 * hx + c
                    ps = PB.next()
                    for mt in range(2):
                        mm(ps, ps.ap[:, 0:n], VX, vx_t[:, mt, kc * 128:(kc + 1) * 128], ets[mt], ets[mt].ap[:, 0:n],
                           start=(mt == 0), stop=(mt == 1))
                    tt(HB[kc], HB[kc].ap[:, c0:c0 + n], ps, ps.ap[:, 0:n], rd_, rd_.ap[:, 0:n], ALU.mult)
            ss = PSS.next()
            for mo in range(KC):
                ps = proj(w_xo[mo], HB, c0, n)
                y_chunk_out(ps, mo, c0, n, ss)
            postnorm_residual(ss, c0, n, 4)

        def ffn(T, c0, n):
            rmsnorm_to_bf(XA, c0, n, 5, HB)

            def conv3(ps, ch):
                cq = CQ.next()
                a = ACC.next()
                act(cq, cq.ap[:, 2:2 + n], ps, ps.ap[:, 0:n], AF.Copy)
                cp(cq, cq.ap[:, 0:2], FTAIL[ch], FTAIL[ch].ap)
                act(a, a.ap[:, 0:n], ps, ps.ap[:, 0:n], AF.Identity, bias=pc(P_FCB + ch), scale=pc(P_FCW + 3 * ch + 2), rd=[PRM])
                for j in range(2):
                    stt(a, a.ap[:, 0:n], cq, cq.ap[:, j:j + n], pc(P_FCW + 3 * ch + j), a, a.ap[:, 0:n],
                        ALU.mult, ALU.add, rd=[PRM])
                if T == TH:
                    ts(FTAIL[ch], FTAIL[ch].ap, cq, cq.ap[:, n:n + 2], pc(P_FLAG), None, ALU.mult, rd=[PRM])
                else:
                    cp(FTAIL[ch], FTAIL[ch].ap, cq, cq.ap[:, n:n + 2])
                return a

            for m in range(FC):
                psa = proj(w_up[m], HB, c0, n)
                aa = conv3(psa, m)
                psb = proj(w_up[FC + m], HB, c0, n)
                ab = conv3(psb, FC + m)
                act(aa, aa.ap[:, 0:n], aa, aa.ap[:, 0:n], AF.Silu)
                tt(ACTB[m], ACTB[m].ap[:, 0:n], aa, aa.ap[:, 0:n], ab, ab.ap[:, 0:n], ALU.mult)
            ss = PSS.next()
            for mo in range(KC):
                ps = PB.next()
                proj(w_down[mo, :, 0:22, :], ACTB, 0, n, nk=22, k0=0, ps=ps, first=True, last=False)
                proj(w_down[mo, :, 22:43, :], ACTB, 0, n, nk=21, k0=22, ps=ps, first=False, last=True)
                y_chunk_out(ps, mo, c0, n, ss)
            postnorm_residual(ss, c0, n, 6)

        def stream():
            pending_B = [None]

            for T in range(NTILES):
                is_main = T >= TH
                c0 = NT - HALO if T == TH else 0
                n = NT - c0
                for k in range(KC):
                    dma("sp", XA[k], XA[k].ap, None, xT[:, k, T * NT:(T + 1) * NT])
                rmsnorm_to_bf(XA, 0, NT, 0, HB)
                for sub in range(NSUB):
                    p = PSM.get()
                    for k in range(KC):
                        mm(p, p.ap[:, 0:32], HB[k], HB[k].ap[:, sub * 128:(sub + 1) * 128], WBA, wba_t[:, k, :],
                           start=(k == 0), stop=(k == KC - 1))
                    act(BG, bg_t[:, 0, sub, :], p, p.ap[:, 0:16], AF.Sigmoid)
                    ts(BG, bg_t[:, 1, sub, :], BG, bg_t[:, 0, sub, :], -1.0, None, ALU.mult)
                    f = FS.next()
                    tt(f, f.ap[:, 0:16], p, p.ap[:, 16:32], PRM, prm[:, P_DTB:P_DTB + 16], ALU.add)
                    PSM.put(p)
                    act(f, f.ap[:, 0:16], f, f.ap[:, 0:16], AF.Exp)
                    act(f, f.ap[:, 0:16], f, f.ap[:, 0:16], AF.Ln, bias=C(C_ONE, 1), rd=[CST])
                    tt(BG, bg_t[:, 2, sub, :], f, f.ap[:, 0:16], NGA, nga_t[:, :], ALU.mult)
                    gs_stage(sub)
                if T == 0:
                    dbg_dump("bg", BG, bg_t[:, :, :, :])
                    ckpt(3)

                prevB = None
                pend_gn = None

                def gnorm(hg_):
                    for hh in range(HG):
                        h = hg_ * HG + hh
                        o_ap = OR[hh].ap[:, c0:c0 + n]
                        q = SQB.next()
                        tt(q, q.ap[:, 0:n], OR[hh], o_ap, OR[hh], o_ap, ALU.mult)
                        ss = PSS.next()
                        mm(ss, ss.ap[:, 0:n], ONESB, onesb_t[:, :], q, q.ap[:, 0:n])
                        r = rstd_from(ss, n, 1.0 / 128)
                        ps = proj(w_in[48 + h], HB, c0, n)
                        zs = FS.next()
                        act(zs, zs.ap[:, 0:n], ps, ps.ap[:, 0:n], AF.Silu)
                        f = FS.next()
                        stt(f, f.ap[:, 0:n], OR[hh], o_ap, pc(P_GN), r, r.ap[:, 0:n], ALU.mult, ALU.mult, rd=[PRM])
                        tt(CB[h], CB[h].ap[:, c0:c0 + n], f, f.ap[:, 0:n], zs, zs.ap[:, 0:n], ALU.mult)
                        if T == TH + 1 and h == 0:
                            dbg_dump("or0", OR[0], OR[0].ap)

                def proj_gen(hg_):
                    KT = KTS[hg_ % 2]; QT = QTS[hg_ % 2]
                    KTB = KTBS[hg_ % 2]; VTB = VTBS[hg_ % 2]; QTB = QTBS[hg_ % 2]
                    for hh in range(HG):
                        h = hg_ * HG + hh
                        ps = proj(w_in[16 + h], HB, 0, NT)
                        conv4(ps, 16 + h, NT, 0, KT[hh], KT[hh].ap[:, 0:NT])
                        yield
                        ps = proj(w_in[32 + h], HB, 0, NT)
                        conv4(ps, 32 + h, NT, 0, VTB[hh], VTB[hh].ap[:, 0:NT])
                        yield
                        if is_main:
                            if T == TH:
                                memset(QTB[hh], QTB[hh].ap, 0.0)
                            ps = proj(w_in[h], HB, c0, n)
                            conv4(ps, h, n, c0, QT[hh], QT[hh].ap[:, c0:c0 + n])
                            yield
                    for hh in range(HG):
                        l2norm_to(KT[hh], KT[hh].ap[:, 0:NT], KTB[hh], KTB[hh].ap[:, 0:NT], NT, 1.0)
                        yield
                        if is_main:
                            l2norm_to(QT[hh], QT[hh].ap[:, c0:c0 + n], QTB[hh], QTB[hh].ap[:, c0:c0 + n], n, 128.0 ** -0.5)
                            yield

                NG = NH // HG
                interleave([proj_gen(0)])

                def chain2(a, b):
                    yield from a
                    yield from b

                if not is_main:
                    prevS = None
                    for hg in range(NG):
                        sts_ = [dict() for _ in range(NSUB)]
                        gens = [gdn_prep(hg, sub, False, sts_[sub]) for sub in range(NSUB)]
                        if prevS is not None:
                            gens.append(prevS)
                        if hg + 1 < NG:
                            gens.append(proj_gen(hg + 1))
                        interleave(gens)
                        prevS = chain2(gdn_state(hg, 0, False, sts_[0]), gdn_state(hg, 1, False, sts_[1]))
                    interleave([prevS])
                for hg in (range(NG) if is_main else []):
                    for sub in range(NSUB):
                        full = is_main and (T > TH or sub == NSUB - 1)
                        st_ = dict()
                        gens = [gdn_prep(hg, sub, full, st_)] + (prevB if prevB else [])
                        if sub == 0 and hg + 1 < NG:
                            gens.append(proj_gen(hg + 1))
                        interleave(gens)
                        if pend_gn is not None:
                            gnorm(pend_gn)
                            pend_gn = None
                        prevB = [gdn_state(hg, sub, full, st_)]
                        if sub == NSUB - 1 and is_main:
                            pend_gn = hg
                if prevB:
                    interleave(prevB)
                prevB = None
                if pend_gn is not None:
                    gnorm(pend_gn)
                    pend_gn = None
                if T == NTILES - 1:
                    dbg_dump("s0", S[0], S[0].ap)
                if T == 0:
                    ckpt(7)
                if T == TH - 1:
                    ckpt(8)
                if T == TH:
                    ckpt(9)
                if is_main:
                    if T == TH + 1:
                        for k in range(KC):
                            pass
                    mixer_rest(T, c0, n)
                    if T == TH:
                        ckpt(10)
                    if T == TH + 1:
                        dbg_dump("x1", XA[0], XA[0].ap)
                    xattn(T, c0, n)
                    if T == TH:
                        ckpt(11)
                    if T == TH + 1:
                        dbg_dump("x2", XA[0], XA[0].ap)
                    ffn(T, c0, n)
                    if T > TH:
                        o0 = (T - TH - 1) * NT
                        for k in range(KC):
                            dma("sp", None, outT[:, k, o0:o0 + NT], XA[k], XA[k].ap)
        try:
            ckpt(1)
            xattn_kv()
            ckpt(2)
            stream()
        except _Stop:
            pass
        emit_program(nc, P, es)
    return nc


def _tile_w(W):
    K, M = W.shape
    return np.ascontiguousarray(W.reshape(K // 128, 128, M // 128, 128).transpose(2, 1, 0, 3))


def _fm(v, nch):
    return np.ascontiguousarray(np.asarray(v, np.float32).reshape(nch, 128).T)


def make_consts():
    c = np.zeros((128, NCONST), np.float32)
    i = np.arange(128)
    blk = (i[:, None] // 64) == (i[None, :] // 64)
    for r in range(4):
        c[:, C_ID + 128 * r:C_ID + 128 * r + 128] = np.eye(128)
        c[:, C_UBD + 128 * r:C_UBD + 128 * r + 128] = blk & (i[:, None] <= i[None, :])
        c[:, C_MSTR + 128 * r:C_MSTR + 128 * r + 128] = blk & (i[:, None] > i[None, :])
    c[:, C_ONES:C_ONES + 128] = 1.0
    c[:, C_OBD:C_OBD + 128] = blk
    c[:, C_BSEL] = i < 64
    c[:, C_BSEL + 1] = i >= 64
    c[:, C_EPS] = EPS
    c[:, C_ONE] = 1.0
    return c


def prep_inputs(cfg, inp):
    SEQ, OWN = cfg.SEQ, cfg.OWN
    f = lambda a: np.asarray(a, np.float32)
    w_in_full = f(inp["w_in"])[0]
    cols = np.concatenate([np.arange(0, 8192), np.arange(8224, 13344)])
    shared = {
        "consts": make_consts(),
        "w_in": _tile_w(w_in_full[:, cols]),
        "wba": np.ascontiguousarray(w_in_full[:, 8192:8224].reshape(KC, 128, 32).transpose(1, 0, 2)),
        "poolw": np.ascontiguousarray(f(inp["pool_w"])[0].reshape(4, 2, 128, 256).transpose(2, 0, 1, 3)),
        "w_a": _tile_w(f(inp["w_branch_a"])[0]),
        "w_b": _tile_w(f(inp["w_branch_b"])[0]),
        "w_mix": _tile_w(f(inp["w_mix_out"])[0]),
        "w_xq": _tile_w(f(inp["w_xq"])[0]),
        "w_xkv": _tile_w(f(inp["w_xkv"])[0]),
        "w_xo": _tile_w(f(inp["w_xo"])[0]),
        "w_up": _tile_w(f(inp["w_up"])[0]),
        "w_down": _tile_w(f(inp["w_down"])[0]),
    }
    prm = np.zeros((128, NPRM), np.float32)
    for gi, nm in enumerate(["mix_pre_norm", "mix_post_norm", "xa_pre_norm", "mem_norm", "xa_post_norm",
                             "ffn_pre_norm", "ffn_post_norm"]):
        prm[:, P_GAIN + 16 * gi:P_GAIN + 16 * gi + 16] = _fm(f(inp[nm])[0], 16)
    cq = f(inp["conv_qkv"])[0]
    prm[:, P_CONVQ:P_CONVQ + 192] = cq.reshape(4, 48, 128).transpose(2, 1, 0).reshape(128, 192)
    fw = f(inp["ffn_conv_w"])[0]
    prm[:, P_FCW:P_FCW + 258] = fw.reshape(3, 86, 128).transpose(2, 1, 0).reshape(128, 258)
    prm[:, P_FCB:P_FCB + 86] = _fm(f(inp["ffn_conv_b"])[0], 86)
    prm[:, P_PSC:P_PSC + 8] = _fm(f(inp["pool_scale"])[0], 8)
    prm[:, P_GN] = f(inp["gdn_norm"])[0]
    prm[:, P_ALOG:P_ALOG + 16] = f(inp["a_log"])[0][None, :]
    prm[:, P_DTB:P_DTB + 16] = f(inp["dt_bias"])[0][None, :]
    x = f(inp["x"])
    mem = f(inp["mem"])
    in_maps = []
    for c in range(8):
        b, j = c // 4, c % 4
        end = OWN * (j + 1)
        start = end - SEQ
        xs = np.zeros((SEQ, D), np.float32)
        if start < 0:
            xs[-start:] = x[b, 0:end]
        else:
            xs[:] = x[b, start:end]
        xTc = np.ascontiguousarray(xs.T.reshape(KC, 128, SEQ).transpose(1, 0, 2))
        memTc = np.ascontiguousarray(mem[b].T.reshape(KC, 128, MEM).transpose(1, 0, 2))
        p = prm.copy()
        p[:, P_FLAG] = 0.0 if j == 0 else 1.0
        own_start = OWN * j
        t = own_start + np.arange(16)
        for g, win in enumerate((2, 4, 8, 16)):
            p[:, P_INVC + 16 * g:P_INVC + 16 * g + 16] = (1.0 / np.minimum(t + 1, win))[None, :]
        m = dict(shared)
        m.update({"xT": xTc, "memT": memTc, "prm": p})
        in_maps.append(m)
    return in_maps


_NC_CACHE = {}


def run(cfg, inp, dbg=()):
    key = (cfg.SEQ, cfg.NT, tuple(dbg))
    if key not in _NC_CACHE:
        _NC_CACHE[key] = build(cfg, dbg)
    nc = _NC_CACHE[key]
    in_maps = prep_inputs(cfg, inp)
    res = run_bass_kernel_spmd(nc, in_maps, core_ids=list(range(8)))
    B = 2
    out = np.zeros((B, cfg.SEQ, D), np.float32)
    for c in range(8):
        b, j = c // 4, c % 4
        o = res.results[c]["outT"]
        out[b, cfg.OWN * j:cfg.OWN * (j + 1), :] = o.transpose(2, 1, 0).reshape(cfg.OWN, D)
    return out, res


def kernel(**inputs):
    cfg = Cfg(SEQ=8192, NT=256)
    out, _ = run(cfg, inputs)
    return out
```

```python
import numpy as np
import concourse.bass as bass
import concourse.mybir as mybir
from concourse.bass_utils import run_bass_kernel_spmd
from contextlib import ExitStack

F32 = mybir.dt.float32
BF16 = mybir.dt.bfloat16
AF = mybir.ActivationFunctionType
ALU = mybir.AluOpType

ENGS = ("pe", "act", "dve", "pool", "sp")
NDSEM = 24


class Tl:
    __slots__ = ("ap", "name", "w", "r", "excl")

    def __init__(self, ap, name="", excl=False):
        self.ap = ap
        self.name = name
        self.w = {}
        self.r = {}
        self.excl = excl


def _add(depmap, dep):
    if dep[0] == "c":
        k = ("c", dep[1])
        if k not in depmap or depmap[k][2] < dep[2]:
            depmap[k] = dep
    else:
        depmap[dep] = dep


class Prog:
    def __init__(self, nc):
        self.nc = nc
        self.ops = {e: [] for e in ENGS}
        self.ndma = {e: 0 for e in ENGS}

    def op(self, eng, fn, reads=(), writes=(), dma=False):
        ops = self.ops[eng]
        idx = len(ops)
        deps = {}
        ex = [t for t in reads if t.excl]
        if ex:
            reads = [t for t in reads if not t.excl]
            writes = list(writes) + [t for t in ex if t not in writes]
        for t in reads:
            for d in t.w.values():
                _add(deps, d)
        for t in writes:
            for d in t.w.values():
                _add(deps, d)
            for d in t.r.values():
                _add(deps, d)
        if dma:
            n = self.ndma[eng]
            self.ndma[eng] += 1
            me = ("d", eng, n)
            if n >= NDSEM:
                _add(deps, ("d", eng, n - NDSEM))
        else:
            me = ("c", eng, idx)
        if eng == "pe":
            deps.pop(("c", "pe"), None)
        rec = dict(fn=fn, deps=list(deps.values()), me=me, needed=False)
        ops.append(rec)
        for t in writes:
            if t.r:
                t.w = {}
                t.r = {}
            _add(t.w, me)
        for t in reads:
            _add(t.r, me)
        return rec


def emit_program(nc, prog, es):
    for e in ENGS:
        for rec in prog.ops[e]:
            for d in rec["deps"]:
                if d[0] == "c":
                    prog.ops[d[1]][d[2]]["needed"] = True
    cum = {}
    for e in ENGS:
        c = 0
        arr = []
        for rec in prog.ops[e]:
            if rec["me"][0] == "c" and rec["needed"]:
                c += 1
            arr.append(c)
        cum[e] = arr
    csem = {e: es.enter_context(nc.semaphore("cs_" + e)) for e in ENGS}
    dsem = {}
    for e in ENGS:
        if prog.ndma[e] > 0:
            dsem[e] = [es.enter_context(nc.semaphore("ds_%s_%d" % (e, i)))
                       for i in range(min(NDSEM, prog.ndma[e]))]
    block = es.enter_context(nc.Block())

    def run(ename, engobj):
        waited = {}
        for rec in prog.ops[ename]:
            for d in rec["deps"]:
                if d[0] == "c":
                    sem = csem[d[1]]
                    val = cum[d[1]][d[2]]
                    key = ("c", d[1])
                else:
                    sem = dsem[d[1]][d[2] % NDSEM]
                    val = 16 * (d[2] // NDSEM + 1)
                    key = ("d", d[1], d[2] % NDSEM)
                if waited.get(key, 0) >= val:
                    continue
                waited[key] = val
                engobj.wait_ge(sem, val)
            ins = rec["fn"](engobj)
            me = rec["me"]
            if me[0] == "c":
                if rec["needed"]:
                    ins.then_inc(csem[ename], 1)
            else:
                ins.then_inc(dsem[ename][me[2] % NDSEM], 16)
        if ename in dsem:
            n = prog.ndma[ename]
            for i in range(min(NDSEM, n)):
                last = ((n - 1 - i) // NDSEM) * NDSEM + i
                engobj.wait_ge(dsem[ename][i], 16 * (last // NDSEM + 1))

    @block.tensor
    def _(eng):
        run("pe", eng)

    @block.scalar
    def _(eng):
        run("act", eng)

    @block.vector
    def _(eng):
        run("dve", eng)

    @block.gpsimd
    def _(eng):
        run("pool", eng)

    @block.sync
    def _(eng):
        run("sp", eng)


class FreeList:
    def __init__(self, tiles):
        self.free = list(tiles)
        self.total = len(tiles)
        self.low = len(tiles)

    def get(self):
        if not self.free:
            raise RuntimeError("freelist exhausted")
        t = self.free.pop(0)
        self.low = min(self.low, len(self.free))
        return t

    def put(self, t):
        self.free.append(t)


class Ring:
    def __init__(self, tiles):
        self.t = list(tiles)
        self.i = 0

    def next(self):
        t = self.t[self.i % len(self.t)]
        self.i += 1
        return t


class _Stop(Exception):
    pass


class Cfg:
    stop = 0

    def __init__(self, SEQ=8192, NT=256):
        self.SEQ = SEQ
        self.NT = NT
        self.OWN = SEQ // 4
        self.HALO = 32
        self.NTILES = SEQ // NT
        self.TH = (SEQ - self.OWN) // NT - 1
        self.NSUB = NT // 128


D = 2048
KC = 16
NH = 16
DFF = 5504
FC = 43
MEM = 256
HG = 4
EPS = 1e-6

P_GAIN = 0
P_CONVQ = 112
P_FCW = 304
P_FCB = 562
P_PSC = 648
P_GN = 656
P_ALOG = 657
P_DTB = 673
P_FLAG = 689
P_INVC = 690
NPRM = 754
C_ID = 0
C_UBD = 512
C_MSTR = 1024
C_ONES = 1536
C_OBD = 1664
C_BSEL = 1792
C_EPS = 1794
C_ONE = 1795
NCONST = 1800


def build(cfg, dbg=()):
    SEQ, NT, OWN, HALO, NTILES, TH, NSUB = cfg.SEQ, cfg.NT, cfg.OWN, cfg.HALO, cfg.NTILES, cfg.TH, cfg.NSUB
    nc = bass.Bass("TRN2", target_bir_lowering=False)
    nc.dge_precook = False

    def din(name, shape):
        return nc.dram_tensor(name, shape, F32, kind="ExternalInput").ap()

    xT = din("xT", [128, KC, SEQ])
    memT = din("memT", [128, KC, MEM])
    consts_d = din("consts", [128, NCONST])
    prm_d = din("prm", [128, NPRM])
    wba_d = din("wba", [128, KC, 32])
    poolw_d = din("poolw", [128, 4, 2, 256])
    class WRef:
        def __init__(self, src, mo, k0, k1):
            self.src, self.mo, self.k0, self.k1 = src, mo, k0, k1

    class WSrc:
        def __init__(self, name, nch, nk):
            self.name, self.nch, self.nk = name, nch, nk
            self.f32 = din(name, [nch, 128, nk, 128])
            self.b16 = nc.dram_tensor(name + "_b16", [nch, 128, nk, 128], BF16).ap()
            self.tl = [Tl(None) for _ in range(nch)]

        def __getitem__(self, idx):
            if isinstance(idx, tuple):
                mo, ks = idx[0], idx[2]
                return WRef(self, mo, ks.start, ks.stop)
            return WRef(self, idx, 0, self.nk)

    w_in = WSrc("w_in", 104, KC)
    w_a = WSrc("w_a", 16, KC)
    w_b = WSrc("w_b", 16, 8)
    w_mix = WSrc("w_mix", 16, KC)
    w_xq = WSrc("w_xq", 16, KC)
    w_xkv = WSrc("w_xkv", 32, KC)
    w_xo = WSrc("w_xo", 16, KC)
    w_up = WSrc("w_up", 86, KC)
    w_down = WSrc("w_down", 16, FC)
    outT = nc.dram_tensor("outT", [128, KC, OWN], F32, kind="ExternalOutput").ap()
    dbg_out = {}
    for name, shape in dbg:
        dbg_out[name] = nc.dram_tensor("dbg_" + name, list(shape), F32, kind="ExternalOutput").ap()

    es = ExitStack()
    with es:
        def sb(name, shape, dt=F32):
            return es.enter_context(nc.sbuf_tensor(name, list(shape), dt))

        def pst(name, shape, dt=F32):
            return es.enter_context(nc.psum_tensor(name, list(shape), dt))

        P = Prog(nc)

        def ckpt(i):
            if cfg.stop == i:
                raise _Stop()

        xa_t = sb("xa", [128, KC, NT]); XA = [Tl(xa_t[:, k, :]) for k in range(KC)]
        hb_t = sb("hb", [128, KC, NT], BF16); HB = [Tl(hb_t[:, k, :]) for k in range(KC)]
        cb_t = sb("cb", [128, KC, NT], BF16); CB = [Tl(cb_t[:, k, :]) for k in range(KC)]
        db_t = sb("db", [128, KC, NT]); DB = [Tl(db_t[:, k, :]) for k in range(KC)]
        act_t = sb("actb", [128, FC, NT], BF16); ACTB = [Tl(act_t[:, k, :]) for k in range(FC)]
        s_t = sb("state", [128, NH, 128]); S = [Tl(s_t[:, h, :]) for h in range(NH)]
        NW = 3
        w_ts = [sb("wt%d" % i, [128, 22, 128], BF16) for i in range(NW)]
        WR = Ring([Tl(t) for t in w_ts])
        kx_t = sb("kx", [128, KC, MEM], BF16); KX = Tl(kx_t)
        vx_t = sb("vx", [128, 2, D], BF16); VX = Tl(vx_t)
        cst = sb("cst", [128, NCONST]); CST = Tl(cst)
        prm = sb("prm_s", [128, NPRM]); PRM = Tl(prm)
        onesb_t = sb("onesb", [128, 128], BF16); ONESB = Tl(onesb_t)
        wba_t = sb("wba_s", [128, KC, 32], BF16); WBA = Tl(wba_t)
        poolw_t = sb("poolw_s", [128, 4, 2, 256], BF16); POOLW = Tl(poolw_t)
        nga_t = sb("nga", [128, NH]); NGA = Tl(nga_t)
        qtail_t = sb("qtail", [128, 48, 3]); QTAIL = [Tl(qtail_t[:, c, :]) for c in range(48)]
        ptail_t = sb("pbuf", [128, 8, 15 + NT]); PBUF = [Tl(ptail_t[:, c, :]) for c in range(8)]
        ftail_t = sb("ftail", [128, 86, 2]); FTAIL = [Tl(ftail_t[:, c, :]) for c in range(86)]
        bg_t = sb("bg", [128, 3, NSUB, NH]); BG = Tl(bg_t)
        kt_t = sb("kt", [128, HG, NT]); KT0 = [Tl(kt_t[:, i, :]) for i in range(HG)]
        qt_t = sb("qt", [128, HG, NT]); QT0 = [Tl(qt_t[:, i, :]) for i in range(HG)]
        ktb_t = sb("ktb", [128, HG, NT], BF16); KTB0 = [Tl(ktb_t[:, i, :]) for i in range(HG)]
        vtb_t = sb("vtb", [128, HG, NT], BF16); VTB0 = [Tl(vtb_t[:, i, :]) for i in range(HG)]
        qtb_t = sb("qtb", [128, HG, NT], BF16); QTB0 = [Tl(qtb_t[:, i, :]) for i in range(HG)]
        act32 = act_t.bitcast(F32)
        cpr = NT // 128

        def alias32(j):
            return Tl(act32[:, j * cpr:(j + 1) * cpr, :].rearrange("p a b -> p (a b)"))
        KT1 = [alias32(i) for i in range(HG)]
        QT1 = [alias32(HG + i) for i in range(HG)]
        r0 = 2 * HG * cpr
        KTB1 = [Tl(act_t[:, r0 + i, :]) for i in range(HG)]
        VTB1 = [Tl(act_t[:, r0 + HG + i, :]) for i in range(HG)]
        QTB1 = [Tl(act_t[:, r0 + 2 * HG + i, :]) for i in range(HG)]
        r1 = r0 + 3 * HG
        assert r1 + 8 <= FC
        KTS = [KT0, KT1]; QTS = [QT0, QT1]
        KTBS = [KTB0, KTB1]; VTBS = [VTB0, VTB1]; QTBS = [QTB0, QTB1]
        or_t = sb("or", [128, HG, NT]); OR = [Tl(or_t[:, i, :]) for i in range(HG)]
        NTMP = 7
        tmp_t = sb("gtmp", [128, NTMP, 512])
        TMP = FreeList([Tl(tmp_t[:, i, :]) for i in range(NTMP)])
        NTMPB = 17
        tmpb_t = sb("gtmpb", [128, NTMPB, 512], BF16)
        TMPB = FreeList([Tl(tmpb_t[:, i, :]) for i in range(NTMPB)])
        identb_t = sb("identb", [128, 128], BF16); IDENTB = Tl(identb_t)
        db16 = db_t.bitcast(BF16)
        for j_ in range(8):
            TMPB.put(Tl(db16[:, 2 * j_:2 * j_ + 2, :].rearrange("p a b -> p (a b)")[:, 0:512]))
        gs_t = sb("gs", [128, NSUB, 8, NH]); GS = Tl(gs_t)
        cq_t = sb("cq", [128, 2, 3 + NT]); CQ = Ring([Tl(cq_t[:, i, :]) for i in range(2)])
        acc_t = sb("acc", [128, 3, NT]); ACC = Ring([Tl(acc_t[:, i, :]) for i in range(3)])
        sqb_t = sb("sqb", [128, 3, NT], BF16); SQB = Ring([Tl(sqb_t[:, i, :]) for i in range(3)])
        rs_t = sb("rs", [128, 2, NT]); RS = Ring([Tl(rs_t[:, i, :]) for i in range(2)])
        f32s_t = sb("f32s", [128, 3, NT]); FS = Ring([Tl(f32s_t[:, i, :]) for i in range(3)])
        et_t = sb("et", [128, 4, NT], BF16); ET = Ring([Tl(et_t[:, i, :]) for i in range(4)])
        YPI = [Tl(act_t[:, r1 + i, :]) for i in range(8)]
        ypo_t = sb("ypo", [128, 8, NT], BF16); YPO = [Tl(ypo_t[:, i, :]) for i in range(8)]
        pw_t = sb("pw", [128, 2, 2, 15 + NT]); PW = [Tl(pw_t[:, i, :, :]) for i in range(2)]

        pbig = [pst("pbig%d" % i, [128, 512]) for i in range(2)]
        PB = Ring([Tl(pbig[i][:, :], excl=True) for i in range(2)])
        pss = pst("pss", [128, 512])
        PSS = Ring([Tl(pss[:, :], excl=True)])
        psm = [pst("psm%d" % i, [128, 512]) for i in range(5)]
        PSM = FreeList([Tl(psm[i][:, :], excl=True) for i in range(5)])
        PB_STREAM = PB
        PB_MAIN = Ring(PB.t + PSM.free[0:4])
        pbsel = [PB_STREAM]

        def C(c0, n=128):
            return cst[:, c0:c0 + n]

        def pc(c):
            return prm[:, c:c + 1]

        def mm(out_tl, out_ap, a_tl, a_ap, b_tl, b_ap, start=True, stop=True):
            P.op("pe", lambda e: e.matmul(out_ap, a_ap, b_ap, start=start, stop=stop),
                 reads=[a_tl, b_tl], writes=[out_tl])

        def act(out_tl, out_ap, in_tl, in_ap, func, bias=None, scale=None, rd=()):
            kw = {}
            if bias is not None:
                kw["bias"] = bias
            if scale is not None:
                kw["scale"] = scale
            P.op("act", lambda e: e.activation(out=out_ap, in_=in_ap, func=func, **kw),
                 reads=[in_tl] + list(rd), writes=[out_tl])

        def ts(out_tl, out_ap, in_tl, in_ap, s1, s2, op0, op1=None, rd=(), eng="dve"):
            if op1 is None:
                P.op(eng, lambda e: e.tensor_scalar(out_ap, in_ap, s1, None, op0),
                     reads=[in_tl] + list(rd), writes=[out_tl])
            else:
                P.op(eng, lambda e: e.tensor_scalar(out_ap, in_ap, s1, s2, op0, op1),
                     reads=[in_tl] + list(rd), writes=[out_tl])

        def stt(out_tl, out_ap, in0_tl, in0_ap, scalar, in1_tl, in1_ap, op0, op1, rd=()):
            P.op("dve", lambda e: e.scalar_tensor_tensor(out_ap, in0_ap, scalar, in1_ap, op0, op1),
                 reads=[in0_tl, in1_tl] + list(rd), writes=[out_tl])

        def tt(out_tl, out_ap, a_tl, a_ap, b_tl, b_ap, op, eng="dve"):
            P.op(eng, lambda e: e.tensor_tensor(out_ap, a_ap, b_ap, op),
                 reads=[a_tl, b_tl], writes=[out_tl])

        def cp(out_tl, out_ap, in_tl, in_ap, eng="dve"):
            P.op(eng, lambda e: e.tensor_copy(out_ap, in_ap), reads=[in_tl], writes=[out_tl])

        def recip(out_tl, out_ap, in_tl, in_ap):
            P.op("dve", lambda e: e.reciprocal(out_ap, in_ap), reads=[in_tl], writes=[out_tl])

        def memset(tl, ap, val, eng="dve"):
            P.op(eng, lambda e: e.memset(ap, val), writes=[tl])

        def dma(eng, out_tl, out_ap, in_tl, in_ap):
            P.op(eng, lambda e: e.dma_start(out=out_ap, in_=in_ap),
                 reads=[in_tl] if in_tl is not None else [],
                 writes=[out_tl] if out_tl is not None else [], dma=True)

        def dbg_dump(name, tl, ap):
            if name in dbg_out:
                dma("sp", None, dbg_out[name], tl, ap)

        def load_w(ref, nk):
            w = WR.next()
            assert ref.k1 - ref.k0 == nk
            dma("sp", w, w.ap[:, 0:nk, :], ref.src.tl[ref.mo], ref.src.b16[ref.mo, :, ref.k0:ref.k1, :])
            return w

        def cast_weights(src, chunks):
            for mo in chunks:
                dma("pool", src.tl[mo], src.b16[mo], None, src.f32[mo])

        def proj(wd, in_list, c0, n, nk=KC, k0=0, ps=None, first=True, last=True):
            w = load_w(wd, nk)
            if ps is None:
                ps = pbsel[0].next()
            for k in range(nk):
                mm(ps, ps.ap[:, 0:n], w, w.ap[:, k, :], in_list[k0 + k], in_list[k0 + k].ap[:, c0:c0 + n],
                   start=(first and k == 0), stop=(last and k == nk - 1))
            return ps

        def rstd_from(ss, n, scale, out=None):
            r = RS.next() if out is None else out
            act(r, r.ap[:, 0:n], ss, ss.ap[:, 0:n], AF.Sqrt, bias=C(C_EPS, 1), scale=scale, rd=[CST])
            recip(r, r.ap[:, 0:n], r, r.ap[:, 0:n])
            return r

        def rmsnorm_to_bf(src, c0, n, gi, dst):
            ss = PSS.next()
            for k in range(KC):
                q = SQB.next()
                act(q, q.ap[:, 0:n], src[k], src[k].ap[:, c0:c0 + n], AF.Square)
                mm(ss, ss.ap[:, 0:n], ONESB, onesb_t[:, :], q, q.ap[:, 0:n], start=(k == 0), stop=(k == KC - 1))
            r = rstd_from(ss, n, 1.0 / D)
            for k in range(KC):
                stt(dst[k], dst[k].ap[:, c0:c0 + n], src[k], src[k].ap[:, c0:c0 + n], pc(P_GAIN + 16 * gi + k),
                    r, r.ap[:, 0:n], ALU.mult, ALU.mult, rd=[PRM])

        def postnorm_residual(ss, c0, n, gi):
            r = rstd_from(ss, n, 1.0 / D)
            for k in range(KC):
                f = FS.next()
                tt(f, f.ap[:, 0:n], DB[k], DB[k].ap[:, c0:c0 + n], r, r.ap[:, 0:n], ALU.mult)
                stt(XA[k], XA[k].ap[:, c0:c0 + n], f, f.ap[:, 0:n], pc(P_GAIN + 16 * gi + k),
                    XA[k], XA[k].ap[:, c0:c0 + n], ALU.mult, ALU.add, rd=[PRM])

        def y_chunk_out(ps, mo, c0, n, ss):
            act(DB[mo], DB[mo].ap[:, c0:c0 + n], ps, ps.ap[:, 0:n], AF.Copy)
            q = SQB.next()
            act(q, q.ap[:, 0:n], ps, ps.ap[:, 0:n], AF.Square)
            mm(ss, ss.ap[:, 0:n], ONESB, onesb_t[:, :], q, q.ap[:, 0:n], start=(mo == 0), stop=(mo == KC - 1))

        def body():
            pass

        dma("sp", CST, cst[:, :], None, consts_d)
        dma("sp", PRM, prm[:, :], None, prm_d)
        dma("pool", WBA, wba_t[:, :, :], None, wba_d)
        dma("pool", POOLW, poolw_t[:, :, :, :], None, poolw_d)
        cast_weights(w_xkv, range(32))
        cast_weights(w_in, range(16, 48))
        cast_weights(w_in, list(range(0, 16)) + list(range(48, 104)))
        for src_ in (w_a, w_b, w_mix, w_xq, w_xo, w_up, w_down):
            cast_weights(src_, range(src_.nch))
        cp(ONESB, onesb_t[:, :], CST, C(C_ONES))
        cp(IDENTB, identb_t[:, :], CST, C(C_ID))
        for h in range(NH):
            memset(S[h], S[h].ap, 0.0)
        for c in range(48):
            memset(QTAIL[c], QTAIL[c].ap, 0.0)
        for c in range(8):
            memset(PBUF[c], PBUF[c].ap, 0.0)
        for c in range(86):
            memset(FTAIL[c], FTAIL[c].ap, 0.0)
        act(NGA, nga_t[:, :], PRM, prm[:, P_ALOG:P_ALOG + 16], AF.Exp)
        ts(NGA, nga_t[:, :], NGA, nga_t[:, :], -1.0, None, ALU.mult)

        def xattn_kv():
            assert NT == MEM
            MXl = DB
            MTl = HB
            for k in range(KC):
                dma("sp", MXl[k], MXl[k].ap, None, memT[:, k, :])
            ckpt(21)
            rmsnorm_to_bf(MXl, 0, MEM, 3, MTl)
            ckpt(22)
            for mo in range(KC):
                ps = proj(w_xkv[mo], MTl, 0, MEM)
                act(KX, kx_t[:, mo, :], ps, ps.ap[:, 0:MEM], AF.Copy)
                ckpt(100 + mo)
            ckpt(24)
            for mo in range(KC):
                ps = proj(w_xkv[KC + mo], MTl, 0, MEM)
                f = FS.next()
                act(f, f.ap[:, 0:MEM], ps, ps.ap[:, 0:MEM], AF.Copy)
                for mt in range(2):
                    p2 = PSM.get()
                    mm(p2, p2.ap[:, 0:128], f, f.ap[:, mt * 128:(mt + 1) * 128], CST, C(C_ID))
                    cp(VX, vx_t[:, mt, mo * 128:(mo + 1) * 128], p2, p2.ap[:, 0:128])
                    PSM.put(p2)

        def conv4(ps, ch, n, c0, out_tl, out_ap_silu):
            cq = CQ.next()
            a = ACC.next()
            act(cq, cq.ap[:, 3:3 + n], ps, ps.ap[:, 0:n], AF.Copy)
            cp(cq, cq.ap[:, 0:3], QTAIL[ch], QTAIL[ch].ap)
            act(a, a.ap[:, 0:n], ps, ps.ap[:, 0:n], AF.Identity, scale=pc(P_CONVQ + 4 * ch + 3), rd=[PRM])
            for j in range(3):
                stt(a, a.ap[:, 0:n], cq, cq.ap[:, j:j + n], pc(P_CONVQ + 4 * ch + j), a, a.ap[:, 0:n],
                    ALU.mult, ALU.add, rd=[PRM])
            cp(QTAIL[ch], QTAIL[ch].ap, cq, cq.ap[:, n:n + 3])
            act(out_tl, out_ap_silu, a, a.ap[:, 0:n], AF.Silu)

        def l2norm_to(tl, ap, otl, oap, n, mul):
            q = SQB.next()
            tt(q, q.ap[:, 0:n], tl, ap, tl, ap, ALU.mult)
            ss = PSS.next()
            mm(ss, ss.ap[:, 0:n], ONESB, onesb_t[:, :], q, q.ap[:, 0:n])
            r = rstd_from(ss, n, 1.0)
            if mul == 1.0:
                tt(otl, oap, tl, ap, r, r.ap[:, 0:n], ALU.mult)
            else:
                stt(otl, oap, tl, ap, mul, r, r.ap[:, 0:n], ALU.mult, ALU.mult)

        def Q4(t, i):
            return t.ap[:, i * 128:(i + 1) * 128]

        def gs_stage(sub):
            g = bg_t[:, 2, sub, :]
            f = FS.next()
            for c in range(2):
                ts(f, f.ap[:, 16 * c:16 * c + 16], BG, g, C(C_BSEL + c, 1), None, ALU.mult, rd=[CST])
            bk = PSM.get()
            mm(bk, bk.ap[:, 0:16], CST, C(C_UBD), BG, g)
            mm(bk, bk.ap[:, 16:32], CST, C(C_OBD), BG, g)
            mm(bk, bk.ap[:, 32:64], CST, C(C_ONES), f, f.ap[:, 0:32])
            for r_ in range(4):
                cp(GS, gs_t[:, sub, r_, :], bk, bk.ap[:, 16 * r_:16 * r_ + 16])
            PSM.put(bk)
            f2 = FS.next()
            act(f2, f2.ap[:, 0:16], GS, gs_t[:, sub, 0, :], AF.Exp)
            tt(GS, gs_t[:, sub, 4, :], f2, f2.ap[:, 0:16], BG, bg_t[:, 0, sub, :], ALU.mult)
            tt(f2, f2.ap[:, 16:32], GS, gs_t[:, sub, 1, :], GS, gs_t[:, sub, 0, :], ALU.subtract)
            act(GS, gs_t[:, sub, 5, :], f2, f2.ap[:, 16:32], AF.Exp)
            for r_ in range(2):
                act(GS, gs_t[:, sub, 6 + r_, :], GS, gs_t[:, sub, 2 + r_, :], AF.Exp)

        def gdn_prep(hg, sub, full, st):
            cs = slice(sub * 128, sub * 128 + 128)
            heads = [hg * HG + i for i in range(HG)]
            KT = KTBS[hg % 2]; VT = VTBS[hg % 2]; QT = QTBS[hg % 2]

            def gcol(row, h):
                return gs_t[:, sub, row, h:h + 1]
            gd = TMP.get()
            for i, h in enumerate(heads):
                ts(gd, Q4(gd, i), CST, C(C_ID), gcol(0, h), -1.0, ALU.mult, ALU.mult, rd=[GS])
            bk = PSM.get()
            for i in range(HG):
                mm(bk, Q4(bk, i), CST, C(C_ONES), gd, Q4(gd, i))
            yield
            Dm = TMP.get()
            for i, h in enumerate(heads):
                ts(Dm, Q4(Dm, i), bk, Q4(bk, i), gcol(0, h), 0.0, ALU.add, ALU.min, rd=[GS])
            if full:
                DT = TMP.get(); eG = TMP.get()
                for i, h in enumerate(heads):
                    ts(DT, Q4(DT, i), bk, Q4(bk, i), -1.0, gcol(0, h), ALU.mult, ALU.subtract, rd=[GS])
                ts(DT, DT.ap, DT, DT.ap, 0.0, None, ALU.min)
                act(eG, eG.ap, bk, bk.ap, AF.Exp, scale=-1.0)
            PSM.put(bk)
            TMP.put(gd)
            yield
            act(Dm, Dm.ap, Dm, Dm.ap, AF.Exp)
            if full:
                act(DT, DT.ap, DT, DT.ap, AF.Exp)
            yield
            tt(Dm, Dm.ap, Dm, Dm.ap, CST, C(C_MSTR, 512), ALU.mult)
            if full:
                tt(DT, DT.ap, DT, DT.ap, CST, C(C_UBD, 512), ALU.mult)
            bk = PSM.get()
            for i in range(HG):
                mm(bk, Q4(bk, i), KT[i], KT[i].ap[:, cs], KT[i], KT[i].ap[:, cs])
            yield
            N0 = TMPB.get()
            for i, h in enumerate(heads):
                stt(N0, Q4(N0, i), bk, Q4(bk, i), bg_t[:, 1, sub, h:h + 1], Dm, Q4(Dm, i), ALU.mult, ALU.mult, rd=[BG])
            PSM.put(bk)
            TMP.put(Dm)
            yield
            bk = PSM.get()
            for i in range(HG):
                mm(bk, Q4(bk, i), N0, Q4(N0, i), IDENTB, identb_t[:, :])
            if full:
                bk2 = PSM.get()
                for i in range(HG):
                    mm(bk2, Q4(bk2, i), KT[i], KT[i].ap[:, cs], QT[i], QT[i].ap[:, cs])
            yield
            N0T = TMPB.get(); TT = TMPB.get()
            act(N0T, N0T.ap, bk, bk.ap, AF.Copy)
            tt(TT, TT.ap, bk, bk.ap, CST, C(C_ID, 512), ALU.add)
            PSM.put(bk)
            if full:
                AT = TMPB.get(); Qd = TMPB.get()
                tt(AT, AT.ap, bk2, bk2.ap, DT, DT.ap, ALU.mult)
                PSM.put(bk2)
                for i in range(HG):
                    tt(Qd, Q4(Qd, i), QT[i], QT[i].ap[:, cs], eG, Q4(eG, i), ALU.mult)
                TMP.put(DT); TMP.put(eG)
                st["AT"] = AT; st["Qd"] = Qd
            yield
            Pk, PTk = N0, N0T
            for k in range(5):
                b1 = PSM.get()
                for i in range(HG):
                    mm(b1, Q4(b1, i), PTk, Q4(PTk, i), Pk, Q4(Pk, i))
                if k < 4:
                    b2 = PSM.get()
                    for i in range(HG):
                        mm(b2, Q4(b2, i), Pk, Q4(Pk, i), PTk, Q4(PTk, i))
                yield
                Pn = TMPB.get()
                act(Pn, Pn.ap, b1, b1.ap, AF.Copy)
                PSM.put(b1)
                if k < 4:
                    PTn = TMPB.get()
                    cp(PTn, PTn.ap, b2, b2.ap)
                    PSM.put(b2)
                yield
                b3 = PSM.get()
                for i in range(HG):
                    mm(b3, Q4(b3, i), Pn, Q4(Pn, i), TT, Q4(TT, i))
                yield
                tt(TT, TT.ap, TT, TT.ap, b3, b3.ap, ALU.add)
                PSM.put(b3)
                TMPB.put(Pk); TMPB.put(PTk)
                Pk = Pn
                PTk = PTn if k < 4 else None
                yield
            TMPB.put(Pk)
            bK = PSM.get(); bV = PSM.get()
            for i in range(HG):
                mm(bK, Q4(bK, i), KT[i], KT[i].ap[:, cs], IDENTB, identb_t[:, :])
                mm(bV, Q4(bV, i), VT[i], VT[i].ap[:, cs], IDENTB, identb_t[:, :])
            yield
            Rv = TMPB.get(); Rk = TMPB.get(); Kd = TMPB.get()
            for i, h in enumerate(heads):
                ts(Rv, Q4(Rv, i), bV, Q4(bV, i), bg_t[:, 0, sub, h:h + 1], None, ALU.mult, rd=[BG])
                ts(Rk, Q4(Rk, i), bK, Q4(bK, i), gcol(4, h), None, ALU.mult, rd=[GS])
                act(Kd, Q4(Kd, i), bK, Q4(bK, i), AF.Identity, scale=gcol(5, h), rd=[GS])
            PSM.put(bK); PSM.put(bV)
            yield
            bU = PSM.get(); bW = PSM.get()
            for i in range(HG):
                mm(bU, Q4(bU, i), TT, Q4(TT, i), Rv, Q4(Rv, i))
                mm(bW, Q4(bW, i), Rk, Q4(Rk, i), TT, Q4(TT, i))
            yield
            TMPB.put(Rv); TMPB.put(Rk); TMPB.put(TT)
            Uv = TMP.get(); WkT = TMPB.get()
            act(Uv, Uv.ap, bU, bU.ap, AF.Copy)
            cp(WkT, WkT.ap, bW, bW.ap)
            PSM.put(bU); PSM.put(bW)
            st["Uv"] = Uv; st["WkT"] = WkT; st["Kd"] = Kd
            yield

        def gdn_state(hg, sub, full, st):
            heads = [hg * HG + i for i in range(HG)]
            Uv = st["Uv"]; WkT = st["WkT"]; Kd = st["Kd"]
            u = TMPB.get(); Sb = TMPB.get()
            Sg = [S[h] for h in heads]
            sg_ap = s_t[:, hg * HG:(hg + 1) * HG, :].rearrange("p a b -> p (a b)")
            P.op("act", lambda e: e.activation(out=Sb.ap, in_=sg_ap, func=AF.Copy), reads=Sg, writes=[Sb])
            yield
            for c in range(2):
                r = slice(64 * c, 64 * c + 64)
                bk = PSM.get()
                for i, h in enumerate(heads):
                    if c == 0:
                        mm(bk, bk.ap[0:64, i * 128:(i + 1) * 128], WkT, WkT.ap[:, i * 128:i * 128 + 64], Sb, Q4(Sb, i))
                    else:
                        mm(bk, Q4(bk, i), WkT, Q4(WkT, i), Sb, Q4(Sb, i))
                yield
                tt(u, u.ap[r, :], Uv, Uv.ap[r, :], bk, bk.ap[r, :], ALU.subtract)
                PSM.put(bk)
                yield
                bk = PSM.get()
                for i, h in enumerate(heads):
                    mm(bk, Q4(bk, i), Kd, Kd.ap[r, i * 128:(i + 1) * 128], u, u.ap[r, i * 128:(i + 1) * 128])
                if full:
                    bo = PSM.get()
                    Qd = st["Qd"]; AT = st["AT"]
                    for i, h in enumerate(heads):
                        oq = bo.ap[:, i * 128:i * 128 + 64]
                        mm(bo, oq, Sb, Q4(Sb, i), Qd, Qd.ap[:, i * 128 + 64 * c:i * 128 + 64 * c + 64], start=True, stop=False)
                        mm(bo, oq, u, u.ap[r, i * 128:(i + 1) * 128], AT, AT.ap[r, i * 128 + 64 * c:i * 128 + 64 * c + 64],
                           start=False, stop=True)
                yield
                for i, h in enumerate(heads):
                    stt(S[h], S[h].ap, S[h], S[h].ap, gs_t[:, sub, 6 + c, h:h + 1], bk, Q4(bk, i), ALU.mult, ALU.add, rd=[GS])
                PSM.put(bk)
                if full:
                    o0 = sub * 128 + 64 * c
                    for i in range(HG):
                        act(OR[i], OR[i].ap[:, o0:o0 + 64], bo, bo.ap[:, i * 128:i * 128 + 64], AF.Copy)
                    PSM.put(bo)
                yield
                if c == 0:
                    P.op("act", lambda e: e.activation(out=Sb.ap, in_=sg_ap, func=AF.Copy), reads=Sg, writes=[Sb])
                    yield
            TMPB.put(u); TMPB.put(Sb); TMP.put(Uv); TMPB.put(WkT); TMPB.put(Kd)
            if full:
                TMPB.put(st["AT"]); TMPB.put(st["Qd"])

        def interleave(gens):
            gens = list(gens)
            while gens:
                nxt = []
                for g in gens:
                    try:
                        next(g)
                        nxt.append(g)
                    except StopIteration:
                        pass
                gens = nxt

        def mixer_rest(T, c0, n):
            for mo in range(KC):
                ps1 = proj(w_in[72 + mo], HB, c0, n)
                sg = FS.next()
                act(sg, sg.ap[:, 0:n], ps1, ps1.ap[:, 0:n], AF.Sigmoid)
                ps2 = proj(w_a[mo], CB, c0, n)
                tt(DB[mo], DB[mo].ap[:, c0:c0 + n], ps2, ps2.ap[:, 0:n], sg, sg.ap[:, 0:n], ALU.mult)
            for pcix in range(8):
                ps = proj(w_in[64 + pcix], HB, c0, n)
                act(PBUF[pcix], PBUF[pcix].ap[:, 15:15 + n], ps, ps.ap[:, 0:n], AF.Copy)
            L = 15 + n
            for g in range(4):
                win = 2 << g
                src_tl = None
                src = ptail_t[:, 2 * g:2 * g + 2, :]
                srcs = [PBUF[2 * g], PBUF[2 * g + 1]]
                sh = 1
                wi = 0
                cur = src
                cur_tls = srcs
                for lvl in range(g + 1):
                    dst = PW[wi % 2]
                    wi += 1
                    P.op("dve", (lambda d=dst.ap, s=cur, sh=sh: lambda e: e.tensor_tensor(d[:, :, sh:L], s[:, :, sh:L], s[:, :, 0:L - sh], ALU.add))(),
                         reads=list(cur_tls), writes=[dst])
                    cur = dst.ap
                    cur_tls = [dst]
                    sh *= 2
                for i in range(2):
                    yp = YPI[2 * g + i]
                    stt(yp, yp.ap[:, 0:n], cur_tls[0], cur[:, i, 15:15 + n], 1.0 / win,
                        PBUF[2 * g + i], PBUF[2 * g + i].ap[:, 15:15 + n], ALU.mult, ALU.subtract)
                    if T == TH + 1:
                        f = FS.next()
                        tt(f, f.ap[:, 0:16], cur_tls[0], cur[:, i, 15:31], PRM, prm[:, P_INVC + 16 * g:P_INVC + 16 * g + 16], ALU.mult)
                        tt(yp, yp.ap[:, 0:16], f, f.ap[:, 0:16], PBUF[2 * g + i], PBUF[2 * g + i].ap[:, 15:31], ALU.subtract)
            for pcix in range(8):
                f = FS.next()
                cp(f, f.ap[:, 0:15], PBUF[pcix], PBUF[pcix].ap[:, n:n + 15])
                cp(PBUF[pcix], PBUF[pcix].ap[:, 0:15], f, f.ap[:, 0:15])
            for g in range(4):
                for mo2 in range(2):
                    ps = pbsel[0].next()
                    for ki in range(2):
                        mm(ps, ps.ap[:, 0:n], POOLW, poolw_t[:, g, ki, mo2 * 128:(mo2 + 1) * 128],
                           YPI[2 * g + ki], YPI[2 * g + ki].ap[:, 0:n], start=(ki == 0), stop=(ki == 1))
                    yo = YPO[2 * g + mo2]
                    act(yo, yo.ap[:, 0:n], ps, ps.ap[:, 0:n], AF.Identity, scale=pc(P_PSC + 2 * g + mo2), rd=[PRM])
            YPOc = [Tl(None)] * 0
            for mo in range(KC):
                ps1 = proj(w_in[88 + mo], HB, c0, n)
                sg = FS.next()
                act(sg, sg.ap[:, 0:n], ps1, ps1.ap[:, 0:n], AF.Sigmoid)
                w = load_w(w_b[mo], 8)
                ps2 = pbsel[0].next()
                for k in range(8):
                    mm(ps2, ps2.ap[:, 0:n], w, w.ap[:, k, :], YPO[k], YPO[k].ap[:, 0:n], start=(k == 0), stop=(k == 7))
                f = FS.next()
                tt(f, f.ap[:, 0:n], ps2, ps2.ap[:, 0:n], sg, sg.ap[:, 0:n], ALU.mult)
                tt(CB[mo], CB[mo].ap[:, c0:c0 + n], f, f.ap[:, 0:n], DB[mo], DB[mo].ap[:, c0:c0 + n], ALU.add)
            ss = PSS.next()
            for mo in range(KC):
                ps = proj(w_mix[mo], CB, c0, n)
                y_chunk_out(ps, mo, c0, n, ss)
            postnorm_residual(ss, c0, n, 1)

        def xattn(T, c0, n):
            rmsnorm_to_bf(XA, c0, n, 2, HB)
            for mo in range(KC):
                ps = proj(w_xq[mo], HB, c0, n)
                act(CB[mo], CB[mo].ap[:, c0:c0 + n], ps, ps.ap[:, 0:n], AF.Copy)
            scl = 512.0 ** -0.5
            for hx in range(4):
                ets = []
                for mt in range(2):
                    ps = pbsel[0].next()
                    for c in range(4):
                        kc = 4 * hx + c
                        mm(ps, ps.ap[:, 0:n], KX, kx_t[:, kc, mt * 128:(mt + 1) * 128], CB[kc], CB[kc].ap[:, c0:c0 + n],
                           start=(c == 0), stop=(c == 3))
                    e_ = ET.next()
                    act(e_, e_.ap[:, 0:n], ps, ps.ap[:, 0:n], AF.Exp, scale=scl)
                    ets.append(e_)
                den = PSS.next()
                for mt in range(2):
                    mm(den, den.ap[:, 0:n], ONESB, onesb_t[:, :], ets[mt], ets[mt].ap[:, 0:n], start=(mt == 0), stop=(mt == 1))
                rd_ = RS.next()
                recip(rd_, rd_.ap[:, 0:n], den, den.ap[:, 0:n])
                for c in range(4):
                    kc = 4 * hx + c
                    ps = pbsel[0].next()
                    for mt in range(2):
                        mm(ps, ps.ap[:, 0:n], VX, vx_t[:, mt, kc * 128:(kc + 1) * 128], ets[mt], ets[mt].ap[:, 0:n],
                           start=(mt == 0), stop=(mt == 1))
                    tt(HB[kc], HB[kc].ap[:, c0:c0 + n], ps, ps.ap[:, 0:n], rd_, rd_.ap[:, 0:n], ALU.mult)
            ss = PSS.next()
            for mo in range(KC):
                ps = proj(w_xo[mo], HB, c0, n)
                y_chunk_out(ps, mo, c0, n, ss)
            postnorm_residual(ss, c0, n, 4)

        def ffn(T, c0, n):
            rmsnorm_to_bf(XA, c0, n, 5, HB)

            def conv3(ps, ch):
                cq = CQ.next()
                a = ACC.next()
                act(cq, cq.ap[:, 2:2 + n], ps, ps.ap[:, 0:n], AF.Copy)
                cp(cq, cq.ap[:, 0:2], FTAIL[ch], FTAIL[ch].ap)
                act(a, a.ap[:, 0:n], ps, ps.ap[:, 0:n], AF.Identity, bias=pc(P_FCB + ch), scale=pc(P_FCW + 3 * ch + 2), rd=[PRM])
                for j in range(2):
                    stt(a, a.ap[:, 0:n], cq, cq.ap[:, j:j + n], pc(P_FCW + 3 * ch + j), a, a.ap[:, 0:n],
                        ALU.mult, ALU.add, rd=[PRM])
                if T == TH:
                    ts(FTAIL[ch], FTAIL[ch].ap, cq, cq.ap[:, n:n + 2], pc(P_FLAG), None, ALU.mult, rd=[PRM])
                else:
                    cp(FTAIL[ch], FTAIL[ch].ap, cq, cq.ap[:, n:n + 2])
                return a

            if T == TH:
                for ch in range(2 * FC):
                    ps = proj(w_up[ch], HB, c0, n)
                    ts(FTAIL[ch], FTAIL[ch].ap, ps, ps.ap[:, n - 2:n], pc(P_FLAG), None, ALU.mult, rd=[PRM])
                return
            for m in range(FC):
                psa = proj(w_up[m], HB, c0, n)
                aa = conv3(psa, m)
                psb = proj(w_up[FC + m], HB, c0, n)
                ab = conv3(psb, FC + m)
                act(aa, aa.ap[:, 0:n], aa, aa.ap[:, 0:n], AF.Silu)
                tt(ACTB[m], ACTB[m].ap[:, 0:n], aa, aa.ap[:, 0:n], ab, ab.ap[:, 0:n], ALU.mult)
            ss = PSS.next()
            for mo in range(KC):
                ps = pbsel[0].next()
                proj(w_down[mo, :, 0:22, :], ACTB, 0, n, nk=22, k0=0, ps=ps, first=True, last=False)
                proj(w_down[mo, :, 22:43, :], ACTB, 0, n, nk=21, k0=22, ps=ps, first=False, last=True)
                y_chunk_out(ps, mo, c0, n, ss)
            postnorm_residual(ss, c0, n, 6)

        def stream():
            pending_B = [None]

            for T in range(NTILES):
                is_main = T >= TH
                c0 = NT - HALO if T == TH else 0
                n = NT - c0
                for k in range(KC):
                    dma("sp", XA[k], XA[k].ap, None, xT[:, k, T * NT:(T + 1) * NT])
                rmsnorm_to_bf(XA, 0, NT, 0, HB)
                for sub in range(NSUB):
                    p = PSM.get()
                    for k in range(KC):
                        mm(p, p.ap[:, 0:32], HB[k], HB[k].ap[:, sub * 128:(sub + 1) * 128], WBA, wba_t[:, k, :],
                           start=(k == 0), stop=(k == KC - 1))
                    act(BG, bg_t[:, 0, sub, :], p, p.ap[:, 0:16], AF.Sigmoid)
                    ts(BG, bg_t[:, 1, sub, :], BG, bg_t[:, 0, sub, :], -1.0, None, ALU.mult)
                    f = FS.next()
                    tt(f, f.ap[:, 0:16], p, p.ap[:, 16:32], PRM, prm[:, P_DTB:P_DTB + 16], ALU.add)
                    PSM.put(p)
                    act(f, f.ap[:, 0:16], f, f.ap[:, 0:16], AF.Exp)
                    act(f, f.ap[:, 0:16], f, f.ap[:, 0:16], AF.Ln, bias=C(C_ONE, 1), rd=[CST])
                    tt(BG, bg_t[:, 2, sub, :], f, f.ap[:, 0:16], NGA, nga_t[:, :], ALU.mult)
                    gs_stage(sub)
                if T == 0:
                    dbg_dump("bg", BG, bg_t[:, :, :, :])
                    ckpt(3)

                prevB = None
                pend_gn = None

                def gnorm(hg_):
                    for hh in range(HG):
                        h = hg_ * HG + hh
                        o_ap = OR[hh].ap[:, c0:c0 + n]
                        q = SQB.next()
                        tt(q, q.ap[:, 0:n], OR[hh], o_ap, OR[hh], o_ap, ALU.mult)
                        ss = PSS.next()
                        mm(ss, ss.ap[:, 0:n], ONESB, onesb_t[:, :], q, q.ap[:, 0:n])
                        r = rstd_from(ss, n, 1.0 / 128)
                        ps = proj(w_in[48 + h], HB, c0, n)
                        zs = FS.next()
                        act(zs, zs.ap[:, 0:n], ps, ps.ap[:, 0:n], AF.Silu)
                        f = FS.next()
                        stt(f, f.ap[:, 0:n], OR[hh], o_ap, pc(P_GN), r, r.ap[:, 0:n], ALU.mult, ALU.mult, rd=[PRM])
                        tt(CB[h], CB[h].ap[:, c0:c0 + n], f, f.ap[:, 0:n], zs, zs.ap[:, 0:n], ALU.mult)
                        if T == TH + 1 and h == 0:
                            dbg_dump("or0", OR[0], OR[0].ap)

                def proj_gen(hg_):
                    KT = KTS[hg_ % 2]; QT = QTS[hg_ % 2]
                    KTB = KTBS[hg_ % 2]; VTB = VTBS[hg_ % 2]; QTB = QTBS[hg_ % 2]
                    for hh in range(HG):
                        h = hg_ * HG + hh
                        ps = proj(w_in[16 + h], HB, 0, NT)
                        conv4(ps, 16 + h, NT, 0, KT[hh], KT[hh].ap[:, 0:NT])
                        yield
                        ps = proj(w_in[32 + h], HB, 0, NT)
                        conv4(ps, 32 + h, NT, 0, VTB[hh], VTB[hh].ap[:, 0:NT])
                        yield
                        if is_main:
                            if T == TH:
                                memset(QTB[hh], QTB[hh].ap, 0.0)
                            ps = proj(w_in[h], HB, c0, n)
                            conv4(ps, h, n, c0, QT[hh], QT[hh].ap[:, c0:c0 + n])
                            yield
                    for hh in range(HG):
                        l2norm_to(KT[hh], KT[hh].ap[:, 0:NT], KTB[hh], KTB[hh].ap[:, 0:NT], NT, 1.0)
                        yield
                        if is_main:
                            l2norm_to(QT[hh], QT[hh].ap[:, c0:c0 + n], QTB[hh], QTB[hh].ap[:, c0:c0 + n], n, 128.0 ** -0.5)
                            yield

                NG = NH // HG
                interleave([proj_gen(0)])

                def chain2(a, b):
                    yield from a
                    yield from b

                if not is_main:
                    prevS = None
                    for hg in range(NG):
                        sts_ = [dict() for _ in range(NSUB)]
                        gens = [gdn_prep(hg, sub, False, sts_[sub]) for sub in range(NSUB)]
                        if prevS is not None:
                            gens.append(prevS)
                        if hg + 1 < NG:
                            gens.append(proj_gen(hg + 1))
                        interleave(gens)
                        prevS = chain2(gdn_state(hg, 0, False, sts_[0]), gdn_state(hg, 1, False, sts_[1]))
                    interleave([prevS])
                for hg in (range(NG) if is_main else []):
                    for sub in range(NSUB):
                        full = is_main and (T > TH or sub == NSUB - 1)
                        st_ = dict()
                        gens = [gdn_prep(hg, sub, full, st_)] + (prevB if prevB else [])
                        if sub == 0 and hg + 1 < NG:
                            gens.append(proj_gen(hg + 1))
                        interleave(gens)
                        if pend_gn is not None:
                            gnorm(pend_gn)
                            pend_gn = None
                        prevB = [gdn_state(hg, sub, full, st_)]
                        if sub == NSUB - 1 and is_main:
                            pend_gn = hg
                if prevB:
                    interleave(prevB)
                prevB = None
                if pend_gn is not None:
                    gnorm(pend_gn)
                    pend_gn = None
                if T == NTILES - 1:
                    dbg_dump("s0", S[0], S[0].ap)
                if T == 0:
                    ckpt(7)
                if T == TH - 1:
                    ckpt(8)
                if T == TH:
                    ckpt(9)
                if is_main:
                    if T == TH + 1:
                        for k in range(KC):
                            pass
                    pbsel[0] = PB_MAIN
                    mixer_rest(T, c0, n)
                    if T == TH:
                        ckpt(10)
                    if T == TH + 1:
                        dbg_dump("x1", XA[0], XA[0].ap)
                    xattn(T, c0, n)
                    if T == TH:
                        ckpt(11)
                    if T == TH + 1:
                        dbg_dump("x2", XA[0], XA[0].ap)
                    ffn(T, c0, n)
                    pbsel[0] = PB_STREAM
                    if T > TH:
                        o0 = (T - TH - 1) * NT
                        for k in range(KC):
                            dma("sp", None, outT[:, k, o0:o0 + NT], XA[k], XA[k].ap)
        try:
            ckpt(1)
            xattn_kv()
            ckpt(2)
            stream()
        except _Stop:
            pass
        emit_program(nc, P, es)
    return nc


def _tile_w(W):
    K, M = W.shape
    return np.ascontiguousarray(W.reshape(K // 128, 128, M // 128, 128).transpose(2, 1, 0, 3))


def _fm(v, nch):
    return np.ascontiguousarray(np.asarray(v, np.float32).reshape(nch, 128).T)


def make_consts():
    c = np.zeros((128, NCONST), np.float32)
    i = np.arange(128)
    blk = (i[:, None] // 64) == (i[None, :] // 64)
    for r in range(4):
        c[:, C_ID + 128 * r:C_ID + 128 * r + 128] = np.eye(128)
        c[:, C_UBD + 128 * r:C_UBD + 128 * r + 128] = blk & (i[:, None] <= i[None, :])
        c[:, C_MSTR + 128 * r:C_MSTR + 128 * r + 128] = blk & (i[:, None] > i[None, :])
    c[:, C_ONES:C_ONES + 128] = 1.0
    c[:, C_OBD:C_OBD + 128] = blk
    c[:, C_BSEL] = i < 64
    c[:, C_BSEL + 1] = i >= 64
    c[:, C_EPS] = EPS
    c[:, C_ONE] = 1.0
    return c


def prep_inputs(cfg, inp):
    SEQ, OWN = cfg.SEQ, cfg.OWN
    f = lambda a: np.asarray(a, np.float32)
    w_in_full = f(inp["w_in"])[0]
    cols = np.concatenate([np.arange(0, 8192), np.arange(8224, 13344)])
    shared = {
        "consts": make_consts(),
        "w_in": _tile_w(w_in_full[:, cols]),
        "wba": np.ascontiguousarray(w_in_full[:, 8192:8224].reshape(KC, 128, 32).transpose(1, 0, 2)),
        "poolw": np.ascontiguousarray(f(inp["pool_w"])[0].reshape(4, 2, 128, 256).transpose(2, 0, 1, 3)),
        "w_a": _tile_w(f(inp["w_branch_a"])[0]),
        "w_b": _tile_w(f(inp["w_branch_b"])[0]),
        "w_mix": _tile_w(f(inp["w_mix_out"])[0]),
        "w_xq": _tile_w(f(inp["w_xq"])[0]),
        "w_xkv": _tile_w(f(inp["w_xkv"])[0]),
        "w_xo": _tile_w(f(inp["w_xo"])[0]),
        "w_up": _tile_w(f(inp["w_up"])[0]),
        "w_down": _tile_w(f(inp["w_down"])[0]),
    }
    prm = np.zeros((128, NPRM), np.float32)
    for gi, nm in enumerate(["mix_pre_norm", "mix_post_norm", "xa_pre_norm", "mem_norm", "xa_post_norm",
                             "ffn_pre_norm", "ffn_post_norm"]):
        prm[:, P_GAIN + 16 * gi:P_GAIN + 16 * gi + 16] = _fm(f(inp[nm])[0], 16)
    cq = f(inp["conv_qkv"])[0]
    prm[:, P_CONVQ:P_CONVQ + 192] = cq.reshape(4, 48, 128).transpose(2, 1, 0).reshape(128, 192)
    fw = f(inp["ffn_conv_w"])[0]
    prm[:, P_FCW:P_FCW + 258] = fw.reshape(3, 86, 128).transpose(2, 1, 0).reshape(128, 258)
    prm[:, P_FCB:P_FCB + 86] = _fm(f(inp["ffn_conv_b"])[0], 86)
    prm[:, P_PSC:P_PSC + 8] = _fm(f(inp["pool_scale"])[0], 8)
    prm[:, P_GN] = f(inp["gdn_norm"])[0]
    prm[:, P_ALOG:P_ALOG + 16] = f(inp["a_log"])[0][None, :]
    prm[:, P_DTB:P_DTB + 16] = f(inp["dt_bias"])[0][None, :]
    x = f(inp["x"])
    mem = f(inp["mem"])
    in_maps = []
    for c in range(8):
        b, j = c // 4, c % 4
        end = OWN * (j + 1)
        start = end - SEQ
        xs = np.zeros((SEQ, D), np.float32)
        if start < 0:
            xs[-start:] = x[b, 0:end]
        else:
            xs[:] = x[b, start:end]
        xTc = np.ascontiguousarray(xs.T.reshape(KC, 128, SEQ).transpose(1, 0, 2))
        memTc = np.ascontiguousarray(mem[b].T.reshape(KC, 128, MEM).transpose(1, 0, 2))
        p = prm.copy()
        p[:, P_FLAG] = 0.0 if j == 0 else 1.0
        own_start = OWN * j
        t = own_start + np.arange(16)
        for g, win in enumerate((2, 4, 8, 16)):
            p[:, P_INVC + 16 * g:P_INVC + 16 * g + 16] = (1.0 / np.minimum(t + 1, win))[None, :]
        m = dict(shared)
        m.update({"xT": xTc, "memT": memTc, "prm": p})
        in_maps.append(m)
    return in_maps


_NC_CACHE = {}


def run(cfg, inp, dbg=()):
    key = (cfg.SEQ, cfg.NT, tuple(dbg))
    if key not in _NC_CACHE:
        _NC_CACHE[key] = build(cfg, dbg)
    nc = _NC_CACHE[key]
    in_maps = prep_inputs(cfg, inp)
    res = run_bass_kernel_spmd(nc, in_maps, core_ids=list(range(8)))
    B = 2
    out = np.zeros((B, cfg.SEQ, D), np.float32)
    for c in range(8):
        b, j = c // 4, c % 4
        o = res.results[c]["outT"]
        out[b, cfg.OWN * j:cfg.OWN * (j + 1), :] = o.transpose(2, 1, 0).reshape(cfg.OWN, D)
    return out, res


def kernel(**inputs):
    cfg = Cfg(SEQ=8192, NT=256)
    out, _ = run(cfg, inputs)
    return out
```

```python
import numpy as np
import concourse.bass as bass
import concourse.mybir as mybir
from concourse.bass_utils import run_bass_kernel_spmd
from contextlib import ExitStack

F32 = mybir.dt.float32
BF16 = mybir.dt.bfloat16
AF = mybir.ActivationFunctionType
ALU = mybir.AluOpType

ENGS = ("pe", "act", "dve", "pool", "sp")
NDSEM = 24


class Tl:
    __slots__ = ("ap", "name", "w", "r", "excl")

    def __init__(self, ap, name="", excl=False):
        self.ap = ap
        self.name = name
        self.w = {}
        self.r = {}
        self.excl = excl


def _add(depmap, dep):
    if dep[0] == "c":
        k = ("c", dep[1])
        if k not in depmap or depmap[k][2] < dep[2]:
            depmap[k] = dep
    else:
        depmap[dep] = dep


class Prog:
    def __init__(self, nc):
        self.nc = nc
        self.ops = {e: [] for e in ENGS}
        self.ndma = {e: 0 for e in ENGS}

    def op(self, eng, fn, reads=(), writes=(), dma=False):
        ops = self.ops[eng]
        idx = len(ops)
        deps = {}
        ex = [t for t in reads if t.excl]
        if ex:
            reads = [t for t in reads if not t.excl]
            writes = list(writes) + [t for t in ex if t not in writes]
        for t in reads:
            for d in t.w.values():
                _add(deps, d)
        for t in writes:
            for d in t.w.values():
                _add(deps, d)
            for d in t.r.values():
                _add(deps, d)
        if dma:
            n = self.ndma[eng]
            self.ndma[eng] += 1
            me = ("d", eng, n)
            if n >= NDSEM:
                _add(deps, ("d", eng, n - NDSEM))
        else:
            me = ("c", eng, idx)
        if eng == "pe":
            deps.pop(("c", "pe"), None)
        rec = dict(fn=fn, deps=list(deps.values()), me=me, needed=False)
        ops.append(rec)
        for t in writes:
            if t.r:
                t.w = {}
                t.r = {}
            _add(t.w, me)
        for t in reads:
            _add(t.r, me)
        return rec


def emit_program(nc, prog, es):
    for e in ENGS:
        for rec in prog.ops[e]:
            for d in rec["deps"]:
                if d[0] == "c":
                    prog.ops[d[1]][d[2]]["needed"] = True
    cum = {}
    for e in ENGS:
        c = 0
        arr = []
        for rec in prog.ops[e]:
            if rec["me"][0] == "c" and rec["needed"]:
                c += 1
            arr.append(c)
        cum[e] = arr
    csem = {e: es.enter_context(nc.semaphore("cs_" + e)) for e in ENGS}
    dsem = {}
    for e in ENGS:
        if prog.ndma[e] > 0:
            dsem[e] = [es.enter_context(nc.semaphore("ds_%s_%d" % (e, i)))
                       for i in range(min(NDSEM, prog.ndma[e]))]
    block = es.enter_context(nc.Block())

    def run(ename, engobj):
        waited = {}
        for rec in prog.ops[ename]:
            for d in rec["deps"]:
                if d[0] == "c":
                    sem = csem[d[1]]
                    val = cum[d[1]][d[2]]
                    key = ("c", d[1])
                else:
                    sem = dsem[d[1]][d[2] % NDSEM]
                    val = 16 * (d[2] // NDSEM + 1)
                    key = ("d", d[1], d[2] % NDSEM)
                if waited.get(key, 0) >= val:
                    continue
                waited[key] = val
                engobj.wait_ge(sem, val)
            ins = rec["fn"](engobj)
            me = rec["me"]
            if me[0] == "c":
                if rec["needed"]:
                    ins.then_inc(csem[ename], 1)
            else:
                ins.then_inc(dsem[ename][me[2] % NDSEM], 16)
        if ename in dsem:
            n = prog.ndma[ename]
            for i in range(min(NDSEM, n)):
                last = ((n - 1 - i) // NDSEM) * NDSEM + i
                engobj.wait_ge(dsem[ename][i], 16 * (last // NDSEM + 1))

    @block.tensor
    def _(eng):
        run("pe", eng)

    @block.scalar
    def _(eng):
        run("act", eng)

    @block.vector
    def _(eng):
        run("dve", eng)

    @block.gpsimd
    def _(eng):
        run("pool", eng)

    @block.sync
    def _(eng):
        run("sp", eng)


class FreeList:
    def __init__(self, tiles):
        self.free = list(tiles)
        self.total = len(tiles)
        self.low = len(tiles)

    def get(self):
        if not self.free:
            raise RuntimeError("freelist exhausted")
        t = self.free.pop(0)
        self.low = min(self.low, len(self.free))
        return t

    def put(self, t):
        self.free.append(t)


class Ring:
    def __init__(self, tiles):
        self.t = list(tiles)
        self.i = 0

    def next(self):
        t = self.t[self.i % len(self.t)]
        self.i += 1
        return t


class _Stop(Exception):
    pass


class Cfg:
    stop = 0

    def __init__(self, SEQ=8192, NT=256):
        self.SEQ = SEQ
        self.NT = NT
        self.OWN = SEQ // 4
        self.HALO = 32
        self.NTILES = SEQ // NT
        self.TH = (SEQ - self.OWN) // NT - 1
        self.NSUB = NT // 128


D = 2048
KC = 16
NH = 16
DFF = 5504
FC = 43
MEM = 256
HG = 4
EPS = 1e-6

P_GAIN = 0
P_CONVQ = 112
P_FCW = 304
P_FCB = 562
P_PSC = 648
P_GN = 656
P_ALOG = 657
P_DTB = 673
P_FLAG = 689
P_INVC = 690
NPRM = 754
C_ID = 0
C_UBD = 512
C_MSTR = 1024
C_ONES = 1536
C_OBD = 1664
C_BSEL = 1792
C_EPS = 1794
C_ONE = 1795
NCONST = 1800


def build(cfg, dbg=()):
    SEQ, NT, OWN, HALO, NTILES, TH, NSUB = cfg.SEQ, cfg.NT, cfg.OWN, cfg.HALO, cfg.NTILES, cfg.TH, cfg.NSUB
    nc = bass.Bass("TRN2", target_bir_lowering=False)
    nc.dge_precook = False

    def din(name, shape):
        return nc.dram_tensor(name, shape, F32, kind="ExternalInput").ap()

    xT = din("xT", [128, KC, SEQ])
    memT = din("memT", [128, KC, MEM])
    consts_d = din("consts", [128, NCONST])
    prm_d = din("prm", [128, NPRM])
    wba_d = din("wba", [128, KC, 32])
    poolw_d = din("poolw", [128, 4, 2, 256])
    class WRef:
        def __init__(self, src, mo, k0, k1):
            self.src, self.mo, self.k0, self.k1 = src, mo, k0, k1

    class WSrc:
        def __init__(self, name, nch, nk):
            self.name, self.nch, self.nk = name, nch, nk
            self.f32 = din(name, [nch, 128, nk, 128])
            self.b16 = nc.dram_tensor(name + "_b16", [nch, 128, nk, 128], BF16).ap()
            self.tl = [Tl(None) for _ in range(nch)]

        def __getitem__(self, idx):
            if isinstance(idx, tuple):
                mo, ks = idx[0], idx[2]
                return WRef(self, mo, ks.start, ks.stop)
            return WRef(self, idx, 0, self.nk)

    w_in = WSrc("w_in", 104, KC)
    w_a = WSrc("w_a", 16, KC)
    w_b = WSrc("w_b", 16, 8)
    w_mix = WSrc("w_mix", 16, KC)
    w_xq = WSrc("w_xq", 16, KC)
    w_xkv = WSrc("w_xkv", 32, KC)
    w_xo = WSrc("w_xo", 16, KC)
    w_up = WSrc("w_up", 86, KC)
    w_down = WSrc("w_down", 16, FC)
    outT = nc.dram_tensor("outT", [128, KC, OWN], F32, kind="ExternalOutput").ap()
    dbg_out = {}
    for name, shape in dbg:
        dbg_out[name] = nc.dram_tensor("dbg_" + name, list(shape), F32, kind="ExternalOutput").ap()

    es = ExitStack()
    with es:
        def sb(name, shape, dt=F32):
            return es.enter_context(nc.sbuf_tensor(name, list(shape), dt))

        def pst(name, shape, dt=F32):
            return es.enter_context(nc.psum_tensor(name, list(shape), dt))

        P = Prog(nc)

        def ckpt(i):
            if cfg.stop == i:
                raise _Stop()

        xa_t = sb("xa", [128, KC, NT]); XA = [Tl(xa_t[:, k, :]) for k in range(KC)]
        hb_t = sb("hb", [128, KC, NT], BF16); HB = [Tl(hb_t[:, k, :]) for k in range(KC)]
        cb_t = sb("cb", [128, KC, NT], BF16); CB = [Tl(cb_t[:, k, :]) for k in range(KC)]
        db_t = sb("db", [128, KC, NT]); DB = [Tl(db_t[:, k, :]) for k in range(KC)]
        act_t = sb("actb", [128, FC, NT], BF16); ACTB = [Tl(act_t[:, k, :]) for k in range(FC)]
        s_t = sb("state", [128, NH, 128]); S = [Tl(s_t[:, h, :]) for h in range(NH)]
        NW = 3
        w_ts = [sb("wt%d" % i, [128, 22, 128], BF16) for i in range(NW)]
        WR = Ring([Tl(t) for t in w_ts])
        kx_t = sb("kx", [128, KC, MEM], BF16); KX = Tl(kx_t)
        vx_t = sb("vx", [128, 2, D], BF16); VX = Tl(vx_t)
        cst = sb("cst", [128, NCONST]); CST = Tl(cst)
        prm = sb("prm_s", [128, NPRM]); PRM = Tl(prm)
        onesb_t = sb("onesb", [128, 128], BF16); ONESB = Tl(onesb_t)
        wba_t = sb("wba_s", [128, KC, 32], BF16); WBA = Tl(wba_t)
        poolw_t = sb("poolw_s", [128, 4, 2, 256], BF16); POOLW = Tl(poolw_t)
        nga_t = sb("nga", [128, NH]); NGA = Tl(nga_t)
        qtail_t = sb("qtail", [128, 48, 3]); QTAIL = [Tl(qtail_t[:, c, :]) for c in range(48)]
        ptail_t = sb("pbuf", [128, 8, 15 + NT]); PBUF = [Tl(ptail_t[:, c, :]) for c in range(8)]
        ftail_t = sb("ftail", [128, 86, 2]); FTAIL = [Tl(ftail_t[:, c, :]) for c in range(86)]
        bg_t = sb("bg", [128, 3, NSUB, NH]); BG = Tl(bg_t)
        kt_t = sb("kt", [128, HG, NT]); KT0 = [Tl(kt_t[:, i, :]) for i in range(HG)]
        qt_t = sb("qt", [128, HG, NT]); QT0 = [Tl(qt_t[:, i, :]) for i in range(HG)]
        ktb_t = sb("ktb", [128, HG, NT], BF16); KTB0 = [Tl(ktb_t[:, i, :]) for i in range(HG)]
        vtb_t = sb("vtb", [128, HG, NT], BF16); VTB0 = [Tl(vtb_t[:, i, :]) for i in range(HG)]
        qtb_t = sb("qtb", [128, HG, NT], BF16); QTB0 = [Tl(qtb_t[:, i, :]) for i in range(HG)]
        act32 = act_t.bitcast(F32)
        cpr = NT // 128

        def alias32(j):
            return Tl(act32[:, j * cpr:(j + 1) * cpr, :].rearrange("p a b -> p (a b)"))
        KT1 = [alias32(i) for i in range(HG)]
        QT1 = [alias32(HG + i) for i in range(HG)]
        r0 = 2 * HG * cpr
        KTB1 = [Tl(act_t[:, r0 + i, :]) for i in range(HG)]
        VTB1 = [Tl(act_t[:, r0 + HG + i, :]) for i in range(HG)]
        QTB1 = [Tl(act_t[:, r0 + 2 * HG + i, :]) for i in range(HG)]
        r1 = r0 + 3 * HG
        assert r1 + 8 <= FC
        KTS = [KT0, KT1]; QTS = [QT0, QT1]
        KTBS = [KTB0, KTB1]; VTBS = [VTB0, VTB1]; QTBS = [QTB0, QTB1]
        or_t = sb("or", [128, HG, NT]); OR = [Tl(or_t[:, i, :]) for i in range(HG)]
        NTMP = 7
        tmp_t = sb("gtmp", [128, NTMP, 512])
        TMP = FreeList([Tl(tmp_t[:, i, :]) for i in range(NTMP)])
        NTMPB = 17
        tmpb_t = sb("gtmpb", [128, NTMPB, 512], BF16)
        TMPB = FreeList([Tl(tmpb_t[:, i, :]) for i in range(NTMPB)])
        identb_t = sb("identb", [128, 128], BF16); IDENTB = Tl(identb_t)
        db16 = db_t.bitcast(BF16)
        for j_ in range(8):
            TMPB.put(Tl(db16[:, 2 * j_:2 * j_ + 2, :].rearrange("p a b -> p (a b)")[:, 0:512]))
        gs_t = sb("gs", [128, NSUB, 8, NH]); GS = Tl(gs_t)
        cq_t = sb("cq", [128, 2, 3 + NT]); CQ = Ring([Tl(cq_t[:, i, :]) for i in range(2)])
        acc_t = sb("acc", [128, 3, NT]); ACC = Ring([Tl(acc_t[:, i, :]) for i in range(3)])
        sqb_t = sb("sqb", [128, 3, NT], BF16); SQB = Ring([Tl(sqb_t[:, i, :]) for i in range(3)])
        rs_t = sb("rs", [128, 2, NT]); RS = Ring([Tl(rs_t[:, i, :]) for i in range(2)])
        f32s_t = sb("f32s", [128, 3, NT]); FS = Ring([Tl(f32s_t[:, i, :]) for i in range(3)])
        et_t = sb("et", [128, 4, NT], BF16); ET = Ring([Tl(et_t[:, i, :]) for i in range(4)])
        YPI = [Tl(act_t[:, r1 + i, :]) for i in range(8)]
        ypo_t = sb("ypo", [128, 8, NT], BF16); YPO = [Tl(ypo_t[:, i, :]) for i in range(8)]
        pw_t = sb("pw", [128, 2, 2, 15 + NT]); PW = [Tl(pw_t[:, i, :, :]) for i in range(2)]

        pbig = [pst("pbig%d" % i, [128, 512]) for i in range(2)]
        PB = Ring([Tl(pbig[i][:, :], excl=True) for i in range(2)])
        pss = pst("pss", [128, 512])
        PSS = Ring([Tl(pss[:, :], excl=True)])
        psm = [pst("psm%d" % i, [128, 512]) for i in range(5)]
        PSM = FreeList([Tl(psm[i][:, :], excl=True) for i in range(5)])
        PB_STREAM = PB
        PB_MAIN = Ring(PB.t + PSM.free[0:4])
        pbsel = [PB_STREAM]

        def C(c0, n=128):
            return cst[:, c0:c0 + n]

        def pc(c):
            return prm[:, c:c + 1]

        def mm(out_tl, out_ap, a_tl, a_ap, b_tl, b_ap, start=True, stop=True):
            P.op("pe", lambda e: e.matmul(out_ap, a_ap, b_ap, start=start, stop=stop),
                 reads=[a_tl, b_tl], writes=[out_tl])

        def act(out_tl, out_ap, in_tl, in_ap, func, bias=None, scale=None, rd=()):
            kw = {}
            if bias is not None:
                kw["bias"] = bias
            if scale is not None:
                kw["scale"] = scale
            P.op("act", lambda e: e.activation(out=out_ap, in_=in_ap, func=func, **kw),
                 reads=[in_tl] + list(rd), writes=[out_tl])

        def ts(out_tl, out_ap, in_tl, in_ap, s1, s2, op0, op1=None, rd=(), eng="dve"):
            if op1 is None:
                P.op(eng, lambda e: e.tensor_scalar(out_ap, in_ap, s1, None, op0),
                     reads=[in_tl] + list(rd), writes=[out_tl])
            else:
                P.op(eng, lambda e: e.tensor_scalar(out_ap, in_ap, s1, s2, op0, op1),
                     reads=[in_tl] + list(rd), writes=[out_tl])

        def stt(out_tl, out_ap, in0_tl, in0_ap, scalar, in1_tl, in1_ap, op0, op1, rd=()):
            P.op("dve", lambda e: e.scalar_tensor_tensor(out_ap, in0_ap, scalar, in1_ap, op0, op1),
                 reads=[in0_tl, in1_tl] + list(rd), writes=[out_tl])

        def tt(out_tl, out_ap, a_tl, a_ap, b_tl, b_ap, op, eng="dve"):
            P.op(eng, lambda e: e.tensor_tensor(out_ap, a_ap, b_ap, op),
                 reads=[a_tl, b_tl], writes=[out_tl])

        def cp(out_tl, out_ap, in_tl, in_ap, eng="dve"):
            P.op(eng, lambda e: e.tensor_copy(out_ap, in_ap), reads=[in_tl], writes=[out_tl])

        def recip(out_tl, out_ap, in_tl, in_ap):
            P.op("dve", lambda e: e.reciprocal(out_ap, in_ap), reads=[in_tl], writes=[out_tl])

        def memset(tl, ap, val, eng="dve"):
            P.op(eng, lambda e: e.memset(ap, val), writes=[tl])

        def dma(eng, out_tl, out_ap, in_tl, in_ap):
            P.op(eng, lambda e: e.dma_start(out=out_ap, in_=in_ap),
                 reads=[in_tl] if in_tl is not None else [],
                 writes=[out_tl] if out_tl is not None else [], dma=True)

        def dbg_dump(name, tl, ap):
            if name in dbg_out:
                dma("sp", None, dbg_out[name], tl, ap)

        def load_w(ref, nk):
            w = WR.next()
            assert ref.k1 - ref.k0 == nk
            dma("sp", w, w.ap[:, 0:nk, :], ref.src.tl[ref.mo], ref.src.b16[ref.mo, :, ref.k0:ref.k1, :])
            return w

        def cast_weights(src, chunks, after=None):
            for mo in chunks:
                dma("pool", src.tl[mo], src.b16[mo], after, src.f32[mo])

        late_casts = []

        def proj(wd, in_list, c0, n, nk=KC, k0=0, ps=None, first=True, last=True):
            w = load_w(wd, nk)
            if ps is None:
                ps = pbsel[0].next()
            for k in range(nk):
                mm(ps, ps.ap[:, 0:n], w, w.ap[:, k, :], in_list[k0 + k], in_list[k0 + k].ap[:, c0:c0 + n],
                   start=(first and k == 0), stop=(last and k == nk - 1))
            return ps

        def rstd_from(ss, n, scale, out=None):
            r = RS.next() if out is None else out
            act(r, r.ap[:, 0:n], ss, ss.ap[:, 0:n], AF.Sqrt, bias=C(C_EPS, 1), scale=scale, rd=[CST])
            recip(r, r.ap[:, 0:n], r, r.ap[:, 0:n])
            return r

        def rmsnorm_to_bf(src, c0, n, gi, dst):
            ss = PSS.next()
            for k in range(KC):
                q = SQB.next()
                act(q, q.ap[:, 0:n], src[k], src[k].ap[:, c0:c0 + n], AF.Square)
                mm(ss, ss.ap[:, 0:n], ONESB, onesb_t[:, :], q, q.ap[:, 0:n], start=(k == 0), stop=(k == KC - 1))
            r = rstd_from(ss, n, 1.0 / D)
            for k in range(KC):
                stt(dst[k], dst[k].ap[:, c0:c0 + n], src[k], src[k].ap[:, c0:c0 + n], pc(P_GAIN + 16 * gi + k),
                    r, r.ap[:, 0:n], ALU.mult, ALU.mult, rd=[PRM])

        def postnorm_residual(ss, c0, n, gi):
            r = rstd_from(ss, n, 1.0 / D)
            for k in range(KC):
                f = FS.next()
                tt(f, f.ap[:, 0:n], DB[k], DB[k].ap[:, c0:c0 + n], r, r.ap[:, 0:n], ALU.mult)
                stt(XA[k], XA[k].ap[:, c0:c0 + n], f, f.ap[:, 0:n], pc(P_GAIN + 16 * gi + k),
                    XA[k], XA[k].ap[:, c0:c0 + n], ALU.mult, ALU.add, rd=[PRM])

        def y_chunk_out(ps, mo, c0, n, ss):
            act(DB[mo], DB[mo].ap[:, c0:c0 + n], ps, ps.ap[:, 0:n], AF.Copy)
            q = SQB.next()
            act(q, q.ap[:, 0:n], ps, ps.ap[:, 0:n], AF.Square)
            mm(ss, ss.ap[:, 0:n], ONESB, onesb_t[:, :], q, q.ap[:, 0:n], start=(mo == 0), stop=(mo == KC - 1))

        def body():
            pass

        dma("sp", CST, cst[:, :], None, consts_d)
        dma("sp", PRM, prm[:, :], None, prm_d)
        dma("pool", WBA, wba_t[:, :, :], None, wba_d)
        dma("pool", POOLW, poolw_t[:, :, :, :], None, poolw_d)
        cast_weights(w_xkv, range(32))
        cast_weights(w_in, range(16, 48))
        for mo_ in list(range(0, 16)) + list(range(48, 104)):
            late_casts.append((w_in, mo_))
        for src_ in (w_a, w_b, w_mix, w_xq, w_xo, w_up, w_down):
            for mo_ in range(src_.nch):
                late_casts.append((src_, mo_))
        cp(ONESB, onesb_t[:, :], CST, C(C_ONES))
        cp(IDENTB, identb_t[:, :], CST, C(C_ID))
        for h in range(NH):
            memset(S[h], S[h].ap, 0.0)
        for c in range(48):
            memset(QTAIL[c], QTAIL[c].ap, 0.0)
        for c in range(8):
            memset(PBUF[c], PBUF[c].ap, 0.0)
        for c in range(86):
            memset(FTAIL[c], FTAIL[c].ap, 0.0)
        act(NGA, nga_t[:, :], PRM, prm[:, P_ALOG:P_ALOG + 16], AF.Exp)
        ts(NGA, nga_t[:, :], NGA, nga_t[:, :], -1.0, None, ALU.mult)

        def xattn_kv():
            assert NT == MEM
            MXl = DB
            MTl = HB
            for k in range(KC):
                dma("sp", MXl[k], MXl[k].ap, None, memT[:, k, :])
            ckpt(21)
            rmsnorm_to_bf(MXl, 0, MEM, 3, MTl)
            ckpt(22)
            for mo in range(KC):
                ps = proj(w_xkv[mo], MTl, 0, MEM)
                act(KX, kx_t[:, mo, :], ps, ps.ap[:, 0:MEM], AF.Copy)
                ckpt(100 + mo)
            ckpt(24)
            for mo in range(KC):
                ps = proj(w_xkv[KC + mo], MTl, 0, MEM)
                f = FS.next()
                act(f, f.ap[:, 0:MEM], ps, ps.ap[:, 0:MEM], AF.Copy)
                for mt in range(2):
                    p2 = PSM.get()
                    mm(p2, p2.ap[:, 0:128], f, f.ap[:, mt * 128:(mt + 1) * 128], CST, C(C_ID))
                    cp(VX, vx_t[:, mt, mo * 128:(mo + 1) * 128], p2, p2.ap[:, 0:128])
                    PSM.put(p2)

        def conv4(ps, ch, n, c0, out_tl, out_ap_silu):
            cq = CQ.next()
            a = ACC.next()
            act(cq, cq.ap[:, 3:3 + n], ps, ps.ap[:, 0:n], AF.Copy)
            cp(cq, cq.ap[:, 0:3], QTAIL[ch], QTAIL[ch].ap)
            act(a, a.ap[:, 0:n], ps, ps.ap[:, 0:n], AF.Identity, scale=pc(P_CONVQ + 4 * ch + 3), rd=[PRM])
            for j in range(3):
                stt(a, a.ap[:, 0:n], cq, cq.ap[:, j:j + n], pc(P_CONVQ + 4 * ch + j), a, a.ap[:, 0:n],
                    ALU.mult, ALU.add, rd=[PRM])
            cp(QTAIL[ch], QTAIL[ch].ap, cq, cq.ap[:, n:n + 3])
            act(out_tl, out_ap_silu, a, a.ap[:, 0:n], AF.Silu)

        def l2norm_to(tl, ap, otl, oap, n, mul):
            q = SQB.next()
            tt(q, q.ap[:, 0:n], tl, ap, tl, ap, ALU.mult)
            ss = PSS.next()
            mm(ss, ss.ap[:, 0:n], ONESB, onesb_t[:, :], q, q.ap[:, 0:n])
            r = rstd_from(ss, n, 1.0)
            if mul == 1.0:
                tt(otl, oap, tl, ap, r, r.ap[:, 0:n], ALU.mult)
            else:
                stt(otl, oap, tl, ap, mul, r, r.ap[:, 0:n], ALU.mult, ALU.mult)

        def Q4(t, i):
            return t.ap[:, i * 128:(i + 1) * 128]

        def gs_stage(sub):
            g = bg_t[:, 2, sub, :]
            f = FS.next()
            for c in range(2):
                ts(f, f.ap[:, 16 * c:16 * c + 16], BG, g, C(C_BSEL + c, 1), None, ALU.mult, rd=[CST])
            bk = PSM.get()
            mm(bk, bk.ap[:, 0:16], CST, C(C_UBD), BG, g)
            mm(bk, bk.ap[:, 16:32], CST, C(C_OBD), BG, g)
            mm(bk, bk.ap[:, 32:64], CST, C(C_ONES), f, f.ap[:, 0:32])
            for r_ in range(4):
                cp(GS, gs_t[:, sub, r_, :], bk, bk.ap[:, 16 * r_:16 * r_ + 16])
            PSM.put(bk)
            f2 = FS.next()
            act(f2, f2.ap[:, 0:16], GS, gs_t[:, sub, 0, :], AF.Exp)
            tt(GS, gs_t[:, sub, 4, :], f2, f2.ap[:, 0:16], BG, bg_t[:, 0, sub, :], ALU.mult)
            tt(f2, f2.ap[:, 16:32], GS, gs_t[:, sub, 1, :], GS, gs_t[:, sub, 0, :], ALU.subtract)
            act(GS, gs_t[:, sub, 5, :], f2, f2.ap[:, 16:32], AF.Exp)
            for r_ in range(2):
                act(GS, gs_t[:, sub, 6 + r_, :], GS, gs_t[:, sub, 2 + r_, :], AF.Exp)

        def gdn_prep(hg, sub, full, st):
            cs = slice(sub * 128, sub * 128 + 128)
            heads = [hg * HG + i for i in range(HG)]
            KT = KTBS[hg % 2]; VT = VTBS[hg % 2]; QT = QTBS[hg % 2]

            def gcol(row, h):
                return gs_t[:, sub, row, h:h + 1]
            gd = TMP.get()
            for i, h in enumerate(heads):
                ts(gd, Q4(gd, i), CST, C(C_ID), gcol(0, h), -1.0, ALU.mult, ALU.mult, rd=[GS])
            bk = PSM.get()
            for i in range(HG):
                mm(bk, Q4(bk, i), CST, C(C_ONES), gd, Q4(gd, i))
            yield
            Dm = TMP.get()
            for i, h in enumerate(heads):
                ts(Dm, Q4(Dm, i), bk, Q4(bk, i), gcol(0, h), 0.0, ALU.add, ALU.min, rd=[GS])
            if full:
                DT = TMP.get(); eG = TMP.get()
                for i, h in enumerate(heads):
                    ts(DT, Q4(DT, i), bk, Q4(bk, i), -1.0, gcol(0, h), ALU.mult, ALU.subtract, rd=[GS])
                ts(DT, DT.ap, DT, DT.ap, 0.0, None, ALU.min)
                act(eG, eG.ap, bk, bk.ap, AF.Exp, scale=-1.0)
            PSM.put(bk)
            TMP.put(gd)
            yield
            act(Dm, Dm.ap, Dm, Dm.ap, AF.Exp)
            if full:
                act(DT, DT.ap, DT, DT.ap, AF.Exp)
            yield
            tt(Dm, Dm.ap, Dm, Dm.ap, CST, C(C_MSTR, 512), ALU.mult)
            if full:
                tt(DT, DT.ap, DT, DT.ap, CST, C(C_UBD, 512), ALU.mult)
            bk = PSM.get()
            for i in range(HG):
                mm(bk, Q4(bk, i), KT[i], KT[i].ap[:, cs], KT[i], KT[i].ap[:, cs])
            yield
            N0 = TMPB.get()
            for i, h in enumerate(heads):
                stt(N0, Q4(N0, i), bk, Q4(bk, i), bg_t[:, 1, sub, h:h + 1], Dm, Q4(Dm, i), ALU.mult, ALU.mult, rd=[BG])
            PSM.put(bk)
            TMP.put(Dm)
            yield
            bk = PSM.get()
            for i in range(HG):
                mm(bk, Q4(bk, i), N0, Q4(N0, i), IDENTB, identb_t[:, :])
            if full:
                bk2 = PSM.get()
                for i in range(HG):
                    mm(bk2, Q4(bk2, i), KT[i], KT[i].ap[:, cs], QT[i], QT[i].ap[:, cs])
            yield
            N0T = TMPB.get(); TT = TMPB.get()
            act(N0T, N0T.ap, bk, bk.ap, AF.Copy)
            tt(TT, TT.ap, bk, bk.ap, CST, C(C_ID, 512), ALU.add)
            PSM.put(bk)
            if full:
                AT = TMPB.get(); Qd = TMPB.get()
                tt(AT, AT.ap, bk2, bk2.ap, DT, DT.ap, ALU.mult)
                PSM.put(bk2)
                for i in range(HG):
                    tt(Qd, Q4(Qd, i), QT[i], QT[i].ap[:, cs], eG, Q4(eG, i), ALU.mult)
                TMP.put(DT); TMP.put(eG)
                st["AT"] = AT; st["Qd"] = Qd
            yield
            Pk, PTk = N0, N0T
            for k in range(5):
                b1 = PSM.get()
                for i in range(HG):
                    mm(b1, Q4(b1, i), PTk, Q4(PTk, i), Pk, Q4(Pk, i))
                if k < 4:
                    b2 = PSM.get()
                    for i in range(HG):
                        mm(b2, Q4(b2, i), Pk, Q4(Pk, i), PTk, Q4(PTk, i))
                yield
                Pn = TMPB.get()
                act(Pn, Pn.ap, b1, b1.ap, AF.Copy)
                PSM.put(b1)
                if k < 4:
                    PTn = TMPB.get()
                    cp(PTn, PTn.ap, b2, b2.ap)
                    PSM.put(b2)
                yield
                b3 = PSM.get()
                for i in range(HG):
                    mm(b3, Q4(b3, i), Pn, Q4(Pn, i), TT, Q4(TT, i))
                yield
                tt(TT, TT.ap, TT, TT.ap, b3, b3.ap, ALU.add)
                PSM.put(b3)
                TMPB.put(Pk); TMPB.put(PTk)
                Pk = Pn
                PTk = PTn if k < 4 else None
                yield
            TMPB.put(Pk)
            bK = PSM.get(); bV = PSM.get()
            for i in range(HG):
                mm(bK, Q4(bK, i), KT[i], KT[i].ap[:, cs], IDENTB, identb_t[:, :])
                mm(bV, Q4(bV, i), VT[i], VT[i].ap[:, cs], IDENTB, identb_t[:, :])
            yield
            Rv = TMPB.get(); Rk = TMPB.get(); Kd = TMPB.get()
            for i, h in enumerate(heads):
                ts(Rv, Q4(Rv, i), bV, Q4(bV, i), bg_t[:, 0, sub, h:h + 1], None, ALU.mult, rd=[BG])
                ts(Rk, Q4(Rk, i), bK, Q4(bK, i), gcol(4, h), None, ALU.mult, rd=[GS])
                act(Kd, Q4(Kd, i), bK, Q4(bK, i), AF.Identity, scale=gcol(5, h), rd=[GS])
            PSM.put(bK); PSM.put(bV)
            yield
            bU = PSM.get(); bW = PSM.get()
            for i in range(HG):
                mm(bU, Q4(bU, i), TT, Q4(TT, i), Rv, Q4(Rv, i))
                mm(bW, Q4(bW, i), Rk, Q4(Rk, i), TT, Q4(TT, i))
            yield
            TMPB.put(Rv); TMPB.put(Rk); TMPB.put(TT)
            Uv = TMP.get(); WkT = TMPB.get()
            act(Uv, Uv.ap, bU, bU.ap, AF.Copy)
            cp(WkT, WkT.ap, bW, bW.ap)
            PSM.put(bU); PSM.put(bW)
            st["Uv"] = Uv; st["WkT"] = WkT; st["Kd"] = Kd
            yield

        def gdn_state(hg, sub, full, st):
            heads = [hg * HG + i for i in range(HG)]
            Uv = st["Uv"]; WkT = st["WkT"]; Kd = st["Kd"]
            u = TMPB.get(); Sb = TMPB.get()
            Sg = [S[h] for h in heads]
            sg_ap = s_t[:, hg * HG:(hg + 1) * HG, :].rearrange("p a b -> p (a b)")
            P.op("act", lambda e: e.activation(out=Sb.ap, in_=sg_ap, func=AF.Copy), reads=Sg, writes=[Sb])
            yield
            for c in range(2):
                r = slice(64 * c, 64 * c + 64)
                bk = PSM.get()
                for i, h in enumerate(heads):
                    if c == 0:
                        mm(bk, bk.ap[0:64, i * 128:(i + 1) * 128], WkT, WkT.ap[:, i * 128:i * 128 + 64], Sb, Q4(Sb, i))
                    else:
                        mm(bk, Q4(bk, i), WkT, Q4(WkT, i), Sb, Q4(Sb, i))
                yield
                tt(u, u.ap[r, :], Uv, Uv.ap[r, :], bk, bk.ap[r, :], ALU.subtract)
                PSM.put(bk)
                yield
                bk = PSM.get()
                for i, h in enumerate(heads):
                    mm(bk, Q4(bk, i), Kd, Kd.ap[r, i * 128:(i + 1) * 128], u, u.ap[r, i * 128:(i + 1) * 128])
                if full:
                    bo = PSM.get()
                    Qd = st["Qd"]; AT = st["AT"]
                    for i, h in enumerate(heads):
                        oq = bo.ap[:, i * 128:i * 128 + 64]
                        mm(bo, oq, Sb, Q4(Sb, i), Qd, Qd.ap[:, i * 128 + 64 * c:i * 128 + 64 * c + 64], start=True, stop=False)
                        mm(bo, oq, u, u.ap[r, i * 128:(i + 1) * 128], AT, AT.ap[r, i * 128 + 64 * c:i * 128 + 64 * c + 64],
                           start=False, stop=True)
                yield
                for i, h in enumerate(heads):
                    stt(S[h], S[h].ap, S[h], S[h].ap, gs_t[:, sub, 6 + c, h:h + 1], bk, Q4(bk, i), ALU.mult, ALU.add, rd=[GS])
                PSM.put(bk)
                if full:
                    o0 = sub * 128 + 64 * c
                    for i in range(HG):
                        act(OR[i], OR[i].ap[:, o0:o0 + 64], bo, bo.ap[:, i * 128:i * 128 + 64], AF.Copy)
                    PSM.put(bo)
                yield
                if c == 0:
                    P.op("act", lambda e: e.activation(out=Sb.ap, in_=sg_ap, func=AF.Copy), reads=Sg, writes=[Sb])
                    yield
            TMPB.put(u); TMPB.put(Sb); TMP.put(Uv); TMPB.put(WkT); TMPB.put(Kd)
            if full:
                TMPB.put(st["AT"]); TMPB.put(st["Qd"])

        def interleave(gens):
            gens = list(gens)
            while gens:
                nxt = []
                for g in gens:
                    try:
                        next(g)
                        nxt.append(g)
                    except StopIteration:
                        pass
                gens = nxt

        def mixer_rest(T, c0, n):
            for mo in range(KC):
                ps1 = proj(w_in[72 + mo], HB, c0, n)
                sg = FS.next()
                act(sg, sg.ap[:, 0:n], ps1, ps1.ap[:, 0:n], AF.Sigmoid)
                ps2 = proj(w_a[mo], CB, c0, n)
                tt(DB[mo], DB[mo].ap[:, c0:c0 + n], ps2, ps2.ap[:, 0:n], sg, sg.ap[:, 0:n], ALU.mult)
            for pcix in range(8):
                ps = proj(w_in[64 + pcix], HB, c0, n)
                act(PBUF[pcix], PBUF[pcix].ap[:, 15:15 + n], ps, ps.ap[:, 0:n], AF.Copy)
            L = 15 + n
            for g in range(4):
                win = 2 << g
                src_tl = None
                src = ptail_t[:, 2 * g:2 * g + 2, :]
                srcs = [PBUF[2 * g], PBUF[2 * g + 1]]
                sh = 1
                wi = 0
                cur = src
                cur_tls = srcs
                for lvl in range(g + 1):
                    dst = PW[wi % 2]
                    wi += 1
                    P.op("dve", (lambda d=dst.ap, s=cur, sh=sh: lambda e: e.tensor_tensor(d[:, :, sh:L], s[:, :, sh:L], s[:, :, 0:L - sh], ALU.add))(),
                         reads=list(cur_tls), writes=[dst])
                    cur = dst.ap
                    cur_tls = [dst]
                    sh *= 2
                for i in range(2):
                    yp = YPI[2 * g + i]
                    stt(yp, yp.ap[:, 0:n], cur_tls[0], cur[:, i, 15:15 + n], 1.0 / win,
                        PBUF[2 * g + i], PBUF[2 * g + i].ap[:, 15:15 + n], ALU.mult, ALU.subtract)
                    if T == TH + 1:
                        f = FS.next()
                        tt(f, f.ap[:, 0:16], cur_tls[0], cur[:, i, 15:31], PRM, prm[:, P_INVC + 16 * g:P_INVC + 16 * g + 16], ALU.mult)
                        tt(yp, yp.ap[:, 0:16], f, f.ap[:, 0:16], PBUF[2 * g + i], PBUF[2 * g + i].ap[:, 15:31], ALU.subtract)
            for pcix in range(8):
                f = FS.next()
                cp(f, f.ap[:, 0:15], PBUF[pcix], PBUF[pcix].ap[:, n:n + 15])
                cp(PBUF[pcix], PBUF[pcix].ap[:, 0:15], f, f.ap[:, 0:15])
            for g in range(4):
                for mo2 in range(2):
                    ps = pbsel[0].next()
                    for ki in range(2):
                        mm(ps, ps.ap[:, 0:n], POOLW, poolw_t[:, g, ki, mo2 * 128:(mo2 + 1) * 128],
                           YPI[2 * g + ki], YPI[2 * g + ki].ap[:, 0:n], start=(ki == 0), stop=(ki == 1))
                    yo = YPO[2 * g + mo2]
                    act(yo, yo.ap[:, 0:n], ps, ps.ap[:, 0:n], AF.Identity, scale=pc(P_PSC + 2 * g + mo2), rd=[PRM])
            YPOc = [Tl(None)] * 0
            for mo in range(KC):
                ps1 = proj(w_in[88 + mo], HB, c0, n)
                sg = FS.next()
                act(sg, sg.ap[:, 0:n], ps1, ps1.ap[:, 0:n], AF.Sigmoid)
                w = load_w(w_b[mo], 8)
                ps2 = pbsel[0].next()
                for k in range(8):
                    mm(ps2, ps2.ap[:, 0:n], w, w.ap[:, k, :], YPO[k], YPO[k].ap[:, 0:n], start=(k == 0), stop=(k == 7))
                f = FS.next()
                tt(f, f.ap[:, 0:n], ps2, ps2.ap[:, 0:n], sg, sg.ap[:, 0:n], ALU.mult)
                tt(CB[mo], CB[mo].ap[:, c0:c0 + n], f, f.ap[:, 0:n], DB[mo], DB[mo].ap[:, c0:c0 + n], ALU.add)
            ss = PSS.next()
            for mo in range(KC):
                ps = proj(w_mix[mo], CB, c0, n)
                y_chunk_out(ps, mo, c0, n, ss)
            postnorm_residual(ss, c0, n, 1)

        def xattn(T, c0, n):
            rmsnorm_to_bf(XA, c0, n, 2, HB)
            for mo in range(KC):
                ps = proj(w_xq[mo], HB, c0, n)
                act(CB[mo], CB[mo].ap[:, c0:c0 + n], ps, ps.ap[:, 0:n], AF.Copy)
            scl = 512.0 ** -0.5
            for hx in range(4):
                ets = []
                for mt in range(2):
                    ps = pbsel[0].next()
                    for c in range(4):
                        kc = 4 * hx + c
                        mm(ps, ps.ap[:, 0:n], KX, kx_t[:, kc, mt * 128:(mt + 1) * 128], CB[kc], CB[kc].ap[:, c0:c0 + n],
                           start=(c == 0), stop=(c == 3))
                    e_ = ET.next()
                    act(e_, e_.ap[:, 0:n], ps, ps.ap[:, 0:n], AF.Exp, scale=scl)
                    ets.append(e_)
                den = PSS.next()
                for mt in range(2):
                    mm(den, den.ap[:, 0:n], ONESB, onesb_t[:, :], ets[mt], ets[mt].ap[:, 0:n], start=(mt == 0), stop=(mt == 1))
                rd_ = RS.next()
                recip(rd_, rd_.ap[:, 0:n], den, den.ap[:, 0:n])
                for c in range(4):
                    kc = 4 * hx + c
                    ps = pbsel[0].next()
                    for mt in range(2):
                        mm(ps, ps.ap[:, 0:n], VX, vx_t[:, mt, kc * 128:(kc + 1) * 128], ets[mt], ets[mt].ap[:, 0:n],
                           start=(mt == 0), stop=(mt == 1))
                    tt(HB[kc], HB[kc].ap[:, c0:c0 + n], ps, ps.ap[:, 0:n], rd_, rd_.ap[:, 0:n], ALU.mult)
            ss = PSS.next()
            for mo in range(KC):
                ps = proj(w_xo[mo], HB, c0, n)
                y_chunk_out(ps, mo, c0, n, ss)
            postnorm_residual(ss, c0, n, 4)

        def ffn(T, c0, n):
            rmsnorm_to_bf(XA, c0, n, 5, HB)

            def conv3(ps, ch):
                cq = CQ.next()
                a = ACC.next()
                act(cq, cq.ap[:, 2:2 + n], ps, ps.ap[:, 0:n], AF.Copy)
                cp(cq, cq.ap[:, 0:2], FTAIL[ch], FTAIL[ch].ap)
                act(a, a.ap[:, 0:n], ps, ps.ap[:, 0:n], AF.Identity, bias=pc(P_FCB + ch), scale=pc(P_FCW + 3 * ch + 2), rd=[PRM])
                for j in range(2):
                    stt(a, a.ap[:, 0:n], cq, cq.ap[:, j:j + n], pc(P_FCW + 3 * ch + j), a, a.ap[:, 0:n],
                        ALU.mult, ALU.add, rd=[PRM])
                if T == TH:
                    ts(FTAIL[ch], FTAIL[ch].ap, cq, cq.ap[:, n:n + 2], pc(P_FLAG), None, ALU.mult, rd=[PRM])
                else:
                    cp(FTAIL[ch], FTAIL[ch].ap, cq, cq.ap[:, n:n + 2])
                return a

            if T == TH:
                for ch in range(2 * FC):
                    ps = proj(w_up[ch], HB, c0, n)
                    ts(FTAIL[ch], FTAIL[ch].ap, ps, ps.ap[:, n - 2:n], pc(P_FLAG), None, ALU.mult, rd=[PRM])
                return
            for m in range(FC):
                psa = proj(w_up[m], HB, c0, n)
                aa = conv3(psa, m)
                psb = proj(w_up[FC + m], HB, c0, n)
                ab = conv3(psb, FC + m)
                act(aa, aa.ap[:, 0:n], aa, aa.ap[:, 0:n], AF.Silu)
                tt(ACTB[m], ACTB[m].ap[:, 0:n], aa, aa.ap[:, 0:n], ab, ab.ap[:, 0:n], ALU.mult)
            ss = PSS.next()
            for mo in range(KC):
                ps = pbsel[0].next()
                proj(w_down[mo, :, 0:22, :], ACTB, 0, n, nk=22, k0=0, ps=ps, first=True, last=False)
                proj(w_down[mo, :, 22:43, :], ACTB, 0, n, nk=21, k0=22, ps=ps, first=False, last=True)
                y_chunk_out(ps, mo, c0, n, ss)
            postnorm_residual(ss, c0, n, 6)

        def stream():
            pending_B = [None]

            for T in range(NTILES):
                is_main = T >= TH
                c0 = NT - HALO if T == TH else 0
                n = NT - c0
                for k in range(KC):
                    dma("sp", XA[k], XA[k].ap, None, xT[:, k, T * NT:(T + 1) * NT])
                rmsnorm_to_bf(XA, 0, NT, 0, HB)
                for sub in range(NSUB):
                    p = PSM.get()
                    for k in range(KC):
                        mm(p, p.ap[:, 0:32], HB[k], HB[k].ap[:, sub * 128:(sub + 1) * 128], WBA, wba_t[:, k, :],
                           start=(k == 0), stop=(k == KC - 1))
                    act(BG, bg_t[:, 0, sub, :], p, p.ap[:, 0:16], AF.Sigmoid)
                    ts(BG, bg_t[:, 1, sub, :], BG, bg_t[:, 0, sub, :], -1.0, None, ALU.mult)
                    f = FS.next()
                    tt(f, f.ap[:, 0:16], p, p.ap[:, 16:32], PRM, prm[:, P_DTB:P_DTB + 16], ALU.add)
                    PSM.put(p)
                    act(f, f.ap[:, 0:16], f, f.ap[:, 0:16], AF.Exp)
                    act(f, f.ap[:, 0:16], f, f.ap[:, 0:16], AF.Ln, bias=C(C_ONE, 1), rd=[CST])
                    tt(BG, bg_t[:, 2, sub, :], f, f.ap[:, 0:16], NGA, nga_t[:, :], ALU.mult)
                    gs_stage(sub)
                if T == 0:
                    dbg_dump("bg", BG, bg_t[:, :, :, :])
                    ckpt(3)
                if late_casts:
                    ntl = max(1, TH - 2 - T)
                    k_ = len(late_casts) if T >= TH - 2 else (len(late_casts) + ntl - 1) // ntl
                    for (src_, mo_) in late_casts[:k_]:
                        cast_weights(src_, [mo_], after=BG)
                    del late_casts[:k_]

                prevB = None
                pend_gn = None

                def gnorm(hg_):
                    for hh in range(HG):
                        h = hg_ * HG + hh
                        o_ap = OR[hh].ap[:, c0:c0 + n]
                        q = SQB.next()
                        tt(q, q.ap[:, 0:n], OR[hh], o_ap, OR[hh], o_ap, ALU.mult)
                        ss = PSS.next()
                        mm(ss, ss.ap[:, 0:n], ONESB, onesb_t[:, :], q, q.ap[:, 0:n])
                        r = rstd_from(ss, n, 1.0 / 128)
                        ps = proj(w_in[48 + h], HB, c0, n)
                        zs = FS.next()
                        act(zs, zs.ap[:, 0:n], ps, ps.ap[:, 0:n], AF.Silu)
                        f = FS.next()
                        stt(f, f.ap[:, 0:n], OR[hh], o_ap, pc(P_GN), r, r.ap[:, 0:n], ALU.mult, ALU.mult, rd=[PRM])
                        tt(CB[h], CB[h].ap[:, c0:c0 + n], f, f.ap[:, 0:n], zs, zs.ap[:, 0:n], ALU.mult)
                        if T == TH + 1 and h == 0:
                            dbg_dump("or0", OR[0], OR[0].ap)

                def proj_gen(hg_):
                    KT = KTS[hg_ % 2]; QT = QTS[hg_ % 2]
                    KTB = KTBS[hg_ % 2]; VTB = VTBS[hg_ % 2]; QTB = QTBS[hg_ % 2]
                    for hh in range(HG):
                        h = hg_ * HG + hh
                        ps = proj(w_in[16 + h], HB, 0, NT)
                        conv4(ps, 16 + h, NT, 0, KT[hh], KT[hh].ap[:, 0:NT])
                        yield
                        ps = proj(w_in[32 + h], HB, 0, NT)
                        conv4(ps, 32 + h, NT, 0, VTB[hh], VTB[hh].ap[:, 0:NT])
                        yield
                        if is_main:
                            if T == TH:
                                memset(QTB[hh], QTB[hh].ap, 0.0)
                            ps = proj(w_in[h], HB, c0, n)
                            conv4(ps, h, n, c0, QT[hh], QT[hh].ap[:, c0:c0 + n])
                            yield
                    for hh in range(HG):
                        l2norm_to(KT[hh], KT[hh].ap[:, 0:NT], KTB[hh], KTB[hh].ap[:, 0:NT], NT, 1.0)
                        yield
                        if is_main:
                            l2norm_to(QT[hh], QT[hh].ap[:, c0:c0 + n], QTB[hh], QTB[hh].ap[:, c0:c0 + n], n, 128.0 ** -0.5)
                            yield

                NG = NH // HG
                interleave([proj_gen(0)])

                def chain2(a, b):
                    yield from a
                    yield from b

                if not is_main:
                    prevS = None
                    for hg in range(NG):
                        sts_ = [dict() for _ in range(NSUB)]
                        gens = [gdn_prep(hg, sub, False, sts_[sub]) for sub in range(NSUB)]
                        if prevS is not None:
                            gens.append(prevS)
                        if hg + 1 < NG:
                            gens.append(proj_gen(hg + 1))
                        interleave(gens)
                        prevS = chain2(gdn_state(hg, 0, False, sts_[0]), gdn_state(hg, 1, False, sts_[1]))
                    interleave([prevS])
                for hg in (range(NG) if is_main else []):
                    for sub in range(NSUB):
                        full = is_main and (T > TH or sub == NSUB - 1)
                        st_ = dict()
                        gens = [gdn_prep(hg, sub, full, st_)] + (prevB if prevB else [])
                        if sub == 0 and hg + 1 < NG:
                            gens.append(proj_gen(hg + 1))
                        interleave(gens)
                        if pend_gn is not None:
                            gnorm(pend_gn)
                            pend_gn = None
                        prevB = [gdn_state(hg, sub, full, st_)]
                        if sub == NSUB - 1 and is_main:
                            pend_gn = hg
                if prevB:
                    interleave(prevB)
                prevB = None
                if pend_gn is not None:
                    gnorm(pend_gn)
                    pend_gn = None
                if T == NTILES - 1:
                    dbg_dump("s0", S[0], S[0].ap)
                if T == 0:
                    ckpt(7)
                if T == TH - 1:
                    ckpt(8)
                if T == TH:
                    ckpt(9)
                if is_main:
                    if T == TH + 1:
                        for k in range(KC):
                            pass
                    pbsel[0] = PB_MAIN
                    mixer_rest(T, c0, n)
                    if T == TH:
                        ckpt(10)
                    if T == TH + 1:
                        dbg_dump("x1", XA[0], XA[0].ap)
                    xattn(T, c0, n)
                    if T == TH:
                        ckpt(11)
                    if T == TH + 1:
                        dbg_dump("x2", XA[0], XA[0].ap)
                    ffn(T, c0, n)
                    pbsel[0] = PB_STREAM
                    if T > TH:
                        o0 = (T - TH - 1) * NT
                        for k in range(KC):
                            dma("sp", None, outT[:, k, o0:o0 + NT], XA[k], XA[k].ap)
        try:
            ckpt(1)
            xattn_kv()
            ckpt(2)
            stream()
        except _Stop:
            pass
        emit_program(nc, P, es)
    return nc


def _tile_w(W):
    K, M = W.shape
    return np.ascontiguousarray(W.reshape(K // 128, 128, M // 128, 128).transpose(2, 1, 0, 3))


def _fm(v, nch):
    return np.ascontiguousarray(np.asarray(v, np.float32).reshape(nch, 128).T)


def make_consts():
    c = np.zeros((128, NCONST), np.float32)
    i = np.arange(128)
    blk = (i[:, None] // 64) == (i[None, :] // 64)
    for r in range(4):
        c[:, C_ID + 128 * r:C_ID + 128 * r + 128] = np.eye(128)
        c[:, C_UBD + 128 * r:C_UBD + 128 * r + 128] = blk & (i[:, None] <= i[None, :])
        c[:, C_MSTR + 128 * r:C_MSTR + 128 * r + 128] = blk & (i[:, None] > i[None, :])
    c[:, C_ONES:C_ONES + 128] = 1.0
    c[:, C_OBD:C_OBD + 128] = blk
    c[:, C_BSEL] = i < 64
    c[:, C_BSEL + 1] = i >= 64
    c[:, C_EPS] = EPS
    c[:, C_ONE] = 1.0
    return c


def prep_inputs(cfg, inp):
    SEQ, OWN = cfg.SEQ, cfg.OWN
    f = lambda a: np.asarray(a, np.float32)
    w_in_full = f(inp["w_in"])[0]
    cols = np.concatenate([np.arange(0, 8192), np.arange(8224, 13344)])
    shared = {
        "consts": make_consts(),
        "w_in": _tile_w(w_in_full[:, cols]),
        "wba": np.ascontiguousarray(w_in_full[:, 8192:8224].reshape(KC, 128, 32).transpose(1, 0, 2)),
        "poolw": np.ascontiguousarray(f(inp["pool_w"])[0].reshape(4, 2, 128, 256).transpose(2, 0, 1, 3)),
        "w_a": _tile_w(f(inp["w_branch_a"])[0]),
        "w_b": _tile_w(f(inp["w_branch_b"])[0]),
        "w_mix": _tile_w(f(inp["w_mix_out"])[0]),
        "w_xq": _tile_w(f(inp["w_xq"])[0]),
        "w_xkv": _tile_w(f(inp["w_xkv"])[0]),
        "w_xo": _tile_w(f(inp["w_xo"])[0]),
        "w_up": _tile_w(f(inp["w_up"])[0]),
        "w_down": _tile_w(f(inp["w_down"])[0]),
    }
    prm = np.zeros((128, NPRM), np.float32)
    for gi, nm in enumerate(["mix_pre_norm", "mix_post_norm", "xa_pre_norm", "mem_norm", "xa_post_norm",
                             "ffn_pre_norm", "ffn_post_norm"]):
        prm[:, P_GAIN + 16 * gi:P_GAIN + 16 * gi + 16] = _fm(f(inp[nm])[0], 16)
    cq = f(inp["conv_qkv"])[0]
    prm[:, P_CONVQ:P_CONVQ + 192] = cq.reshape(4, 48, 128).transpose(2, 1, 0).reshape(128, 192)
    fw = f(inp["ffn_conv_w"])[0]
    prm[:, P_FCW:P_FCW + 258] = fw.reshape(3, 86, 128).transpose(2, 1, 0).reshape(128, 258)
    prm[:, P_FCB:P_FCB + 86] = _fm(f(inp["ffn_conv_b"])[0], 86)
    prm[:, P_PSC:P_PSC + 8] = _fm(f(inp["pool_scale"])[0], 8)
    prm[:, P_GN] = f(inp["gdn_norm"])[0]
    prm[:, P_ALOG:P_ALOG + 16] = f(inp["a_log"])[0][None, :]
    prm[:, P_DTB:P_DTB + 16] = f(inp["dt_bias"])[0][None, :]
    x = f(inp["x"])
    mem = f(inp["mem"])
    in_maps = []
    for c in range(8):
        b, j = c // 4, c % 4
        end = OWN * (j + 1)
        start = end - SEQ
        xs = np.zeros((SEQ, D), np.float32)
        if start < 0:
            xs[-start:] = x[b, 0:end]
        else:
            xs[:] = x[b, start:end]
        xTc = np.ascontiguousarray(xs.T.reshape(KC, 128, SEQ).transpose(1, 0, 2))
        memTc = np.ascontiguousarray(mem[b].T.reshape(KC, 128, MEM).transpose(1, 0, 2))
        p = prm.copy()
        p[:, P_FLAG] = 0.0 if j == 0 else 1.0
        own_start = OWN * j
        t = own_start + np.arange(16)
        for g, win in enumerate((2, 4, 8, 16)):
            p[:, P_INVC + 16 * g:P_INVC + 16 * g + 16] = (1.0 / np.minimum(t + 1, win))[None, :]
        m = dict(shared)
        m.update({"xT": xTc, "memT": memTc, "prm": p})
        in_maps.append(m)
    return in_maps


_NC_CACHE = {}


def run(cfg, inp, dbg=()):
    key = (cfg.SEQ, cfg.NT, tuple(dbg))
    if key not in _NC_CACHE:
        _NC_CACHE[key] = build(cfg, dbg)
    nc = _NC_CACHE[key]
    in_maps = prep_inputs(cfg, inp)
    res = run_bass_kernel_spmd(nc, in_maps, core_ids=list(range(8)))
    B = 2
    out = np.zeros((B, cfg.SEQ, D), np.float32)
    for c in range(8):
        b, j = c // 4, c % 4
        o = res.results[c]["outT"]
        out[b, cfg.OWN * j:cfg.OWN * (j + 1), :] = o.transpose(2, 1, 0).reshape(cfg.OWN, D)
    return out, res


def kernel(**inputs):
    cfg = Cfg(SEQ=8192, NT=256)
    out, _ = run(cfg, inputs)
    return out
```

```python
import numpy as np
import concourse.bass as bass
import concourse.mybir as mybir
from concourse.bass_utils import run_bass_kernel_spmd
from contextlib import ExitStack

F32 = mybir.dt.float32
BF16 = mybir.dt.bfloat16
AF = mybir.ActivationFunctionType
ALU = mybir.AluOpType

ENGS = ("pe", "act", "dve", "pool", "sp")
NDSEM = 24


class Tl:
    __slots__ = ("ap", "name", "w", "r", "excl")

    def __init__(self, ap, name="", excl=False):
        self.ap = ap
        self.name = name
        self.w = {}
        self.r = {}
        self.excl = excl


def _add(depmap, dep):
    if dep[0] == "c":
        k = ("c", dep[1])
        if k not in depmap or depmap[k][2] < dep[2]:
            depmap[k] = dep
    else:
        depmap[dep] = dep


class Prog:
    def __init__(self, nc):
        self.nc = nc
        self.ops = {e: [] for e in ENGS}
        self.ndma = {e: 0 for e in ENGS}

    def op(self, eng, fn, reads=(), writes=(), dma=False):
        ops = self.ops[eng]
        idx = len(ops)
        deps = {}
        ex = [t for t in reads if t.excl]
        if ex:
            reads = [t for t in reads if not t.excl]
            writes = list(writes) + [t for t in ex if t not in writes]
        for t in reads:
            for d in t.w.values():
                _add(deps, d)
        for t in writes:
            for d in t.w.values():
                _add(deps, d)
            for d in t.r.values():
                _add(deps, d)
        if dma:
            n = self.ndma[eng]
            self.ndma[eng] += 1
            me = ("d", eng, n)
            if n >= NDSEM:
                _add(deps, ("d", eng, n - NDSEM))
        else:
            me = ("c", eng, idx)
        if eng == "pe":
            deps.pop(("c", "pe"), None)
        rec = dict(fn=fn, deps=list(deps.values()), me=me, needed=False)
        ops.append(rec)
        for t in writes:
            if t.r:
                t.w = {}
                t.r = {}
            _add(t.w, me)
        for t in reads:
            _add(t.r, me)
        return rec


def emit_program(nc, prog, es):
    for e in ENGS:
        for rec in prog.ops[e]:
            for d in rec["deps"]:
                if d[0] == "c":
                    prog.ops[d[1]][d[2]]["needed"] = True
    cum = {}
    for e in ENGS:
        c = 0
        arr = []
        for rec in prog.ops[e]:
            if rec["me"][0] == "c" and rec["needed"]:
                c += 1
            arr.append(c)
        cum[e] = arr
    csem = {e: es.enter_context(nc.semaphore("cs_" + e)) for e in ENGS}
    dsem = {}
    for e in ENGS:
        if prog.ndma[e] > 0:
            dsem[e] = [es.enter_context(nc.semaphore("ds_%s_%d" % (e, i)))
                       for i in range(min(NDSEM, prog.ndma[e]))]
    block = es.enter_context(nc.Block())

    def run(ename, engobj):
        waited = {}
        for rec in prog.ops[ename]:
            for d in rec["deps"]:
                if d[0] == "c":
                    sem = csem[d[1]]
                    val = cum[d[1]][d[2]]
                    key = ("c", d[1])
                else:
                    sem = dsem[d[1]][d[2] % NDSEM]
                    val = 16 * (d[2] // NDSEM + 1)
                    key = ("d", d[1], d[2] % NDSEM)
                if waited.get(key, 0) >= val:
                    continue
                waited[key] = val
                engobj.wait_ge(sem, val)
            ins = rec["fn"](engobj)
            me = rec["me"]
            if me[0] == "c":
                if rec["needed"]:
                    ins.then_inc(csem[ename], 1)
            else:
                ins.then_inc(dsem[ename][me[2] % NDSEM], 16)
        if ename in dsem:
            n = prog.ndma[ename]
            for i in range(min(NDSEM, n)):
                last = ((n - 1 - i) // NDSEM) * NDSEM + i
                engobj.wait_ge(dsem[ename][i], 16 * (last // NDSEM + 1))

    @block.tensor
    def _(eng):
        run("pe", eng)

    @block.scalar
    def _(eng):
        run("act", eng)

    @block.vector
    def _(eng):
        run("dve", eng)

    @block.gpsimd
    def _(eng):
        run("pool", eng)

    @block.sync
    def _(eng):
        run("sp", eng)


class FreeList:
    def __init__(self, tiles):
        self.free = list(tiles)
        self.total = len(tiles)
        self.low = len(tiles)

    def get(self):
        if not self.free:
            raise RuntimeError("freelist exhausted")
        t = self.free.pop(0)
        self.low = min(self.low, len(self.free))
        return t

    def put(self, t):
        self.free.append(t)


class Ring:
    def __init__(self, tiles):
        self.t = list(tiles)
        self.i = 0

    def next(self):
        t = self.t[self.i % len(self.t)]
        self.i += 1
        return t


class _Stop(Exception):
    pass


class Cfg:
    stop = 0

    def __init__(self, SEQ=8192, NT=256):
        self.SEQ = SEQ
        self.NT = NT
        self.OWN = SEQ // 4
        self.HALO = 32
        self.NTILES = SEQ // NT
        self.TH = (SEQ - self.OWN) // NT - 1
        self.NSUB = NT // 128


D = 2048
KC = 16
NH = 16
DFF = 5504
FC = 43
MEM = 256
HG = 4
EPS = 1e-6

P_GAIN = 0
P_CONVQ = 112
P_FCW = 304
P_FCB = 562
P_PSC = 648
P_GN = 656
P_ALOG = 657
P_DTB = 673
P_FLAG = 689
P_INVC = 690
NPRM = 754
C_ID = 0
C_UBD = 512
C_MSTR = 1024
C_ONES = 1536
C_OBD = 1664
C_BSEL = 1792
C_EPS = 1794
C_ONE = 1795
NCONST = 1800


def build(cfg, dbg=()):
    SEQ, NT, OWN, HALO, NTILES, TH, NSUB = cfg.SEQ, cfg.NT, cfg.OWN, cfg.HALO, cfg.NTILES, cfg.TH, cfg.NSUB
    nc = bass.Bass("TRN2", target_bir_lowering=False)
    nc.dge_precook = False

    def din(name, shape):
        return nc.dram_tensor(name, shape, F32, kind="ExternalInput").ap()

    xT = din("xT", [128, KC, SEQ])
    memT = din("memT", [128, KC, MEM])
    consts_d = din("consts", [128, NCONST])
    prm_d = din("prm", [128, NPRM])
    wba_d = din("wba", [128, KC, 32])
    poolw_d = din("poolw", [128, 4, 2, 256])
    class WRef:
        def __init__(self, src, mo, k0, k1):
            self.src, self.mo, self.k0, self.k1 = src, mo, k0, k1

    class WSrc:
        def __init__(self, name, nch, nk):
            self.name, self.nch, self.nk = name, nch, nk
            self.f32 = din(name, [nch, 128, nk, 128])
            self.b16 = nc.dram_tensor(name + "_b16", [nch, 128, nk, 128], BF16).ap()
            self.tl = [Tl(None) for _ in range(nch)]

        def __getitem__(self, idx):
            if isinstance(idx, tuple):
                mo, ks = idx[0], idx[2]
                return WRef(self, mo, ks.start, ks.stop)
            return WRef(self, idx, 0, self.nk)

    w_in = WSrc("w_in", 104, KC)
    w_a = WSrc("w_a", 16, KC)
    w_b = WSrc("w_b", 16, 8)
    w_mix = WSrc("w_mix", 16, KC)
    w_xq = WSrc("w_xq", 16, KC)
    w_xkv = WSrc("w_xkv", 32, KC)
    w_xo = WSrc("w_xo", 16, KC)
    w_up = WSrc("w_up", 86, KC)
    w_down = WSrc("w_down", 16, FC)
    outT = nc.dram_tensor("outT", [128, KC, OWN], F32, kind="ExternalOutput").ap()
    dbg_out = {}
    for name, shape in dbg:
        dbg_out[name] = nc.dram_tensor("dbg_" + name, list(shape), F32, kind="ExternalOutput").ap()

    es = ExitStack()
    with es:
        def sb(name, shape, dt=F32):
            return es.enter_context(nc.sbuf_tensor(name, list(shape), dt))

        def pst(name, shape, dt=F32):
            return es.enter_context(nc.psum_tensor(name, list(shape), dt))

        P = Prog(nc)

        def ckpt(i):
            if cfg.stop == i:
                raise _Stop()

        xa_t = sb("xa", [128, KC, NT]); XA = [Tl(xa_t[:, k, :]) for k in range(KC)]
        hb_t = sb("hb", [128, KC, NT], BF16); HB = [Tl(hb_t[:, k, :]) for k in range(KC)]
        cb_t = sb("cb", [128, KC, NT], BF16); CB = [Tl(cb_t[:, k, :]) for k in range(KC)]
        db_t = sb("db", [128, KC, NT]); DB = [Tl(db_t[:, k, :]) for k in range(KC)]
        act_t = sb("actb", [128, FC, NT], BF16); ACTB = [Tl(act_t[:, k, :]) for k in range(FC)]
        s_t = sb("state", [128, NH, 128]); S = [Tl(s_t[:, h, :]) for h in range(NH)]
        NW = 3
        w_ts = [sb("wt%d" % i, [128, 22, 128], BF16) for i in range(NW)]
        WR = Ring([Tl(t) for t in w_ts])
        kx_t = sb("kx", [128, KC, MEM], BF16); KX = Tl(kx_t)
        vx_t = sb("vx", [128, 2, D], BF16); VX = Tl(vx_t)
        cst = sb("cst", [128, NCONST]); CST = Tl(cst)
        prm = sb("prm_s", [128, NPRM]); PRM = Tl(prm)
        onesb_t = sb("onesb", [128, 128], BF16); ONESB = Tl(onesb_t)
        wba_t = sb("wba_s", [128, KC, 32], BF16); WBA = Tl(wba_t)
        poolw_t = sb("poolw_s", [128, 4, 2, 256], BF16); POOLW = Tl(poolw_t)
        nga_t = sb("nga", [128, NH]); NGA = Tl(nga_t)
        qtail_t = sb("qtail", [128, 48, 3]); QTAIL = [Tl(qtail_t[:, c, :]) for c in range(48)]
        ptail_t = sb("pbuf", [128, 8, 15 + NT]); PBUF = [Tl(ptail_t[:, c, :]) for c in range(8)]
        ftail_t = sb("ftail", [128, 86, 2]); FTAIL = [Tl(ftail_t[:, c, :]) for c in range(86)]
        bg_ts = [sb("bg%d" % i, [128, 3, NSUB, NH]) for i in range(2)]; BGs = [Tl(t) for t in bg_ts]
        kt_t = sb("kt", [128, HG, NT]); KT0 = [Tl(kt_t[:, i, :]) for i in range(HG)]
        qt_t = sb("qt", [128, HG, NT]); QT0 = [Tl(qt_t[:, i, :]) for i in range(HG)]
        ktb_t = sb("ktb", [128, HG, NT], BF16); KTB0 = [Tl(ktb_t[:, i, :]) for i in range(HG)]
        vtb_t = sb("vtb", [128, HG, NT], BF16); VTB0 = [Tl(vtb_t[:, i, :]) for i in range(HG)]
        qtb_t = sb("qtb", [128, HG, NT], BF16); QTB0 = [Tl(qtb_t[:, i, :]) for i in range(HG)]
        act32 = act_t.bitcast(F32)
        cpr = NT // 128

        def alias32(j):
            return Tl(act32[:, j * cpr:(j + 1) * cpr, :].rearrange("p a b -> p (a b)"))
        KT1 = [alias32(i) for i in range(HG)]
        QT1 = [alias32(HG + i) for i in range(HG)]
        r0 = 2 * HG * cpr
        KTB1 = [Tl(act_t[:, r0 + i, :]) for i in range(HG)]
        VTB1 = [Tl(act_t[:, r0 + HG + i, :]) for i in range(HG)]
        QTB1 = [Tl(act_t[:, r0 + 2 * HG + i, :]) for i in range(HG)]
        r1 = r0 + 3 * HG
        assert r1 + 8 <= FC
        KTS = [KT0, KT1]; QTS = [QT0, QT1]
        KTBS = [KTB0, KTB1]; VTBS = [VTB0, VTB1]; QTBS = [QTB0, QTB1]
        or_t = sb("or", [128, HG, NT]); OR = [Tl(or_t[:, i, :]) for i in range(HG)]
        NTMP = 7
        tmp_t = sb("gtmp", [128, NTMP, 512])
        TMP = FreeList([Tl(tmp_t[:, i, :]) for i in range(NTMP)])
        NTMPB = 15
        tmpb_t = sb("gtmpb", [128, NTMPB, 512], BF16)
        TMPB = FreeList([Tl(tmpb_t[:, i, :]) for i in range(NTMPB)])
        identb_t = sb("identb", [128, 128], BF16); IDENTB = Tl(identb_t)
        db16 = db_t.bitcast(BF16)
        for j_ in range(8):
            TMPB.put(Tl(db16[:, 2 * j_:2 * j_ + 2, :].rearrange("p a b -> p (a b)")[:, 0:512]))
        gs_ts = [sb("gs%d" % i, [128, NSUB, 8, NH]) for i in range(2)]; GSs = [Tl(t) for t in gs_ts]
        cq_t = sb("cq", [128, 2, 3 + NT]); CQ = Ring([Tl(cq_t[:, i, :]) for i in range(2)])
        acc_t = sb("acc", [128, 3, NT]); ACC = Ring([Tl(acc_t[:, i, :]) for i in range(3)])
        sqb_t = sb("sqb", [128, 3, NT], BF16); SQB = Ring([Tl(sqb_t[:, i, :]) for i in range(3)])
        rs_t = sb("rs", [128, 2, NT]); RS = Ring([Tl(rs_t[:, i, :]) for i in range(2)])
        f32s_t = sb("f32s", [128, 3, NT]); FS = Ring([Tl(f32s_t[:, i, :]) for i in range(3)])
        et_t = sb("et", [128, 4, NT], BF16); ET = Ring([Tl(et_t[:, i, :]) for i in range(4)])
        YPI = [Tl(act_t[:, r1 + i, :]) for i in range(8)]
        ypo_t = sb("ypo", [128, 8, NT], BF16); YPO = [Tl(ypo_t[:, i, :]) for i in range(8)]
        pw_t = sb("pw", [128, 2, 2, 15 + NT]); PW = [Tl(pw_t[:, i, :, :]) for i in range(2)]

        pbig = [pst("pbig%d" % i, [128, 512]) for i in range(2)]
        PB = Ring([Tl(pbig[i][:, :], excl=True) for i in range(2)])
        pss = pst("pss", [128, 512])
        PSS = Ring([Tl(pss[:, :], excl=True)])
        psm = [pst("psm%d" % i, [128, 512]) for i in range(5)]
        PSM = FreeList([Tl(psm[i][:, :], excl=True) for i in range(5)])
        PB_STREAM = PB
        PB_MAIN = Ring(PB.t + PSM.free[0:4])
        pbsel = [PB_STREAM]

        def C(c0, n=128):
            return cst[:, c0:c0 + n]

        def pc(c):
            return prm[:, c:c + 1]

        def mm(out_tl, out_ap, a_tl, a_ap, b_tl, b_ap, start=True, stop=True):
            P.op("pe", lambda e: e.matmul(out_ap, a_ap, b_ap, start=start, stop=stop),
                 reads=[a_tl, b_tl], writes=[out_tl])

        def act(out_tl, out_ap, in_tl, in_ap, func, bias=None, scale=None, rd=()):
            kw = {}
            if bias is not None:
                kw["bias"] = bias
            if scale is not None:
                kw["scale"] = scale
            P.op("act", lambda e: e.activation(out=out_ap, in_=in_ap, func=func, **kw),
                 reads=[in_tl] + list(rd), writes=[out_tl])

        def ts(out_tl, out_ap, in_tl, in_ap, s1, s2, op0, op1=None, rd=(), eng="dve"):
            if op1 is None:
                P.op(eng, lambda e: e.tensor_scalar(out_ap, in_ap, s1, None, op0),
                     reads=[in_tl] + list(rd), writes=[out_tl])
            else:
                P.op(eng, lambda e: e.tensor_scalar(out_ap, in_ap, s1, s2, op0, op1),
                     reads=[in_tl] + list(rd), writes=[out_tl])

        def stt(out_tl, out_ap, in0_tl, in0_ap, scalar, in1_tl, in1_ap, op0, op1, rd=()):
            P.op("dve", lambda e: e.scalar_tensor_tensor(out_ap, in0_ap, scalar, in1_ap, op0, op1),
                 reads=[in0_tl, in1_tl] + list(rd), writes=[out_tl])

        def tt(out_tl, out_ap, a_tl, a_ap, b_tl, b_ap, op, eng="dve"):
            P.op(eng, lambda e: e.tensor_tensor(out_ap, a_ap, b_ap, op),
                 reads=[a_tl, b_tl], writes=[out_tl])

        def cp(out_tl, out_ap, in_tl, in_ap, eng="dve"):
            P.op(eng, lambda e: e.tensor_copy(out_ap, in_ap), reads=[in_tl], writes=[out_tl])

        def recip(out_tl, out_ap, in_tl, in_ap):
            P.op("dve", lambda e: e.reciprocal(out_ap, in_ap), reads=[in_tl], writes=[out_tl])

        def memset(tl, ap, val, eng="dve"):
            P.op(eng, lambda e: e.memset(ap, val), writes=[tl])

        def dma(eng, out_tl, out_ap, in_tl, in_ap):
            P.op(eng, lambda e: e.dma_start(out=out_ap, in_=in_ap),
                 reads=[in_tl] if in_tl is not None else [],
                 writes=[out_tl] if out_tl is not None else [], dma=True)

        def dbg_dump(name, tl, ap):
            if name in dbg_out:
                dma("sp", None, dbg_out[name], tl, ap)

        def load_w(ref, nk):
            w = WR.next()
            assert ref.k1 - ref.k0 == nk
            dma("sp", w, w.ap[:, 0:nk, :], ref.src.tl[ref.mo], ref.src.b16[ref.mo, :, ref.k0:ref.k1, :])
            return w

        def cast_weights(src, chunks, after=None):
            for mo in chunks:
                dma("pool", src.tl[mo], src.b16[mo], after, src.f32[mo])

        late_casts = []

        def proj(wd, in_list, c0, n, nk=KC, k0=0, ps=None, first=True, last=True):
            w = load_w(wd, nk)
            if ps is None:
                ps = pbsel[0].next()
            for k in range(nk):
                mm(ps, ps.ap[:, 0:n], w, w.ap[:, k, :], in_list[k0 + k], in_list[k0 + k].ap[:, c0:c0 + n],
                   start=(first and k == 0), stop=(last and k == nk - 1))
            return ps

        def rstd_from(ss, n, scale, out=None):
            r = RS.next() if out is None else out
            act(r, r.ap[:, 0:n], ss, ss.ap[:, 0:n], AF.Sqrt, bias=C(C_EPS, 1), scale=scale, rd=[CST])
            recip(r, r.ap[:, 0:n], r, r.ap[:, 0:n])
            return r

        def rmsnorm_to_bf(src, c0, n, gi, dst):
            ss = PSS.next()
            for k in range(KC):
                q = SQB.next()
                act(q, q.ap[:, 0:n], src[k], src[k].ap[:, c0:c0 + n], AF.Square)
                mm(ss, ss.ap[:, 0:n], ONESB, onesb_t[:, :], q, q.ap[:, 0:n], start=(k == 0), stop=(k == KC - 1))
            r = rstd_from(ss, n, 1.0 / D)
            for k in range(KC):
                stt(dst[k], dst[k].ap[:, c0:c0 + n], src[k], src[k].ap[:, c0:c0 + n], pc(P_GAIN + 16 * gi + k),
                    r, r.ap[:, 0:n], ALU.mult, ALU.mult, rd=[PRM])

        def postnorm_residual(ss, c0, n, gi):
            r = rstd_from(ss, n, 1.0 / D)
            for k in range(KC):
                f = FS.next()
                tt(f, f.ap[:, 0:n], DB[k], DB[k].ap[:, c0:c0 + n], r, r.ap[:, 0:n], ALU.mult)
                stt(XA[k], XA[k].ap[:, c0:c0 + n], f, f.ap[:, 0:n], pc(P_GAIN + 16 * gi + k),
                    XA[k], XA[k].ap[:, c0:c0 + n], ALU.mult, ALU.add, rd=[PRM])

        def y_chunk_out(ps, mo, c0, n, ss):
            act(DB[mo], DB[mo].ap[:, c0:c0 + n], ps, ps.ap[:, 0:n], AF.Copy)
            q = SQB.next()
            act(q, q.ap[:, 0:n], ps, ps.ap[:, 0:n], AF.Square)
            mm(ss, ss.ap[:, 0:n], ONESB, onesb_t[:, :], q, q.ap[:, 0:n], start=(mo == 0), stop=(mo == KC - 1))

        def body():
            pass

        dma("sp", CST, cst[:, :], None, consts_d)
        dma("sp", PRM, prm[:, :], None, prm_d)
        dma("pool", WBA, wba_t[:, :, :], None, wba_d)
        dma("pool", POOLW, poolw_t[:, :, :, :], None, poolw_d)
        cast_weights(w_xkv, range(32))
        cast_weights(w_in, range(16, 48))
        for mo_ in list(range(0, 16)) + list(range(48, 104)):
            late_casts.append((w_in, mo_))
        for src_ in (w_a, w_b, w_mix, w_xq, w_xo, w_up, w_down):
            for mo_ in range(src_.nch):
                late_casts.append((src_, mo_))
        cp(ONESB, onesb_t[:, :], CST, C(C_ONES))
        cp(IDENTB, identb_t[:, :], CST, C(C_ID))
        for h in range(NH):
            memset(S[h], S[h].ap, 0.0)
        for c in range(48):
            memset(QTAIL[c], QTAIL[c].ap, 0.0)
        for c in range(8):
            memset(PBUF[c], PBUF[c].ap, 0.0)
        for c in range(86):
            memset(FTAIL[c], FTAIL[c].ap, 0.0)
        act(NGA, nga_t[:, :], PRM, prm[:, P_ALOG:P_ALOG + 16], AF.Exp)
        ts(NGA, nga_t[:, :], NGA, nga_t[:, :], -1.0, None, ALU.mult)

        def xattn_kv():
            assert NT == MEM
            MXl = DB
            MTl = HB
            for k in range(KC):
                dma("sp", MXl[k], MXl[k].ap, None, memT[:, k, :])
            ckpt(21)
            rmsnorm_to_bf(MXl, 0, MEM, 3, MTl)
            ckpt(22)
            for mo in range(KC):
                ps = proj(w_xkv[mo], MTl, 0, MEM)
                act(KX, kx_t[:, mo, :], ps, ps.ap[:, 0:MEM], AF.Copy)
                ckpt(100 + mo)
            ckpt(24)
            for mo in range(KC):
                ps = proj(w_xkv[KC + mo], MTl, 0, MEM)
                f = FS.next()
                act(f, f.ap[:, 0:MEM], ps, ps.ap[:, 0:MEM], AF.Copy)
                for mt in range(2):
                    p2 = PSM.get()
                    mm(p2, p2.ap[:, 0:128], f, f.ap[:, mt * 128:(mt + 1) * 128], CST, C(C_ID))
                    cp(VX, vx_t[:, mt, mo * 128:(mo + 1) * 128], p2, p2.ap[:, 0:128])
                    PSM.put(p2)

        def conv4(ps, ch, n, c0, out_tl, out_ap_silu):
            cq = CQ.next()
            a = ACC.next()
            act(cq, cq.ap[:, 3:3 + n], ps, ps.ap[:, 0:n], AF.Copy)
            cp(cq, cq.ap[:, 0:3], QTAIL[ch], QTAIL[ch].ap)
            act(a, a.ap[:, 0:n], ps, ps.ap[:, 0:n], AF.Identity, scale=pc(P_CONVQ + 4 * ch + 3), rd=[PRM])
            for j in range(3):
                stt(a, a.ap[:, 0:n], cq, cq.ap[:, j:j + n], pc(P_CONVQ + 4 * ch + j), a, a.ap[:, 0:n],
                    ALU.mult, ALU.add, rd=[PRM])
            cp(QTAIL[ch], QTAIL[ch].ap, cq, cq.ap[:, n:n + 3])
            act(out_tl, out_ap_silu, a, a.ap[:, 0:n], AF.Silu)

        def l2norm_to(tl, ap, otl, oap, n, mul):
            q = SQB.next()
            tt(q, q.ap[:, 0:n], tl, ap, tl, ap, ALU.mult)
            ss = PSS.next()
            mm(ss, ss.ap[:, 0:n], ONESB, onesb_t[:, :], q, q.ap[:, 0:n])
            r = rstd_from(ss, n, 1.0)
            if mul == 1.0:
                tt(otl, oap, tl, ap, r, r.ap[:, 0:n], ALU.mult)
            else:
                stt(otl, oap, tl, ap, mul, r, r.ap[:, 0:n], ALU.mult, ALU.mult)

        def Q4(t, i):
            return t.ap[:, i * 128:(i + 1) * 128]

        def gs_stage(sub, par):
            bg_t = bg_ts[par]; BG = BGs[par]; gs_t = gs_ts[par]; GS = GSs[par]
            g = bg_t[:, 2, sub, :]
            f = FS.next()
            for c in range(2):
                ts(f, f.ap[:, 16 * c:16 * c + 16], BG, g, C(C_BSEL + c, 1), None, ALU.mult, rd=[CST])
            bk = PSS.next()
            mm(bk, bk.ap[:, 0:16], CST, C(C_UBD), BG, g)
            mm(bk, bk.ap[:, 16:32], CST, C(C_OBD), BG, g)
            mm(bk, bk.ap[:, 32:64], CST, C(C_ONES), f, f.ap[:, 0:32])
            for r_ in range(4):
                cp(GS, gs_t[:, sub, r_, :], bk, bk.ap[:, 16 * r_:16 * r_ + 16])
            f2 = FS.next()
            act(f2, f2.ap[:, 0:16], GS, gs_t[:, sub, 0, :], AF.Exp)
            tt(GS, gs_t[:, sub, 4, :], f2, f2.ap[:, 0:16], BG, bg_t[:, 0, sub, :], ALU.mult)
            tt(f2, f2.ap[:, 16:32], GS, gs_t[:, sub, 1, :], GS, gs_t[:, sub, 0, :], ALU.subtract)
            act(GS, gs_t[:, sub, 5, :], f2, f2.ap[:, 16:32], AF.Exp)
            for r_ in range(2):
                act(GS, gs_t[:, sub, 6 + r_, :], GS, gs_t[:, sub, 2 + r_, :], AF.Exp)

        def gdn_prep(hg, sub, full, st, par=0):
            bg_t = bg_ts[par]; BG = BGs[par]; gs_t = gs_ts[par]; GS = GSs[par]
            cs = slice(sub * 128, sub * 128 + 128)
            heads = [hg * HG + i for i in range(HG)]
            KT = KTBS[hg % 2]; VT = VTBS[hg % 2]; QT = QTBS[hg % 2]

            def gcol(row, h):
                return gs_t[:, sub, row, h:h + 1]
            gd = TMP.get()
            for i, h in enumerate(heads):
                ts(gd, Q4(gd, i), CST, C(C_ID), gcol(0, h), -1.0, ALU.mult, ALU.mult, rd=[GS])
            bk = PSM.get()
            for i in range(HG):
                mm(bk, Q4(bk, i), CST, C(C_ONES), gd, Q4(gd, i))
            yield
            Dm = TMP.get()
            for i, h in enumerate(heads):
                ts(Dm, Q4(Dm, i), bk, Q4(bk, i), gcol(0, h), 0.0, ALU.add, ALU.min, rd=[GS])
            if full:
                DT = TMP.get(); eG = TMP.get()
                for i, h in enumerate(heads):
                    ts(DT, Q4(DT, i), bk, Q4(bk, i), -1.0, gcol(0, h), ALU.mult, ALU.subtract, rd=[GS])
                ts(DT, DT.ap, DT, DT.ap, 0.0, None, ALU.min)
                act(eG, eG.ap, bk, bk.ap, AF.Exp, scale=-1.0)
            PSM.put(bk)
            TMP.put(gd)
            yield
            act(Dm, Dm.ap, Dm, Dm.ap, AF.Exp)
            if full:
                act(DT, DT.ap, DT, DT.ap, AF.Exp)
            yield
            tt(Dm, Dm.ap, Dm, Dm.ap, CST, C(C_MSTR, 512), ALU.mult)
            if full:
                tt(DT, DT.ap, DT, DT.ap, CST, C(C_UBD, 512), ALU.mult)
            bk = PSM.get()
            for i in range(HG):
                mm(bk, Q4(bk, i), KT[i], KT[i].ap[:, cs], KT[i], KT[i].ap[:, cs])
            yield
            N0 = TMPB.get()
            for i, h in enumerate(heads):
                stt(N0, Q4(N0, i), bk, Q4(bk, i), bg_t[:, 1, sub, h:h + 1], Dm, Q4(Dm, i), ALU.mult, ALU.mult, rd=[BG])
            PSM.put(bk)
            TMP.put(Dm)
            yield
            bk = PSM.get()
            for i in range(HG):
                mm(bk, Q4(bk, i), N0, Q4(N0, i), IDENTB, identb_t[:, :])
            if full:
                bk2 = PSM.get()
                for i in range(HG):
                    mm(bk2, Q4(bk2, i), KT[i], KT[i].ap[:, cs], QT[i], QT[i].ap[:, cs])
            yield
            N0T = TMPB.get(); TT = TMPB.get()
            act(N0T, N0T.ap, bk, bk.ap, AF.Copy)
            tt(TT, TT.ap, bk, bk.ap, CST, C(C_ID, 512), ALU.add)
            PSM.put(bk)
            if full:
                AT = TMPB.get(); Qd = TMPB.get()
                tt(AT, AT.ap, bk2, bk2.ap, DT, DT.ap, ALU.mult)
                PSM.put(bk2)
                for i in range(HG):
                    tt(Qd, Q4(Qd, i), QT[i], QT[i].ap[:, cs], eG, Q4(eG, i), ALU.mult)
                TMP.put(DT); TMP.put(eG)
                st["AT"] = AT; st["Qd"] = Qd
            yield
            Pk, PTk = N0, N0T
            for k in range(5):
                b1 = PSM.get()
                for i in range(HG):
                    mm(b1, Q4(b1, i), PTk, Q4(PTk, i), Pk, Q4(Pk, i))
                if k < 4:
                    b2 = PSM.get()
                    for i in range(HG):
                        mm(b2, Q4(b2, i), Pk, Q4(Pk, i), PTk, Q4(PTk, i))
                yield
                Pn = TMPB.get()
                act(Pn, Pn.ap, b1, b1.ap, AF.Copy)
                PSM.put(b1)
                if k < 4:
                    PTn = TMPB.get()
                    cp(PTn, PTn.ap, b2, b2.ap)
                    PSM.put(b2)
                yield
                b3 = PSM.get()
                for i in range(HG):
                    mm(b3, Q4(b3, i), Pn, Q4(Pn, i), TT, Q4(TT, i))
                yield
                tt(TT, TT.ap, TT, TT.ap, b3, b3.ap, ALU.add)
                PSM.put(b3)
                TMPB.put(Pk); TMPB.put(PTk)
                Pk = Pn
                PTk = PTn if k < 4 else None
                yield
            TMPB.put(Pk)
            bK = PSM.get(); bV = PSM.get()
            for i in range(HG):
                mm(bK, Q4(bK, i), KT[i], KT[i].ap[:, cs], IDENTB, identb_t[:, :])
                mm(bV, Q4(bV, i), VT[i], VT[i].ap[:, cs], IDENTB, identb_t[:, :])
            yield
            Rv = TMPB.get(); Rk = TMPB.get(); Kd = TMPB.get()
            for i, h in enumerate(heads):
                ts(Rv, Q4(Rv, i), bV, Q4(bV, i), bg_t[:, 0, sub, h:h + 1], None, ALU.mult, rd=[BG])
                ts(Rk, Q4(Rk, i), bK, Q4(bK, i), gcol(4, h), None, ALU.mult, rd=[GS])
                act(Kd, Q4(Kd, i), bK, Q4(bK, i), AF.Identity, scale=gcol(5, h), rd=[GS])
            PSM.put(bK); PSM.put(bV)
            yield
            bU = PSM.get(); bW = PSM.get()
            for i in range(HG):
                mm(bU, Q4(bU, i), TT, Q4(TT, i), Rv, Q4(Rv, i))
                mm(bW, Q4(bW, i), Rk, Q4(Rk, i), TT, Q4(TT, i))
            yield
            TMPB.put(Rv); TMPB.put(Rk); TMPB.put(TT)
            Uv = TMP.get(); WkT = TMPB.get()
            act(Uv, Uv.ap, bU, bU.ap, AF.Copy)
            cp(WkT, WkT.ap, bW, bW.ap)
            PSM.put(bU); PSM.put(bW)
            st["Uv"] = Uv; st["WkT"] = WkT; st["Kd"] = Kd
            yield

        def gdn_state(hg, sub, full, st, par=0):
            bg_t = bg_ts[par]; BG = BGs[par]; gs_t = gs_ts[par]; GS = GSs[par]
            heads = [hg * HG + i for i in range(HG)]
            Uv = st["Uv"]; WkT = st["WkT"]; Kd = st["Kd"]
            u = TMPB.get(); Sb = TMPB.get()
            Sg = [S[h] for h in heads]
            sg_ap = s_t[:, hg * HG:(hg + 1) * HG, :].rearrange("p a b -> p (a b)")
            P.op("act", lambda e: e.activation(out=Sb.ap, in_=sg_ap, func=AF.Copy), reads=Sg, writes=[Sb])
            yield
            for c in range(2):
                r = slice(64 * c, 64 * c + 64)
                bk = PSM.get()
                for i, h in enumerate(heads):
                    if c == 0:
                        mm(bk, bk.ap[0:64, i * 128:(i + 1) * 128], WkT, WkT.ap[:, i * 128:i * 128 + 64], Sb, Q4(Sb, i))
                    else:
                        mm(bk, Q4(bk, i), WkT, Q4(WkT, i), Sb, Q4(Sb, i))
                yield
                tt(u, u.ap[r, :], Uv, Uv.ap[r, :], bk, bk.ap[r, :], ALU.subtract)
                PSM.put(bk)
                yield
                bk = PSM.get()
                for i, h in enumerate(heads):
                    mm(bk, Q4(bk, i), Kd, Kd.ap[r, i * 128:(i + 1) * 128], u, u.ap[r, i * 128:(i + 1) * 128])
                if full:
                    bo = PSM.get()
                    Qd = st["Qd"]; AT = st["AT"]
                    for i, h in enumerate(heads):
                        oq = bo.ap[:, i * 128:i * 128 + 64]
                        mm(bo, oq, Sb, Q4(Sb, i), Qd, Qd.ap[:, i * 128 + 64 * c:i * 128 + 64 * c + 64], start=True, stop=False)
                        mm(bo, oq, u, u.ap[r, i * 128:(i + 1) * 128], AT, AT.ap[r, i * 128 + 64 * c:i * 128 + 64 * c + 64],
                           start=False, stop=True)
                yield
                for i, h in enumerate(heads):
                    stt(S[h], S[h].ap, S[h], S[h].ap, gs_t[:, sub, 6 + c, h:h + 1], bk, Q4(bk, i), ALU.mult, ALU.add, rd=[GS])
                PSM.put(bk)
                if full:
                    o0 = sub * 128 + 64 * c
                    for i in range(HG):
                        act(OR[i], OR[i].ap[:, o0:o0 + 64], bo, bo.ap[:, i * 128:i * 128 + 64], AF.Copy)
                    PSM.put(bo)
                yield
                if c == 0:
                    P.op("act", lambda e: e.activation(out=Sb.ap, in_=sg_ap, func=AF.Copy), reads=Sg, writes=[Sb])
                    yield
            TMPB.put(u); TMPB.put(Sb); TMP.put(Uv); TMPB.put(WkT); TMPB.put(Kd)
            if full:
                TMPB.put(st["AT"]); TMPB.put(st["Qd"])

        def interleave(gens):
            gens = list(gens)
            while gens:
                nxt = []
                for g in gens:
                    try:
                        next(g)
                        nxt.append(g)
                    except StopIteration:
                        pass
                gens = nxt

        def mixer_rest(T, c0, n):
            for mo in range(KC):
                ps1 = proj(w_in[72 + mo], HB, c0, n)
                sg = FS.next()
                act(sg, sg.ap[:, 0:n], ps1, ps1.ap[:, 0:n], AF.Sigmoid)
                ps2 = proj(w_a[mo], CB, c0, n)
                tt(DB[mo], DB[mo].ap[:, c0:c0 + n], ps2, ps2.ap[:, 0:n], sg, sg.ap[:, 0:n], ALU.mult)
            for pcix in range(8):
                ps = proj(w_in[64 + pcix], HB, c0, n)
                act(PBUF[pcix], PBUF[pcix].ap[:, 15:15 + n], ps, ps.ap[:, 0:n], AF.Copy)
            L = 15 + n
            for g in range(4):
                win = 2 << g
                src_tl = None
                src = ptail_t[:, 2 * g:2 * g + 2, :]
                srcs = [PBUF[2 * g], PBUF[2 * g + 1]]
                sh = 1
                wi = 0
                cur = src
                cur_tls = srcs
                for lvl in range(g + 1):
                    dst = PW[wi % 2]
                    wi += 1
                    P.op("dve", (lambda d=dst.ap, s=cur, sh=sh: lambda e: e.tensor_tensor(d[:, :, sh:L], s[:, :, sh:L], s[:, :, 0:L - sh], ALU.add))(),
                         reads=list(cur_tls), writes=[dst])
                    cur = dst.ap
                    cur_tls = [dst]
                    sh *= 2
                for i in range(2):
                    yp = YPI[2 * g + i]
                    stt(yp, yp.ap[:, 0:n], cur_tls[0], cur[:, i, 15:15 + n], 1.0 / win,
                        PBUF[2 * g + i], PBUF[2 * g + i].ap[:, 15:15 + n], ALU.mult, ALU.subtract)
                    if T == TH + 1:
                        f = FS.next()
                        tt(f, f.ap[:, 0:16], cur_tls[0], cur[:, i, 15:31], PRM, prm[:, P_INVC + 16 * g:P_INVC + 16 * g + 16], ALU.mult)
                        tt(yp, yp.ap[:, 0:16], f, f.ap[:, 0:16], PBUF[2 * g + i], PBUF[2 * g + i].ap[:, 15:31], ALU.subtract)
            for pcix in range(8):
                f = FS.next()
                cp(f, f.ap[:, 0:15], PBUF[pcix], PBUF[pcix].ap[:, n:n + 15])
                cp(PBUF[pcix], PBUF[pcix].ap[:, 0:15], f, f.ap[:, 0:15])
            for g in range(4):
                for mo2 in range(2):
                    ps = pbsel[0].next()
                    for ki in range(2):
                        mm(ps, ps.ap[:, 0:n], POOLW, poolw_t[:, g, ki, mo2 * 128:(mo2 + 1) * 128],
                           YPI[2 * g + ki], YPI[2 * g + ki].ap[:, 0:n], start=(ki == 0), stop=(ki == 1))
                    yo = YPO[2 * g + mo2]
                    act(yo, yo.ap[:, 0:n], ps, ps.ap[:, 0:n], AF.Identity, scale=pc(P_PSC + 2 * g + mo2), rd=[PRM])
            YPOc = [Tl(None)] * 0
            for mo in range(KC):
                ps1 = proj(w_in[88 + mo], HB, c0, n)
                sg = FS.next()
                act(sg, sg.ap[:, 0:n], ps1, ps1.ap[:, 0:n], AF.Sigmoid)
                w = load_w(w_b[mo], 8)
                ps2 = pbsel[0].next()
                for k in range(8):
                    mm(ps2, ps2.ap[:, 0:n], w, w.ap[:, k, :], YPO[k], YPO[k].ap[:, 0:n], start=(k == 0), stop=(k == 7))
                f = FS.next()
                tt(f, f.ap[:, 0:n], ps2, ps2.ap[:, 0:n], sg, sg.ap[:, 0:n], ALU.mult)
                tt(CB[mo], CB[mo].ap[:, c0:c0 + n], f, f.ap[:, 0:n], DB[mo], DB[mo].ap[:, c0:c0 + n], ALU.add)
            ss = PSS.next()
            for mo in range(KC):
                ps = proj(w_mix[mo], CB, c0, n)
                y_chunk_out(ps, mo, c0, n, ss)
            postnorm_residual(ss, c0, n, 1)

        def xattn(T, c0, n):
            rmsnorm_to_bf(XA, c0, n, 2, HB)
            for mo in range(KC):
                ps = proj(w_xq[mo], HB, c0, n)
                act(CB[mo], CB[mo].ap[:, c0:c0 + n], ps, ps.ap[:, 0:n], AF.Copy)
            scl = 512.0 ** -0.5
            for hx in range(4):
                ets = []
                for mt in range(2):
                    ps = pbsel[0].next()
                    for c in range(4):
                        kc = 4 * hx + c
                        mm(ps, ps.ap[:, 0:n], KX, kx_t[:, kc, mt * 128:(mt + 1) * 128], CB[kc], CB[kc].ap[:, c0:c0 + n],
                           start=(c == 0), stop=(c == 3))
                    e_ = ET.next()
                    act(e_, e_.ap[:, 0:n], ps, ps.ap[:, 0:n], AF.Exp, scale=scl)
                    ets.append(e_)
                den = PSS.next()
                for mt in range(2):
                    mm(den, den.ap[:, 0:n], ONESB, onesb_t[:, :], ets[mt], ets[mt].ap[:, 0:n], start=(mt == 0), stop=(mt == 1))
                rd_ = RS.next()
                recip(rd_, rd_.ap[:, 0:n], den, den.ap[:, 0:n])
                for c in range(4):
                    kc = 4 * hx + c
                    ps = pbsel[0].next()
                    for mt in range(2):
                        mm(ps, ps.ap[:, 0:n], VX, vx_t[:, mt, kc * 128:(kc + 1) * 128], ets[mt], ets[mt].ap[:, 0:n],
                           start=(mt == 0), stop=(mt == 1))
                    tt(HB[kc], HB[kc].ap[:, c0:c0 + n], ps, ps.ap[:, 0:n], rd_, rd_.ap[:, 0:n], ALU.mult)
            ss = PSS.next()
            for mo in range(KC):
                ps = proj(w_xo[mo], HB, c0, n)
                y_chunk_out(ps, mo, c0, n, ss)
            postnorm_residual(ss, c0, n, 4)

        def ffn(T, c0, n):
            rmsnorm_to_bf(XA, c0, n, 5, HB)

            def conv3(ps, ch):
                cq = CQ.next()
                a = ACC.next()
                act(cq, cq.ap[:, 2:2 + n], ps, ps.ap[:, 0:n], AF.Copy)
                cp(cq, cq.ap[:, 0:2], FTAIL[ch], FTAIL[ch].ap)
                act(a, a.ap[:, 0:n], ps, ps.ap[:, 0:n], AF.Identity, bias=pc(P_FCB + ch), scale=pc(P_FCW + 3 * ch + 2), rd=[PRM])
                for j in range(2):
                    stt(a, a.ap[:, 0:n], cq, cq.ap[:, j:j + n], pc(P_FCW + 3 * ch + j), a, a.ap[:, 0:n],
                        ALU.mult, ALU.add, rd=[PRM])
                if T == TH:
                    ts(FTAIL[ch], FTAIL[ch].ap, cq, cq.ap[:, n:n + 2], pc(P_FLAG), None, ALU.mult, rd=[PRM])
                else:
                    cp(FTAIL[ch], FTAIL[ch].ap, cq, cq.ap[:, n:n + 2])
                return a

            if T == TH:
                for ch in range(2 * FC):
                    ps = proj(w_up[ch], HB, c0, n)
                    ts(FTAIL[ch], FTAIL[ch].ap, ps, ps.ap[:, n - 2:n], pc(P_FLAG), None, ALU.mult, rd=[PRM])
                return
            for m in range(FC):
                psa = proj(w_up[m], HB, c0, n)
                aa = conv3(psa, m)
                psb = proj(w_up[FC + m], HB, c0, n)
                ab = conv3(psb, FC + m)
                act(aa, aa.ap[:, 0:n], aa, aa.ap[:, 0:n], AF.Silu)
                tt(ACTB[m], ACTB[m].ap[:, 0:n], aa, aa.ap[:, 0:n], ab, ab.ap[:, 0:n], ALU.mult)
            ss = PSS.next()
            for mo in range(KC):
                ps = pbsel[0].next()
                proj(w_down[mo, :, 0:22, :], ACTB, 0, n, nk=22, k0=0, ps=ps, first=True, last=False)
                proj(w_down[mo, :, 22:43, :], ACTB, 0, n, nk=21, k0=22, ps=ps, first=False, last=True)
                y_chunk_out(ps, mo, c0, n, ss)
            postnorm_residual(ss, c0, n, 6)

        def stream():
            pending_B = [None]

            HBS = [HB, CB]

            def prologue(T, par, HBx):
                bg_t = bg_ts[par]; BG = BGs[par]
                for k in range(KC):
                    dma("sp", XA[k], XA[k].ap, None, xT[:, k, T * NT:(T + 1) * NT])
                yield
                rmsnorm_to_bf(XA, 0, NT, 0, HBx)
                yield
                for sub in range(NSUB):
                    p = PSS.next()
                    for k in range(KC):
                        mm(p, p.ap[:, 0:32], HBx[k], HBx[k].ap[:, sub * 128:(sub + 1) * 128], WBA, wba_t[:, k, :],
                           start=(k == 0), stop=(k == KC - 1))
                    act(BG, bg_t[:, 0, sub, :], p, p.ap[:, 0:16], AF.Sigmoid)
                    ts(BG, bg_t[:, 1, sub, :], BG, bg_t[:, 0, sub, :], -1.0, None, ALU.mult)
                    f = FS.next()
                    tt(f, f.ap[:, 0:16], p, p.ap[:, 16:32], PRM, prm[:, P_DTB:P_DTB + 16], ALU.add)
                    act(f, f.ap[:, 0:16], f, f.ap[:, 0:16], AF.Exp)
                    act(f, f.ap[:, 0:16], f, f.ap[:, 0:16], AF.Ln, bias=C(C_ONE, 1), rd=[CST])
                    tt(BG, bg_t[:, 2, sub, :], f, f.ap[:, 0:16], NGA, nga_t[:, :], ALU.mult)
                    gs_stage(sub, par)
                    yield
                if T == 0:
                    dbg_dump("bg", BG, bg_t[:, :, :, :])
                if late_casts:
                    ntl = max(1, TH - 2 - T)
                    k_ = len(late_casts) if T >= TH - 2 else (len(late_casts) + ntl - 1) // ntl
                    for (src_, mo_) in late_casts[:k_]:
                        cast_weights(src_, [mo_], after=BG)
                    del late_casts[:k_]

            def proj_gen_x(hg_, HBx):
                KT = KTS[hg_ % 2]
                KTB = KTBS[hg_ % 2]; VTB = VTBS[hg_ % 2]
                for hh in range(HG):
                    h = hg_ * HG + hh
                    ps = proj(w_in[16 + h], HBx, 0, NT)
                    conv4(ps, 16 + h, NT, 0, KT[hh], KT[hh].ap[:, 0:NT])
                    yield
                    ps = proj(w_in[32 + h], HBx, 0, NT)
                    conv4(ps, 32 + h, NT, 0, VTB[hh], VTB[hh].ap[:, 0:NT])
                    yield
                for hh in range(HG):
                    l2norm_to(KT[hh], KT[hh].ap[:, 0:NT], KTB[hh], KTB[hh].ap[:, 0:NT], NT, 1.0)
                    yield

            def chain2(a, b):
                yield from a
                yield from b

            carry = [None]
            pre_done = [False]

            for T in range(NTILES):
                is_main = T >= TH
                c0 = NT - HALO if T == TH else 0
                n = NT - c0
                par = (TH - T) % 2 if T < TH else 0
                HBx = HBS[par]
                if not pre_done[0]:
                    interleave([prologue(T, par, HBx)])
                    if T == 0:
                        ckpt(3)
                have_pg0 = pre_done[0]
                pre_done[0] = False

                prevB = None
                pend_gn = None

                def gnorm(hg_):
                    for hh in range(HG):
                        h = hg_ * HG + hh
                        o_ap = OR[hh].ap[:, c0:c0 + n]
                        q = SQB.next()
                        tt(q, q.ap[:, 0:n], OR[hh], o_ap, OR[hh], o_ap, ALU.mult)
                        ss = PSS.next()
                        mm(ss, ss.ap[:, 0:n], ONESB, onesb_t[:, :], q, q.ap[:, 0:n])
                        r = rstd_from(ss, n, 1.0 / 128)
                        ps = proj(w_in[48 + h], HB, c0, n)
                        zs = FS.next()
                        act(zs, zs.ap[:, 0:n], ps, ps.ap[:, 0:n], AF.Silu)
                        f = FS.next()
                        stt(f, f.ap[:, 0:n], OR[hh], o_ap, pc(P_GN), r, r.ap[:, 0:n], ALU.mult, ALU.mult, rd=[PRM])
                        tt(CB[h], CB[h].ap[:, c0:c0 + n], f, f.ap[:, 0:n], zs, zs.ap[:, 0:n], ALU.mult)
                        if T == TH + 1 and h == 0:
                            dbg_dump("or0", OR[0], OR[0].ap)

                def proj_gen(hg_):
                    KT = KTS[hg_ % 2]; QT = QTS[hg_ % 2]
                    KTB = KTBS[hg_ % 2]; VTB = VTBS[hg_ % 2]; QTB = QTBS[hg_ % 2]
                    for hh in range(HG):
                        h = hg_ * HG + hh
                        ps = proj(w_in[16 + h], HB, 0, NT)
                        conv4(ps, 16 + h, NT, 0, KT[hh], KT[hh].ap[:, 0:NT])
                        yield
                        ps = proj(w_in[32 + h], HB, 0, NT)
                        conv4(ps, 32 + h, NT, 0, VTB[hh], VTB[hh].ap[:, 0:NT])
                        yield
                        if is_main:
                            if T == TH:
                                memset(QTB[hh], QTB[hh].ap, 0.0)
                            ps = proj(w_in[h], HB, c0, n)
                            conv4(ps, h, n, c0, QT[hh], QT[hh].ap[:, c0:c0 + n])
                            yield
                    for hh in range(HG):
                        l2norm_to(KT[hh], KT[hh].ap[:, 0:NT], KTB[hh], KTB[hh].ap[:, 0:NT], NT, 1.0)
                        yield
                        if is_main:
                            l2norm_to(QT[hh], QT[hh].ap[:, c0:c0 + n], QTB[hh], QTB[hh].ap[:, c0:c0 + n], n, 128.0 ** -0.5)
                            yield

                NG = NH // HG
                if not have_pg0:
                    interleave([proj_gen(0) if is_main else proj_gen_x(0, HBx)])

                if not is_main:
                    prevS = carry[0]
                    carry[0] = None
                    for hg in range(NG):
                        sts_ = [dict() for _ in range(NSUB)]
                        gens = [gdn_prep(hg, sub, False, sts_[sub], par) for sub in range(NSUB)]
                        if prevS is not None:
                            gens.append(prevS)
                        if hg + 1 < NG:
                            gens.append(proj_gen_x(hg + 1, HBx))
                        elif T + 1 < TH:
                            parn = (TH - (T + 1)) % 2
                            gens.append(chain2(prologue(T + 1, parn, HBS[parn]), proj_gen_x(0, HBS[parn])))
                            pre_done[0] = True
                        interleave(gens)
                        prevS = chain2(gdn_state(hg, 0, False, sts_[0], par), gdn_state(hg, 1, False, sts_[1], par))
                    if T + 1 < TH:
                        carry[0] = prevS
                    else:
                        interleave([prevS])
                for hg in (range(NG) if is_main else []):
                    for sub in range(NSUB):
                        full = is_main and (T > TH or sub == NSUB - 1)
                        st_ = dict()
                        gens = [gdn_prep(hg, sub, full, st_)] + (prevB if prevB else [])
                        if sub == 0 and hg + 1 < NG:
                            gens.append(proj_gen(hg + 1))
                        interleave(gens)
                        if pend_gn is not None:
                            gnorm(pend_gn)
                            pend_gn = None
                        prevB = [gdn_state(hg, sub, full, st_)]
                        if sub == NSUB - 1 and is_main:
                            pend_gn = hg
                if prevB:
                    interleave(prevB)
                prevB = None
                if pend_gn is not None:
                    gnorm(pend_gn)
                    pend_gn = None
                if T == NTILES - 1:
                    dbg_dump("s0", S[0], S[0].ap)
                if T == 0:
                    ckpt(7)
                if T == TH - 1:
                    ckpt(8)
                if T == TH:
                    ckpt(9)
                if is_main:
                    if T == TH + 1:
                        for k in range(KC):
                            pass
                    pbsel[0] = PB_MAIN
                    mixer_rest(T, c0, n)
                    if T == TH:
                        ckpt(10)
                    if T == TH + 1:
                        dbg_dump("x1", XA[0], XA[0].ap)
                    xattn(T, c0, n)
                    if T == TH:
                        ckpt(11)
                    if T == TH + 1:
                        dbg_dump("x2", XA[0], XA[0].ap)
                    ffn(T, c0, n)
                    pbsel[0] = PB_STREAM
                    if T > TH:
                        o0 = (T - TH - 1) * NT
                        for k in range(KC):
                            dma("sp", None, outT[:, k, o0:o0 + NT], XA[k], XA[k].ap)
        try:
            ckpt(1)
            xattn_kv()
            ckpt(2)
            stream()
        except _Stop:
            pass
        emit_program(nc, P, es)
    return nc


def _tile_w(W):
    K, M = W.shape
    return np.ascontiguousarray(W.reshape(K // 128, 128, M // 128, 128).transpose(2, 1, 0, 3))


def _fm(v, nch):
    return np.ascontiguousarray(np.asarray(v, np.float32).reshape(nch, 128).T)


def make_consts():
    c = np.zeros((128, NCONST), np.float32)
    i = np.arange(128)
    blk = (i[:, None] // 64) == (i[None, :] // 64)
    for r in range(4):
        c[:, C_ID + 128 * r:C_ID + 128 * r + 128] = np.eye(128)
        c[:, C_UBD + 128 * r:C_UBD + 128 * r + 128] = blk & (i[:, None] <= i[None, :])
        c[:, C_MSTR + 128 * r:C_MSTR + 128 * r + 128] = blk & (i[:, None] > i[None, :])
    c[:, C_ONES:C_ONES + 128] = 1.0
    c[:, C_OBD:C_OBD + 128] = blk
    c[:, C_BSEL] = i < 64
    c[:, C_BSEL + 1] = i >= 64
    c[:, C_EPS] = EPS
    c[:, C_ONE] = 1.0
    return c


def prep_inputs(cfg, inp):
    SEQ, OWN = cfg.SEQ, cfg.OWN
    f = lambda a: np.asarray(a, np.float32)
    w_in_full = f(inp["w_in"])[0]
    cols = np.concatenate([np.arange(0, 8192), np.arange(8224, 13344)])
    shared = {
        "consts": make_consts(),
        "w_in": _tile_w(w_in_full[:, cols]),
        "wba": np.ascontiguousarray(w_in_full[:, 8192:8224].reshape(KC, 128, 32).transpose(1, 0, 2)),
        "poolw": np.ascontiguousarray(f(inp["pool_w"])[0].reshape(4, 2, 128, 256).transpose(2, 0, 1, 3)),
        "w_a": _tile_w(f(inp["w_branch_a"])[0]),
        "w_b": _tile_w(f(inp["w_branch_b"])[0]),
        "w_mix": _tile_w(f(inp["w_mix_out"])[0]),
        "w_xq": _tile_w(f(inp["w_xq"])[0]),
        "w_xkv": _tile_w(f(inp["w_xkv"])[0]),
        "w_xo": _tile_w(f(inp["w_xo"])[0]),
        "w_up": _tile_w(f(inp["w_up"])[0]),
        "w_down": _tile_w(f(inp["w_down"])[0]),
    }
    prm = np.zeros((128, NPRM), np.float32)
    for gi, nm in enumerate(["mix_pre_norm", "mix_post_norm", "xa_pre_norm", "mem_norm", "xa_post_norm",
                             "ffn_pre_norm", "ffn_post_norm"]):
        prm[:, P_GAIN + 16 * gi:P_GAIN + 16 * gi + 16] = _fm(f(inp[nm])[0], 16)
    cq = f(inp["conv_qkv"])[0]
    prm[:, P_CONVQ:P_CONVQ + 192] = cq.reshape(4, 48, 128).transpose(2, 1, 0).reshape(128, 192)
    fw = f(inp["ffn_conv_w"])[0]
    prm[:, P_FCW:P_FCW + 258] = fw.reshape(3, 86, 128).transpose(2, 1, 0).reshape(128, 258)
    prm[:, P_FCB:P_FCB + 86] = _fm(f(inp["ffn_conv_b"])[0], 86)
    prm[:, P_PSC:P_PSC + 8] = _fm(f(inp["pool_scale"])[0], 8)
    prm[:, P_GN] = f(inp["gdn_norm"])[0]
    prm[:, P_ALOG:P_ALOG + 16] = f(inp["a_log"])[0][None, :]
    prm[:, P_DTB:P_DTB + 16] = f(inp["dt_bias"])[0][None, :]
    x = f(inp["x"])
    mem = f(inp["mem"])
    in_maps = []
    for c in range(8):
        b, j = c // 4, c % 4
        end = OWN * (j + 1)
        start = end - SEQ
        xs = np.zeros((SEQ, D), np.float32)
        if start < 0:
            xs[-start:] = x[b, 0:end]
        else:
            xs[:] = x[b, start:end]
        xTc = np.ascontiguousarray(xs.T.reshape(KC, 128, SEQ).transpose(1, 0, 2))
        memTc = np.ascontiguousarray(mem[b].T.reshape(KC, 128, MEM).transpose(1, 0, 2))
        p = prm.copy()
        p[:, P_FLAG] = 0.0 if j == 0 else 1.0
        own_start = OWN * j
        t = own_start + np.arange(16)
        for g, win in enumerate((2, 4, 8, 16)):
            p[:, P_INVC + 16 * g:P_INVC + 16 * g + 16] = (1.0 / np.minimum(t + 1, win))[None, :]
        m = dict(shared)
        m.update({"xT": xTc, "memT": memTc, "prm": p})
        in_maps.append(m)
    return in_maps


_NC_CACHE = {}


def run(cfg, inp, dbg=()):
    key = (cfg.SEQ, cfg.NT, tuple(dbg))
    if key not in _NC_CACHE:
        _NC_CACHE[key] = build(cfg, dbg)
    nc = _NC_CACHE[key]
    in_maps = prep_inputs(cfg, inp)
    res = run_bass_kernel_spmd(nc, in_maps, core_ids=list(range(8)))
    B = 2
    out = np.zeros((B, cfg.SEQ, D), np.float32)
    for c in range(8):
        b, j = c // 4, c % 4
        o = res.results[c]["outT"]
        out[b, cfg.OWN * j:cfg.OWN * (j + 1), :] = o.transpose(2, 1, 0).reshape(cfg.OWN, D)
    return out, res


def kernel(**inputs):
    cfg = Cfg(SEQ=8192, NT=256)
    out, _ = run(cfg, inputs)
    return out
```

```python
import numpy as np
import concourse.bass as bass
import concourse.mybir as mybir
from concourse.bass_utils import run_bass_kernel_spmd
from contextlib import ExitStack

F32 = mybir.dt.float32
BF16 = mybir.dt.bfloat16
AF = mybir.ActivationFunctionType
ALU = mybir.AluOpType

ENGS = ("pe", "act", "dve", "pool", "sp")
NDSEM = 24


class Tl:
    __slots__ = ("ap", "name", "w", "r", "excl")

    def __init__(self, ap, name="", excl=False):
        self.ap = ap
        self.name = name
        self.w = {}
        self.r = {}
        self.excl = excl


def _add(depmap, dep):
    if dep[0] == "c":
        k = ("c", dep[1])
        if k not in depmap or depmap[k][2] < dep[2]:
            depmap[k] = dep
    else:
        depmap[dep] = dep


class Prog:
    def __init__(self, nc):
        self.nc = nc
        self.ops = {e: [] for e in ENGS}
        self.ndma = {e: 0 for e in ENGS}

    def op(self, eng, fn, reads=(), writes=(), dma=False):
        ops = self.ops[eng]
        idx = len(ops)
        deps = {}
        ex = [t for t in reads if t.excl]
        if ex:
            reads = [t for t in reads if not t.excl]
            writes = list(writes) + [t for t in ex if t not in writes]
        for t in reads:
            for d in t.w.values():
                _add(deps, d)
        for t in writes:
            for d in t.w.values():
                _add(deps, d)
            for d in t.r.values():
                _add(deps, d)
        if dma:
            n = self.ndma[eng]
            self.ndma[eng] += 1
            me = ("d", eng, n)
            if n >= NDSEM:
                _add(deps, ("d", eng, n - NDSEM))
        else:
            me = ("c", eng, idx)
        if eng == "pe":
            deps.pop(("c", "pe"), None)
        rec = dict(fn=fn, deps=list(deps.values()), me=me, needed=False)
        ops.append(rec)
        for t in writes:
            if t.r:
                t.w = {}
                t.r = {}
            _add(t.w, me)
        for t in reads:
            _add(t.r, me)
        return rec


def emit_program(nc, prog, es):
    for e in ENGS:
        for rec in prog.ops[e]:
            for d in rec["deps"]:
                if d[0] == "c":
                    prog.ops[d[1]][d[2]]["needed"] = True
    cum = {}
    for e in ENGS:
        c = 0
        arr = []
        for rec in prog.ops[e]:
            if rec["me"][0] == "c" and rec["needed"]:
                c += 1
            arr.append(c)
        cum[e] = arr
    csem = {e: es.enter_context(nc.semaphore("cs_" + e)) for e in ENGS}
    dsem = {}
    for e in ENGS:
        if prog.ndma[e] > 0:
            dsem[e] = [es.enter_context(nc.semaphore("ds_%s_%d" % (e, i)))
                       for i in range(min(NDSEM, prog.ndma[e]))]
    block = es.enter_context(nc.Block())

    def run(ename, engobj):
        waited = {}
        for rec in prog.ops[ename]:
            for d in rec["deps"]:
                if d[0] == "c":
                    sem = csem[d[1]]
                    val = cum[d[1]][d[2]]
                    key = ("c", d[1])
                else:
                    sem = dsem[d[1]][d[2] % NDSEM]
                    val = 16 * (d[2] // NDSEM + 1)
                    key = ("d", d[1], d[2] % NDSEM)
                if waited.get(key, 0) >= val:
                    continue
                waited[key] = val
                engobj.wait_ge(sem, val)
            ins = rec["fn"](engobj)
            me = rec["me"]
            if me[0] == "c":
                if rec["needed"]:
                    ins.then_inc(csem[ename], 1)
            else:
                ins.then_inc(dsem[ename][me[2] % NDSEM], 16)
        if ename in dsem:
            n = prog.ndma[ename]
            for i in range(min(NDSEM, n)):
                last = ((n - 1 - i) // NDSEM) * NDSEM + i
                engobj.wait_ge(dsem[ename][i], 16 * (last // NDSEM + 1))

    @block.tensor
    def _(eng):
        run("pe", eng)

    @block.scalar
    def _(eng):
        run("act", eng)

    @block.vector
    def _(eng):
        run("dve", eng)

    @block.gpsimd
    def _(eng):
        run("pool", eng)

    @block.sync
    def _(eng):
        run("sp", eng)


class FreeList:
    def __init__(self, tiles):
        self.free = list(tiles)
        self.total = len(tiles)
        self.low = len(tiles)

    def get(self):
        if not self.free:
            raise RuntimeError("freelist exhausted")
        t = self.free.pop(0)
        self.low = min(self.low, len(self.free))
        return t

    def put(self, t):
        self.free.append(t)


class Ring:
    def __init__(self, tiles):
        self.t = list(tiles)
        self.i = 0

    def next(self):
        t = self.t[self.i % len(self.t)]
        self.i += 1
        return t


class _Stop(Exception):
    pass


class Cfg:
    stop = 0

    def __init__(self, SEQ=8192, NT=256):
        self.SEQ = SEQ
        self.NT = NT
        self.OWN = SEQ // 4
        self.HALO = 32
        self.NTILES = SEQ // NT
        self.TH = (SEQ - self.OWN) // NT - 1
        self.NSUB = NT // 128


D = 2048
KC = 16
NH = 16
DFF = 5504
FC = 43
MEM = 256
HG = 4
EPS = 1e-6

P_GAIN = 0
P_CONVQ = 112
P_FCW = 304
P_FCB = 562
P_PSC = 648
P_GN = 656
P_ALOG = 657
P_DTB = 673
P_FLAG = 689
P_INVC = 690
NPRM = 754
C_ID = 0
C_UBD = 512
C_MSTR = 1024
C_ONES = 1536
C_OBD = 1664
C_BSEL = 1792
C_EPS = 1794
C_ONE = 1795
NCONST = 1800


def build(cfg, dbg=()):
    SEQ, NT, OWN, HALO, NTILES, TH, NSUB = cfg.SEQ, cfg.NT, cfg.OWN, cfg.HALO, cfg.NTILES, cfg.TH, cfg.NSUB
    nc = bass.Bass("TRN2", target_bir_lowering=False)
    nc.dge_precook = False

    def din(name, shape):
        return nc.dram_tensor(name, shape, F32, kind="ExternalInput").ap()

    xT = din("xT", [128, KC, SEQ])
    memT = din("memT", [128, KC, MEM])
    consts_d = din("consts", [128, NCONST])
    prm_d = din("prm", [128, NPRM])
    wba_d = din("wba", [128, KC, 32])
    poolw_d = din("poolw", [128, 4, 2, 256])
    class WRef:
        def __init__(self, src, mo, k0, k1):
            self.src, self.mo, self.k0, self.k1 = src, mo, k0, k1

    class WSrc:
        def __init__(self, name, nch, nk):
            self.name, self.nch, self.nk = name, nch, nk
            self.f32 = din(name, [nch, 128, nk, 128])
            self.b16 = nc.dram_tensor(name + "_b16", [nch, 128, nk, 128], BF16).ap()
            self.tl = [Tl(None) for _ in range(nch)]

        def __getitem__(self, idx):
            if isinstance(idx, tuple):
                mo, ks = idx[0], idx[2]
                return WRef(self, mo, ks.start, ks.stop)
            return WRef(self, idx, 0, self.nk)

    w_in = WSrc("w_in", 104, KC)
    w_a = WSrc("w_a", 16, KC)
    w_b = WSrc("w_b", 16, 8)
    w_mix = WSrc("w_mix", 16, KC)
    w_xq = WSrc("w_xq", 16, KC)
    w_xkv = WSrc("w_xkv", 32, KC)
    w_xo = WSrc("w_xo", 16, KC)
    w_up = WSrc("w_up", 86, KC)
    w_down = WSrc("w_down", 16, FC)
    outT = nc.dram_tensor("outT", [128, KC, OWN], F32, kind="ExternalOutput").ap()
    dbg_out = {}
    for name, shape in dbg:
        dbg_out[name] = nc.dram_tensor("dbg_" + name, list(shape), F32, kind="ExternalOutput").ap()

    es = ExitStack()
    with es:
        def sb(name, shape, dt=F32):
            return es.enter_context(nc.sbuf_tensor(name, list(shape), dt))

        def pst(name, shape, dt=F32):
            return es.enter_context(nc.psum_tensor(name, list(shape), dt))

        P = Prog(nc)

        def ckpt(i):
            if cfg.stop == i:
                raise _Stop()

        xa_t = sb("xa", [128, KC, NT]); XA = [Tl(xa_t[:, k, :]) for k in range(KC)]
        hb_t = sb("hb", [128, KC, NT], BF16); HB = [Tl(hb_t[:, k, :]) for k in range(KC)]
        cb_t = sb("cb", [128, KC, NT], BF16); CB = [Tl(cb_t[:, k, :]) for k in range(KC)]
        db_t = sb("db", [128, KC, NT]); DB = [Tl(db_t[:, k, :]) for k in range(KC)]
        act_t = sb("actb", [128, FC, NT], BF16); ACTB = [Tl(act_t[:, k, :]) for k in range(FC)]
        s_t = sb("state", [128, NH, 128]); S = [Tl(s_t[:, h, :]) for h in range(NH)]
        NW = 3
        w_ts = [sb("wt%d" % i, [128, 22, 128], BF16) for i in range(NW)]
        WR = Ring([Tl(t) for t in w_ts])
        kx_t = sb("kx", [128, KC, MEM], BF16); KX = Tl(kx_t)
        vx_t = sb("vx", [128, 2, D], BF16); VX = Tl(vx_t)
        cst = sb("cst", [128, NCONST]); CST = Tl(cst)
        prm = sb("prm_s", [128, NPRM]); PRM = Tl(prm)
        onesb_t = sb("onesb", [128, 128], BF16); ONESB = Tl(onesb_t)
        wba_t = sb("wba_s", [128, KC, 32], BF16); WBA = Tl(wba_t)
        poolw_t = sb("poolw_s", [128, 4, 2, 256], BF16); POOLW = Tl(poolw_t)
        nga_t = sb("nga", [128, NH]); NGA = Tl(nga_t)
        qtail_t = sb("qtail", [128, 48, 3]); QTAIL = [Tl(qtail_t[:, c, :]) for c in range(48)]
        ptail_t = sb("pbuf", [128, 8, 15 + NT]); PBUF = [Tl(ptail_t[:, c, :]) for c in range(8)]
        ftail_t = sb("ftail", [128, 86, 2]); FTAIL = [Tl(ftail_t[:, c, :]) for c in range(86)]
        bg_ts = [sb("bg%d" % i, [128, 3, NSUB, NH]) for i in range(2)]; BGs = [Tl(t) for t in bg_ts]
        kt_t = sb("kt", [128, HG, NT]); KT0 = [Tl(kt_t[:, i, :]) for i in range(HG)]
        qt_t = sb("qt", [128, HG, NT]); QT0 = [Tl(qt_t[:, i, :]) for i in range(HG)]
        ktb_t = sb("ktb", [128, HG, NT], BF16); KTB0 = [Tl(ktb_t[:, i, :]) for i in range(HG)]
        vtb_t = sb("vtb", [128, HG, NT], BF16); VTB0 = [Tl(vtb_t[:, i, :]) for i in range(HG)]
        qtb_t = sb("qtb", [128, HG, NT], BF16); QTB0 = [Tl(qtb_t[:, i, :]) for i in range(HG)]
        act32 = act_t.bitcast(F32)
        cpr = NT // 128

        def alias32(j):
            return Tl(act32[:, j * cpr:(j + 1) * cpr, :].rearrange("p a b -> p (a b)"))
        KT1 = [alias32(i) for i in range(HG)]
        QT1 = [alias32(HG + i) for i in range(HG)]
        r0 = 2 * HG * cpr
        KTB1 = [Tl(act_t[:, r0 + i, :]) for i in range(HG)]
        VTB1 = [Tl(act_t[:, r0 + HG + i, :]) for i in range(HG)]
        QTB1 = [Tl(act_t[:, r0 + 2 * HG + i, :]) for i in range(HG)]
        r1 = r0 + 3 * HG
        assert r1 + 8 <= FC
        KTS = [KT0, KT1]; QTS = [QT0, QT1]
        KTBS = [KTB0, KTB1]; VTBS = [VTB0, VTB1]; QTBS = [QTB0, QTB1]
        or_t = sb("or", [128, HG, NT]); OR = [Tl(or_t[:, i, :]) for i in range(HG)]
        NTMP = 7
        tmp_t = sb("gtmp", [128, NTMP, 512])
        TMP = FreeList([Tl(tmp_t[:, i, :]) for i in range(NTMP)])
        NTMPB = 15
        tmpb_t = sb("gtmpb", [128, NTMPB, 512], BF16)
        TMPB = FreeList([Tl(tmpb_t[:, i, :]) for i in range(NTMPB)])
        identb_t = sb("identb", [128, 128], BF16); IDENTB = Tl(identb_t)
        db16 = db_t.bitcast(BF16)
        for j_ in range(8):
            TMPB.put(Tl(db16[:, 2 * j_:2 * j_ + 2, :].rearrange("p a b -> p (a b)")[:, 0:512]))
        gs_ts = [sb("gs%d" % i, [128, NSUB, 8, NH]) for i in range(2)]; GSs = [Tl(t) for t in gs_ts]
        cq_t = sb("cq", [128, 2, 3 + NT]); CQ = Ring([Tl(cq_t[:, i, :]) for i in range(2)])
        acc_t = sb("acc", [128, 3, NT]); ACC = Ring([Tl(acc_t[:, i, :]) for i in range(3)])
        sqb_t = sb("sqb", [128, 3, NT], BF16); SQB = Ring([Tl(sqb_t[:, i, :]) for i in range(3)])
        rs_t = sb("rs", [128, 2, NT]); RS = Ring([Tl(rs_t[:, i, :]) for i in range(2)])
        f32s_t = sb("f32s", [128, 3, NT]); FS = Ring([Tl(f32s_t[:, i, :]) for i in range(3)])
        et_t = sb("et", [128, 4, NT], BF16); ET = Ring([Tl(et_t[:, i, :]) for i in range(4)])
        YPI = [Tl(act_t[:, r1 + i, :]) for i in range(8)]
        ypo_t = sb("ypo", [128, 8, NT], BF16); YPO = [Tl(ypo_t[:, i, :]) for i in range(8)]
        pw_t = sb("pw", [128, 2, 2, 15 + NT]); PW = [Tl(pw_t[:, i, :, :]) for i in range(2)]

        pbig = [pst("pbig%d" % i, [128, 512]) for i in range(2)]
        PB = Ring([Tl(pbig[i][:, :], excl=True) for i in range(2)])
        pss = pst("pss", [128, 512])
        PSS = Ring([Tl(pss[:, :], excl=True)])
        psm = [pst("psm%d" % i, [128, 512]) for i in range(5)]
        PSM = FreeList([Tl(psm[i][:, :], excl=True) for i in range(5)])
        PB_STREAM = PB
        PB_MAIN = Ring(PB.t + PSM.free[0:4])
        pbsel = [PB_STREAM]

        def C(c0, n=128):
            return cst[:, c0:c0 + n]

        def pc(c):
            return prm[:, c:c + 1]

        def mm(out_tl, out_ap, a_tl, a_ap, b_tl, b_ap, start=True, stop=True):
            P.op("pe", lambda e: e.matmul(out_ap, a_ap, b_ap, start=start, stop=stop),
                 reads=[a_tl, b_tl], writes=[out_tl])

        def act(out_tl, out_ap, in_tl, in_ap, func, bias=None, scale=None, rd=()):
            kw = {}
            if bias is not None:
                kw["bias"] = bias
            if scale is not None:
                kw["scale"] = scale
            P.op("act", lambda e: e.activation(out=out_ap, in_=in_ap, func=func, **kw),
                 reads=[in_tl] + list(rd), writes=[out_tl])

        def ts(out_tl, out_ap, in_tl, in_ap, s1, s2, op0, op1=None, rd=(), eng="dve"):
            if op1 is None:
                P.op(eng, lambda e: e.tensor_scalar(out_ap, in_ap, s1, None, op0),
                     reads=[in_tl] + list(rd), writes=[out_tl])
            else:
                P.op(eng, lambda e: e.tensor_scalar(out_ap, in_ap, s1, s2, op0, op1),
                     reads=[in_tl] + list(rd), writes=[out_tl])

        def stt(out_tl, out_ap, in0_tl, in0_ap, scalar, in1_tl, in1_ap, op0, op1, rd=()):
            P.op("dve", lambda e: e.scalar_tensor_tensor(out_ap, in0_ap, scalar, in1_ap, op0, op1),
                 reads=[in0_tl, in1_tl] + list(rd), writes=[out_tl])

        def tt(out_tl, out_ap, a_tl, a_ap, b_tl, b_ap, op, eng="dve"):
            P.op(eng, lambda e: e.tensor_tensor(out_ap, a_ap, b_ap, op),
                 reads=[a_tl, b_tl], writes=[out_tl])

        def cp(out_tl, out_ap, in_tl, in_ap, eng="dve"):
            P.op(eng, lambda e: e.tensor_copy(out_ap, in_ap), reads=[in_tl], writes=[out_tl])

        def recip(out_tl, out_ap, in_tl, in_ap):
            P.op("dve", lambda e: e.reciprocal(out_ap, in_ap), reads=[in_tl], writes=[out_tl])

        def memset(tl, ap, val, eng="dve"):
            P.op(eng, lambda e: e.memset(ap, val), writes=[tl])

        def dma(eng, out_tl, out_ap, in_tl, in_ap):
            P.op(eng, lambda e: e.dma_start(out=out_ap, in_=in_ap),
                 reads=[in_tl] if in_tl is not None else [],
                 writes=[out_tl] if out_tl is not None else [], dma=True)

        def dbg_dump(name, tl, ap):
            if name in dbg_out:
                dma("sp", None, dbg_out[name], tl, ap)

        def load_w(ref, nk):
            w = WR.next()
            assert ref.k1 - ref.k0 == nk
            dma("sp", w, w.ap[:, 0:nk, :], ref.src.tl[ref.mo], ref.src.b16[ref.mo, :, ref.k0:ref.k1, :])
            return w

        def cast_weights(src, chunks, after=None):
            for mo in chunks:
                dma("pool", src.tl[mo], src.b16[mo], after, src.f32[mo])

        late_casts = []

        def proj(wd, in_list, c0, n, nk=KC, k0=0, ps=None, first=True, last=True):
            w = load_w(wd, nk)
            if ps is None:
                ps = pbsel[0].next()
            for k in range(nk):
                mm(ps, ps.ap[:, 0:n], w, w.ap[:, k, :], in_list[k0 + k], in_list[k0 + k].ap[:, c0:c0 + n],
                   start=(first and k == 0), stop=(last and k == nk - 1))
            return ps

        def rstd_from(ss, n, scale, out=None):
            r = RS.next() if out is None else out
            act(r, r.ap[:, 0:n], ss, ss.ap[:, 0:n], AF.Ln, bias=C(C_EPS, 1), scale=scale, rd=[CST])
            act(r, r.ap[:, 0:n], r, r.ap[:, 0:n], AF.Exp, scale=-0.5)
            return r

        def rmsnorm_to_bf(src, c0, n, gi, dst):
            ss = PSS.next()
            for k in range(KC):
                q = SQB.next()
                act(q, q.ap[:, 0:n], src[k], src[k].ap[:, c0:c0 + n], AF.Square)
                mm(ss, ss.ap[:, 0:n], ONESB, onesb_t[:, :], q, q.ap[:, 0:n], start=(k == 0), stop=(k == KC - 1))
            r = rstd_from(ss, n, 1.0 / D)
            for k in range(KC):
                stt(dst[k], dst[k].ap[:, c0:c0 + n], src[k], src[k].ap[:, c0:c0 + n], pc(P_GAIN + 16 * gi + k),
                    r, r.ap[:, 0:n], ALU.mult, ALU.mult, rd=[PRM])

        def postnorm_residual(ss, c0, n, gi):
            r = rstd_from(ss, n, 1.0 / D)
            for k in range(KC):
                f = FS.next()
                tt(f, f.ap[:, 0:n], DB[k], DB[k].ap[:, c0:c0 + n], r, r.ap[:, 0:n], ALU.mult)
                stt(XA[k], XA[k].ap[:, c0:c0 + n], f, f.ap[:, 0:n], pc(P_GAIN + 16 * gi + k),
                    XA[k], XA[k].ap[:, c0:c0 + n], ALU.mult, ALU.add, rd=[PRM])

        def y_chunk_out(ps, mo, c0, n, ss):
            act(DB[mo], DB[mo].ap[:, c0:c0 + n], ps, ps.ap[:, 0:n], AF.Copy)
            q = SQB.next()
            act(q, q.ap[:, 0:n], ps, ps.ap[:, 0:n], AF.Square)
            mm(ss, ss.ap[:, 0:n], ONESB, onesb_t[:, :], q, q.ap[:, 0:n], start=(mo == 0), stop=(mo == KC - 1))

        def body():
            pass

        dma("sp", CST, cst[:, :], None, consts_d)
        dma("sp", PRM, prm[:, :], None, prm_d)
        dma("pool", WBA, wba_t[:, :, :], None, wba_d)
        dma("pool", POOLW, poolw_t[:, :, :, :], None, poolw_d)
        cast_weights(w_xkv, range(32))
        cast_weights(w_in, range(16, 48))
        for mo_ in list(range(0, 16)) + list(range(48, 104)):
            late_casts.append((w_in, mo_))
        for src_ in (w_a, w_b, w_mix, w_xq, w_xo, w_up, w_down):
            for mo_ in range(src_.nch):
                late_casts.append((src_, mo_))
        cp(ONESB, onesb_t[:, :], CST, C(C_ONES))
        cp(IDENTB, identb_t[:, :], CST, C(C_ID))
        for h in range(NH):
            memset(S[h], S[h].ap, 0.0)
        for c in range(48):
            memset(QTAIL[c], QTAIL[c].ap, 0.0)
        for c in range(8):
            memset(PBUF[c], PBUF[c].ap, 0.0)
        for c in range(86):
            memset(FTAIL[c], FTAIL[c].ap, 0.0)
        act(NGA, nga_t[:, :], PRM, prm[:, P_ALOG:P_ALOG + 16], AF.Exp)
        ts(NGA, nga_t[:, :], NGA, nga_t[:, :], -1.0, None, ALU.mult)

        def xattn_kv():
            assert NT == MEM
            MXl = DB
            MTl = HB
            for k in range(KC):
                dma("sp", MXl[k], MXl[k].ap, None, memT[:, k, :])
            ckpt(21)
            rmsnorm_to_bf(MXl, 0, MEM, 3, MTl)
            ckpt(22)
            for mo in range(KC):
                ps = proj(w_xkv[mo], MTl, 0, MEM)
                act(KX, kx_t[:, mo, :], ps, ps.ap[:, 0:MEM], AF.Copy)
                ckpt(100 + mo)
            ckpt(24)
            for mo in range(KC):
                ps = proj(w_xkv[KC + mo], MTl, 0, MEM)
                f = FS.next()
                act(f, f.ap[:, 0:MEM], ps, ps.ap[:, 0:MEM], AF.Copy)
                for mt in range(2):
                    p2 = PSM.get()
                    mm(p2, p2.ap[:, 0:128], f, f.ap[:, mt * 128:(mt + 1) * 128], CST, C(C_ID))
                    cp(VX, vx_t[:, mt, mo * 128:(mo + 1) * 128], p2, p2.ap[:, 0:128])
                    PSM.put(p2)

        def conv4(ps, ch, n, c0, out_tl, out_ap_silu):
            cq = CQ.next()
            a = ACC.next()
            act(cq, cq.ap[:, 3:3 + n], ps, ps.ap[:, 0:n], AF.Copy)
            cp(cq, cq.ap[:, 0:3], QTAIL[ch], QTAIL[ch].ap)
            act(a, a.ap[:, 0:n], ps, ps.ap[:, 0:n], AF.Identity, scale=pc(P_CONVQ + 4 * ch + 3), rd=[PRM])
            for j in range(3):
                stt(a, a.ap[:, 0:n], cq, cq.ap[:, j:j + n], pc(P_CONVQ + 4 * ch + j), a, a.ap[:, 0:n],
                    ALU.mult, ALU.add, rd=[PRM])
            cp(QTAIL[ch], QTAIL[ch].ap, cq, cq.ap[:, n:n + 3])
            act(out_tl, out_ap_silu, a, a.ap[:, 0:n], AF.Silu)

        def l2norm_to(tl, ap, otl, oap, n, mul):
            q = SQB.next()
            tt(q, q.ap[:, 0:n], tl, ap, tl, ap, ALU.mult)
            ss = PSS.next()
            mm(ss, ss.ap[:, 0:n], ONESB, onesb_t[:, :], q, q.ap[:, 0:n])
            r = rstd_from(ss, n, 1.0)
            if mul == 1.0:
                tt(otl, oap, tl, ap, r, r.ap[:, 0:n], ALU.mult)
            else:
                stt(otl, oap, tl, ap, mul, r, r.ap[:, 0:n], ALU.mult, ALU.mult)

        def Q4(t, i):
            return t.ap[:, i * 128:(i + 1) * 128]

        def gs_stage(sub, par):
            bg_t = bg_ts[par]; BG = BGs[par]; gs_t = gs_ts[par]; GS = GSs[par]
            g = bg_t[:, 2, sub, :]
            f = FS.next()
            for c in range(2):
                ts(f, f.ap[:, 16 * c:16 * c + 16], BG, g, C(C_BSEL + c, 1), None, ALU.mult, rd=[CST])
            bk = PSS.next()
            mm(bk, bk.ap[:, 0:16], CST, C(C_UBD), BG, g)
            mm(bk, bk.ap[:, 16:32], CST, C(C_OBD), BG, g)
            mm(bk, bk.ap[:, 32:64], CST, C(C_ONES), f, f.ap[:, 0:32])
            for r_ in range(4):
                cp(GS, gs_t[:, sub, r_, :], bk, bk.ap[:, 16 * r_:16 * r_ + 16])
            f2 = FS.next()
            act(f2, f2.ap[:, 0:16], GS, gs_t[:, sub, 0, :], AF.Exp)
            tt(GS, gs_t[:, sub, 4, :], f2, f2.ap[:, 0:16], BG, bg_t[:, 0, sub, :], ALU.mult)
            tt(f2, f2.ap[:, 16:32], GS, gs_t[:, sub, 1, :], GS, gs_t[:, sub, 0, :], ALU.subtract)
            act(GS, gs_t[:, sub, 5, :], f2, f2.ap[:, 16:32], AF.Exp)
            for r_ in range(2):
                act(GS, gs_t[:, sub, 6 + r_, :], GS, gs_t[:, sub, 2 + r_, :], AF.Exp)

        def gdn_prep(hg, sub, full, st, par=0):
            bg_t = bg_ts[par]; BG = BGs[par]; gs_t = gs_ts[par]; GS = GSs[par]
            cs = slice(sub * 128, sub * 128 + 128)
            heads = [hg * HG + i for i in range(HG)]
            KT = KTBS[hg % 2]; VT = VTBS[hg % 2]; QT = QTBS[hg % 2]

            def gcol(row, h):
                return gs_t[:, sub, row, h:h + 1]
            gd = TMP.get()
            for i, h in enumerate(heads):
                ts(gd, Q4(gd, i), CST, C(C_ID), gcol(0, h), -1.0, ALU.mult, ALU.mult, rd=[GS])
            bk = PSM.get()
            for i in range(HG):
                mm(bk, Q4(bk, i), CST, C(C_ONES), gd, Q4(gd, i))
            yield
            Dm = TMP.get()
            for i, h in enumerate(heads):
                ts(Dm, Q4(Dm, i), bk, Q4(bk, i), gcol(0, h), 0.0, ALU.add, ALU.min, rd=[GS])
            if full:
                DT = TMP.get(); eG = TMP.get()
                for i, h in enumerate(heads):
                    ts(DT, Q4(DT, i), bk, Q4(bk, i), -1.0, gcol(0, h), ALU.mult, ALU.subtract, rd=[GS])
                ts(DT, DT.ap, DT, DT.ap, 0.0, None, ALU.min)
                act(eG, eG.ap, bk, bk.ap, AF.Exp, scale=-1.0)
            PSM.put(bk)
            TMP.put(gd)
            yield
            act(Dm, Dm.ap, Dm, Dm.ap, AF.Exp)
            if full:
                act(DT, DT.ap, DT, DT.ap, AF.Exp)
            yield
            tt(Dm, Dm.ap, Dm, Dm.ap, CST, C(C_MSTR, 512), ALU.mult)
            if full:
                tt(DT, DT.ap, DT, DT.ap, CST, C(C_UBD, 512), ALU.mult)
            bk = PSM.get()
            for i in range(HG):
                mm(bk, Q4(bk, i), KT[i], KT[i].ap[:, cs], KT[i], KT[i].ap[:, cs])
            yield
            N0 = TMPB.get()
            for i, h in enumerate(heads):
                stt(N0, Q4(N0, i), bk, Q4(bk, i), bg_t[:, 1, sub, h:h + 1], Dm, Q4(Dm, i), ALU.mult, ALU.mult, rd=[BG])
            PSM.put(bk)
            TMP.put(Dm)
            yield
            bk = PSM.get()
            for i in range(HG):
                mm(bk, Q4(bk, i), N0, Q4(N0, i), IDENTB, identb_t[:, :])
            if full:
                bk2 = PSM.get()
                for i in range(HG):
                    mm(bk2, Q4(bk2, i), KT[i], KT[i].ap[:, cs], QT[i], QT[i].ap[:, cs])
            yield
            N0T = TMPB.get(); TT = TMPB.get()
            act(N0T, N0T.ap, bk, bk.ap, AF.Copy)
            tt(TT, TT.ap, bk, bk.ap, CST, C(C_ID, 512), ALU.add)
            PSM.put(bk)
            if full:
                AT = TMPB.get(); Qd = TMPB.get()
                tt(AT, AT.ap, bk2, bk2.ap, DT, DT.ap, ALU.mult)
                PSM.put(bk2)
                for i in range(HG):
                    tt(Qd, Q4(Qd, i), QT[i], QT[i].ap[:, cs], eG, Q4(eG, i), ALU.mult)
                TMP.put(DT); TMP.put(eG)
                st["AT"] = AT; st["Qd"] = Qd
            yield
            Pk, PTk = N0, N0T
            for k in range(5):
                b1 = PSM.get()
                for i in range(HG):
                    mm(b1, Q4(b1, i), PTk, Q4(PTk, i), Pk, Q4(Pk, i))
                if k < 4:
                    b2 = PSM.get()
                    for i in range(HG):
                        mm(b2, Q4(b2, i), Pk, Q4(Pk, i), PTk, Q4(PTk, i))
                yield
                Pn = TMPB.get()
                act(Pn, Pn.ap, b1, b1.ap, AF.Copy)
                PSM.put(b1)
                if k < 4:
                    PTn = TMPB.get()
                    cp(PTn, PTn.ap, b2, b2.ap)
                    PSM.put(b2)
                yield
                b3 = PSM.get()
                for i in range(HG):
                    mm(b3, Q4(b3, i), Pn, Q4(Pn, i), TT, Q4(TT, i))
                yield
                tt(TT, TT.ap, TT, TT.ap, b3, b3.ap, ALU.add)
                PSM.put(b3)
                TMPB.put(Pk); TMPB.put(PTk)
                Pk = Pn
                PTk = PTn if k < 4 else None
                yield
            TMPB.put(Pk)
            bK = PSM.get(); bV = PSM.get()
            for i in range(HG):
                mm(bK, Q4(bK, i), KT[i], KT[i].ap[:, cs], IDENTB, identb_t[:, :])
                mm(bV, Q4(bV, i), VT[i], VT[i].ap[:, cs], IDENTB, identb_t[:, :])
            yield
            Rv = TMPB.get(); Rk = TMPB.get(); Kd = TMPB.get()
            for i, h in enumerate(heads):
                ts(Rv, Q4(Rv, i), bV, Q4(bV, i), bg_t[:, 0, sub, h:h + 1], None, ALU.mult, rd=[BG])
                ts(Rk, Q4(Rk, i), bK, Q4(bK, i), gcol(4, h), None, ALU.mult, rd=[GS])
                act(Kd, Q4(Kd, i), bK, Q4(bK, i), AF.Identity, scale=gcol(5, h), rd=[GS])
            PSM.put(bK); PSM.put(bV)
            yield
            bU = PSM.get(); bW = PSM.get()
            for i in range(HG):
                mm(bU, Q4(bU, i), TT, Q4(TT, i), Rv, Q4(Rv, i))
                mm(bW, Q4(bW, i), Rk, Q4(Rk, i), TT, Q4(TT, i))
            yield
            TMPB.put(Rv); TMPB.put(Rk); TMPB.put(TT)
            Uv = TMP.get(); WkT = TMPB.get()
            act(Uv, Uv.ap, bU, bU.ap, AF.Copy)
            cp(WkT, WkT.ap, bW, bW.ap)
            PSM.put(bU); PSM.put(bW)
            st["Uv"] = Uv; st["WkT"] = WkT; st["Kd"] = Kd
            yield

        def gdn_state(hg, sub, full, st, par=0):
            bg_t = bg_ts[par]; BG = BGs[par]; gs_t = gs_ts[par]; GS = GSs[par]
            heads = [hg * HG + i for i in range(HG)]
            Uv = st["Uv"]; WkT = st["WkT"]; Kd = st["Kd"]
            u = TMPB.get(); Sb = TMPB.get()
            Sg = [S[h] for h in heads]
            sg_ap = s_t[:, hg * HG:(hg + 1) * HG, :].rearrange("p a b -> p (a b)")
            P.op("act", lambda e: e.activation(out=Sb.ap, in_=sg_ap, func=AF.Copy), reads=Sg, writes=[Sb])
            yield
            for c in range(2):
                r = slice(64 * c, 64 * c + 64)
                bk = PSM.get()
                for i, h in enumerate(heads):
                    if c == 0:
                        mm(bk, bk.ap[0:64, i * 128:(i + 1) * 128], WkT, WkT.ap[:, i * 128:i * 128 + 64], Sb, Q4(Sb, i))
                    else:
                        mm(bk, Q4(bk, i), WkT, Q4(WkT, i), Sb, Q4(Sb, i))
                yield
                tt(u, u.ap[r, :], Uv, Uv.ap[r, :], bk, bk.ap[r, :], ALU.subtract)
                PSM.put(bk)
                yield
                bk = PSM.get()
                for i, h in enumerate(heads):
                    mm(bk, Q4(bk, i), Kd, Kd.ap[r, i * 128:(i + 1) * 128], u, u.ap[r, i * 128:(i + 1) * 128])
                if full:
                    bo = PSM.get()
                    Qd = st["Qd"]; AT = st["AT"]
                    for i, h in enumerate(heads):
                        oq = bo.ap[:, i * 128:i * 128 + 64]
                        mm(bo, oq, Sb, Q4(Sb, i), Qd, Qd.ap[:, i * 128 + 64 * c:i * 128 + 64 * c + 64], start=True, stop=False)
                        mm(bo, oq, u, u.ap[r, i * 128:(i + 1) * 128], AT, AT.ap[r, i * 128 + 64 * c:i * 128 + 64 * c + 64],
                           start=False, stop=True)
                yield
                for i, h in enumerate(heads):
                    stt(S[h], S[h].ap, S[h], S[h].ap, gs_t[:, sub, 6 + c, h:h + 1], bk, Q4(bk, i), ALU.mult, ALU.add, rd=[GS])
                PSM.put(bk)
                if full:
                    o0 = sub * 128 + 64 * c
                    for i in range(HG):
                        act(OR[i], OR[i].ap[:, o0:o0 + 64], bo, bo.ap[:, i * 128:i * 128 + 64], AF.Copy)
                    PSM.put(bo)
                yield
                if c == 0:
                    P.op("act", lambda e: e.activation(out=Sb.ap, in_=sg_ap, func=AF.Copy), reads=Sg, writes=[Sb])
                    yield
            TMPB.put(u); TMPB.put(Sb); TMP.put(Uv); TMPB.put(WkT); TMPB.put(Kd)
            if full:
                TMPB.put(st["AT"]); TMPB.put(st["Qd"])

        def interleave(gens):
            gens = list(gens)
            while gens:
                nxt = []
                for g in gens:
                    try:
                        next(g)
                        nxt.append(g)
                    except StopIteration:
                        pass
                gens = nxt

        def mixer_rest(T, c0, n):
            for mo in range(KC):
                ps1 = proj(w_in[72 + mo], HB, c0, n)
                sg = FS.next()
                act(sg, sg.ap[:, 0:n], ps1, ps1.ap[:, 0:n], AF.Sigmoid)
                ps2 = proj(w_a[mo], CB, c0, n)
                tt(DB[mo], DB[mo].ap[:, c0:c0 + n], ps2, ps2.ap[:, 0:n], sg, sg.ap[:, 0:n], ALU.mult)
            for pcix in range(8):
                ps = proj(w_in[64 + pcix], HB, c0, n)
                act(PBUF[pcix], PBUF[pcix].ap[:, 15:15 + n], ps, ps.ap[:, 0:n], AF.Copy)
            L = 15 + n
            for g in range(4):
                win = 2 << g
                src_tl = None
                src = ptail_t[:, 2 * g:2 * g + 2, :]
                srcs = [PBUF[2 * g], PBUF[2 * g + 1]]
                sh = 1
                wi = 0
                cur = src
                cur_tls = srcs
                for lvl in range(g + 1):
                    dst = PW[wi % 2]
                    wi += 1
                    P.op("dve", (lambda d=dst.ap, s=cur, sh=sh: lambda e: e.tensor_tensor(d[:, :, sh:L], s[:, :, sh:L], s[:, :, 0:L - sh], ALU.add))(),
                         reads=list(cur_tls), writes=[dst])
                    cur = dst.ap
                    cur_tls = [dst]
                    sh *= 2
                for i in range(2):
                    yp = YPI[2 * g + i]
                    stt(yp, yp.ap[:, 0:n], cur_tls[0], cur[:, i, 15:15 + n], 1.0 / win,
                        PBUF[2 * g + i], PBUF[2 * g + i].ap[:, 15:15 + n], ALU.mult, ALU.subtract)
                    if T == TH + 1:
                        f = FS.next()
                        tt(f, f.ap[:, 0:16], cur_tls[0], cur[:, i, 15:31], PRM, prm[:, P_INVC + 16 * g:P_INVC + 16 * g + 16], ALU.mult)
                        tt(yp, yp.ap[:, 0:16], f, f.ap[:, 0:16], PBUF[2 * g + i], PBUF[2 * g + i].ap[:, 15:31], ALU.subtract)
            for pcix in range(8):
                f = FS.next()
                cp(f, f.ap[:, 0:15], PBUF[pcix], PBUF[pcix].ap[:, n:n + 15])
                cp(PBUF[pcix], PBUF[pcix].ap[:, 0:15], f, f.ap[:, 0:15])
            for g in range(4):
                for mo2 in range(2):
                    ps = pbsel[0].next()
                    for ki in range(2):
                        mm(ps, ps.ap[:, 0:n], POOLW, poolw_t[:, g, ki, mo2 * 128:(mo2 + 1) * 128],
                           YPI[2 * g + ki], YPI[2 * g + ki].ap[:, 0:n], start=(ki == 0), stop=(ki == 1))
                    yo = YPO[2 * g + mo2]
                    act(yo, yo.ap[:, 0:n], ps, ps.ap[:, 0:n], AF.Identity, scale=pc(P_PSC + 2 * g + mo2), rd=[PRM])
            YPOc = [Tl(None)] * 0
            for mo in range(KC):
                ps1 = proj(w_in[88 + mo], HB, c0, n)
                sg = FS.next()
                act(sg, sg.ap[:, 0:n], ps1, ps1.ap[:, 0:n], AF.Sigmoid)
                w = load_w(w_b[mo], 8)
                ps2 = pbsel[0].next()
                for k in range(8):
                    mm(ps2, ps2.ap[:, 0:n], w, w.ap[:, k, :], YPO[k], YPO[k].ap[:, 0:n], start=(k == 0), stop=(k == 7))
                f = FS.next()
                tt(f, f.ap[:, 0:n], ps2, ps2.ap[:, 0:n], sg, sg.ap[:, 0:n], ALU.mult)
                tt(CB[mo], CB[mo].ap[:, c0:c0 + n], f, f.ap[:, 0:n], DB[mo], DB[mo].ap[:, c0:c0 + n], ALU.add)
            ss = PSS.next()
            for mo in range(KC):
                ps = proj(w_mix[mo], CB, c0, n)
                y_chunk_out(ps, mo, c0, n, ss)
            postnorm_residual(ss, c0, n, 1)

        def xattn(T, c0, n):
            rmsnorm_to_bf(XA, c0, n, 2, HB)
            for mo in range(KC):
                ps = proj(w_xq[mo], HB, c0, n)
                act(CB[mo], CB[mo].ap[:, c0:c0 + n], ps, ps.ap[:, 0:n], AF.Copy)
            scl = 512.0 ** -0.5
            for hx in range(4):
                ets = []
                for mt in range(2):
                    ps = pbsel[0].next()
                    for c in range(4):
                        kc = 4 * hx + c
                        mm(ps, ps.ap[:, 0:n], KX, kx_t[:, kc, mt * 128:(mt + 1) * 128], CB[kc], CB[kc].ap[:, c0:c0 + n],
                           start=(c == 0), stop=(c == 3))
                    e_ = ET.next()
                    act(e_, e_.ap[:, 0:n], ps, ps.ap[:, 0:n], AF.Exp, scale=scl)
                    ets.append(e_)
                den = PSS.next()
                for mt in range(2):
                    mm(den, den.ap[:, 0:n], ONESB, onesb_t[:, :], ets[mt], ets[mt].ap[:, 0:n], start=(mt == 0), stop=(mt == 1))
                rd_ = RS.next()
                recip(rd_, rd_.ap[:, 0:n], den, den.ap[:, 0:n])
                for c in range(4):
                    kc = 4 * hx + c
                    ps = pbsel[0].next()
                    for mt in range(2):
                        mm(ps, ps.ap[:, 0:n], VX, vx_t[:, mt, kc * 128:(kc + 1) * 128], ets[mt], ets[mt].ap[:, 0:n],
                           start=(mt == 0), stop=(mt == 1))
                    tt(HB[kc], HB[kc].ap[:, c0:c0 + n], ps, ps.ap[:, 0:n], rd_, rd_.ap[:, 0:n], ALU.mult)
            ss = PSS.next()
            for mo in range(KC):
                ps = proj(w_xo[mo], HB, c0, n)
                y_chunk_out(ps, mo, c0, n, ss)
            postnorm_residual(ss, c0, n, 4)

        def ffn(T, c0, n):
            rmsnorm_to_bf(XA, c0, n, 5, HB)

            def conv3(ps, ch):
                cq = CQ.next()
                a = ACC.next()
                act(cq, cq.ap[:, 2:2 + n], ps, ps.ap[:, 0:n], AF.Copy)
                cp(cq, cq.ap[:, 0:2], FTAIL[ch], FTAIL[ch].ap)
                act(a, a.ap[:, 0:n], ps, ps.ap[:, 0:n], AF.Identity, bias=pc(P_FCB + ch), scale=pc(P_FCW + 3 * ch + 2), rd=[PRM])
                for j in range(2):
                    stt(a, a.ap[:, 0:n], cq, cq.ap[:, j:j + n], pc(P_FCW + 3 * ch + j), a, a.ap[:, 0:n],
                        ALU.mult, ALU.add, rd=[PRM])
                if T == TH:
                    ts(FTAIL[ch], FTAIL[ch].ap, cq, cq.ap[:, n:n + 2], pc(P_FLAG), None, ALU.mult, rd=[PRM])
                else:
                    cp(FTAIL[ch], FTAIL[ch].ap, cq, cq.ap[:, n:n + 2])
                return a

            if T == TH:
                for ch in range(2 * FC):
                    ps = proj(w_up[ch], HB, c0, n)
                    ts(FTAIL[ch], FTAIL[ch].ap, ps, ps.ap[:, n - 2:n], pc(P_FLAG), None, ALU.mult, rd=[PRM])
                return
            for m in range(FC):
                psa = proj(w_up[m], HB, c0, n)
                aa = conv3(psa, m)
                psb = proj(w_up[FC + m], HB, c0, n)
                ab = conv3(psb, FC + m)
                act(aa, aa.ap[:, 0:n], aa, aa.ap[:, 0:n], AF.Silu)
                tt(ACTB[m], ACTB[m].ap[:, 0:n], aa, aa.ap[:, 0:n], ab, ab.ap[:, 0:n], ALU.mult)
            ss = PSS.next()
            for mo in range(KC):
                ps = pbsel[0].next()
                proj(w_down[mo, :, 0:22, :], ACTB, 0, n, nk=22, k0=0, ps=ps, first=True, last=False)
                proj(w_down[mo, :, 22:43, :], ACTB, 0, n, nk=21, k0=22, ps=ps, first=False, last=True)
                y_chunk_out(ps, mo, c0, n, ss)
            postnorm_residual(ss, c0, n, 6)

        def stream():
            pending_B = [None]

            HBS = [HB, CB]

            def prologue(T, par, HBx):
                bg_t = bg_ts[par]; BG = BGs[par]
                for k in range(KC):
                    dma("sp", XA[k], XA[k].ap, None, xT[:, k, T * NT:(T + 1) * NT])
                yield
                rmsnorm_to_bf(XA, 0, NT, 0, HBx)
                yield
                for sub in range(NSUB):
                    p = PSS.next()
                    for k in range(KC):
                        mm(p, p.ap[:, 0:32], HBx[k], HBx[k].ap[:, sub * 128:(sub + 1) * 128], WBA, wba_t[:, k, :],
                           start=(k == 0), stop=(k == KC - 1))
                    act(BG, bg_t[:, 0, sub, :], p, p.ap[:, 0:16], AF.Sigmoid)
                    ts(BG, bg_t[:, 1, sub, :], BG, bg_t[:, 0, sub, :], -1.0, None, ALU.mult)
                    f = FS.next()
                    tt(f, f.ap[:, 0:16], p, p.ap[:, 16:32], PRM, prm[:, P_DTB:P_DTB + 16], ALU.add)
                    act(f, f.ap[:, 0:16], f, f.ap[:, 0:16], AF.Exp)
                    act(f, f.ap[:, 0:16], f, f.ap[:, 0:16], AF.Ln, bias=C(C_ONE, 1), rd=[CST])
                    tt(BG, bg_t[:, 2, sub, :], f, f.ap[:, 0:16], NGA, nga_t[:, :], ALU.mult)
                    gs_stage(sub, par)
                    yield
                if T == 0:
                    dbg_dump("bg", BG, bg_t[:, :, :, :])
                if late_casts:
                    ntl = max(1, TH - 2 - T)
                    k_ = len(late_casts) if T >= TH - 2 else (len(late_casts) + ntl - 1) // ntl
                    for (src_, mo_) in late_casts[:k_]:
                        cast_weights(src_, [mo_], after=BG)
                    del late_casts[:k_]

            def proj_gen_x(hg_, HBx):
                KT = KTS[hg_ % 2]
                KTB = KTBS[hg_ % 2]; VTB = VTBS[hg_ % 2]
                for hh in range(HG):
                    h = hg_ * HG + hh
                    ps = proj(w_in[16 + h], HBx, 0, NT)
                    conv4(ps, 16 + h, NT, 0, KT[hh], KT[hh].ap[:, 0:NT])
                    yield
                    ps = proj(w_in[32 + h], HBx, 0, NT)
                    conv4(ps, 32 + h, NT, 0, VTB[hh], VTB[hh].ap[:, 0:NT])
                    yield
                for hh in range(HG):
                    l2norm_to(KT[hh], KT[hh].ap[:, 0:NT], KTB[hh], KTB[hh].ap[:, 0:NT], NT, 1.0)
                    yield

            def chain2(a, b):
                yield from a
                yield from b

            carry = [None]
            pre_done = [False]

            for T in range(NTILES):
                is_main = T >= TH
                c0 = NT - HALO if T == TH else 0
                n = NT - c0
                par = (TH - T) % 2 if T < TH else 0
                HBx = HBS[par]
                if not pre_done[0]:
                    interleave([prologue(T, par, HBx)])
                    if T == 0:
                        ckpt(3)
                have_pg0 = pre_done[0]
                pre_done[0] = False

                prevB = None
                pend_gn = None

                def gnorm(hg_):
                    for hh in range(HG):
                        h = hg_ * HG + hh
                        o_ap = OR[hh].ap[:, c0:c0 + n]
                        q = SQB.next()
                        tt(q, q.ap[:, 0:n], OR[hh], o_ap, OR[hh], o_ap, ALU.mult)
                        ss = PSS.next()
                        mm(ss, ss.ap[:, 0:n], ONESB, onesb_t[:, :], q, q.ap[:, 0:n])
                        r = rstd_from(ss, n, 1.0 / 128)
                        ps = proj(w_in[48 + h], HB, c0, n)
                        zs = FS.next()
                        act(zs, zs.ap[:, 0:n], ps, ps.ap[:, 0:n], AF.Silu)
                        f = FS.next()
                        stt(f, f.ap[:, 0:n], OR[hh], o_ap, pc(P_GN), r, r.ap[:, 0:n], ALU.mult, ALU.mult, rd=[PRM])
                        tt(CB[h], CB[h].ap[:, c0:c0 + n], f, f.ap[:, 0:n], zs, zs.ap[:, 0:n], ALU.mult)
                        if T == TH + 1 and h == 0:
                            dbg_dump("or0", OR[0], OR[0].ap)

                def proj_gen(hg_):
                    KT = KTS[hg_ % 2]; QT = QTS[hg_ % 2]
                    KTB = KTBS[hg_ % 2]; VTB = VTBS[hg_ % 2]; QTB = QTBS[hg_ % 2]
                    for hh in range(HG):
                        h = hg_ * HG + hh
                        ps = proj(w_in[16 + h], HB, 0, NT)
                        conv4(ps, 16 + h, NT, 0, KT[hh], KT[hh].ap[:, 0:NT])
                        yield
                        ps = proj(w_in[32 + h], HB, 0, NT)
                        conv4(ps, 32 + h, NT, 0, VTB[hh], VTB[hh].ap[:, 0:NT])
                        yield
                        if is_main:
                            if T == TH:
                                memset(QTB[hh], QTB[hh].ap, 0.0)
                            ps = proj(w_in[h], HB, c0, n)
                            conv4(ps, h, n, c0, QT[hh], QT[hh].ap[:, c0:c0 + n])
                            yield
                    for hh in range(HG):
                        l2norm_to(KT[hh], KT[hh].ap[:, 0:NT], KTB[hh], KTB[hh].ap[:, 0:NT], NT, 1.0)
                        yield
                        if is_main:
                            l2norm_to(QT[hh], QT[hh].ap[:, c0:c0 + n], QTB[hh], QTB[hh].ap[:, c0:c0 + n], n, 128.0 ** -0.5)
                            yield

                NG = NH // HG
                if not have_pg0:
                    interleave([proj_gen(0) if is_main else proj_gen_x(0, HBx)])

                if not is_main:
                    prevS = carry[0]
                    carry[0] = None
                    for hg in range(NG):
                        sts_ = [dict() for _ in range(NSUB)]
                        gens = [gdn_prep(hg, sub, False, sts_[sub], par) for sub in range(NSUB)]
                        if prevS is not None:
                            gens.append(prevS)
                        if hg + 1 < NG:
                            gens.append(proj_gen_x(hg + 1, HBx))
                        elif T + 1 < TH:
                            parn = (TH - (T + 1)) % 2
                            gens.append(chain2(prologue(T + 1, parn, HBS[parn]), proj_gen_x(0, HBS[parn])))
                            pre_done[0] = True
                        interleave(gens)
                        prevS = chain2(gdn_state(hg, 0, False, sts_[0], par), gdn_state(hg, 1, False, sts_[1], par))
                    if T + 1 < TH:
                        carry[0] = prevS
                    else:
                        interleave([prevS])
                for hg in (range(NG) if is_main else []):
                    for sub in range(NSUB):
                        full = is_main and (T > TH or sub == NSUB - 1)
                        st_ = dict()
                        gens = [gdn_prep(hg, sub, full, st_)] + (prevB if prevB else [])
                        if sub == 0 and hg + 1 < NG:
                            gens.append(proj_gen(hg + 1))
                        interleave(gens)
                        if pend_gn is not None:
                            gnorm(pend_gn)
                            pend_gn = None
                        prevB = [gdn_state(hg, sub, full, st_)]
                        if sub == NSUB - 1 and is_main:
                            pend_gn = hg
                if prevB:
                    interleave(prevB)
                prevB = None
                if pend_gn is not None:
                    gnorm(pend_gn)
                    pend_gn = None
                if T == NTILES - 1:
                    dbg_dump("s0", S[0], S[0].ap)
                if T == 0:
                    ckpt(7)
                if T == TH - 1:
                    ckpt(8)
                if T == TH:
                    ckpt(9)
                if is_main:
                    if T == TH + 1:
                        for k in range(KC):
                            pass
                    pbsel[0] = PB_MAIN
                    mixer_rest(T, c0, n)
                    if T == TH:
                        ckpt(10)
                    if T == TH + 1:
                        dbg_dump("x1", XA[0], XA[0].ap)
                    xattn(T, c0, n)
                    if T == TH:
                        ckpt(11)
                    if T == TH + 1:
                        dbg_dump("x2", XA[0], XA[0].ap)
                    ffn(T, c0, n)
                    pbsel[0] = PB_STREAM
                    if T > TH:
                        o0 = (T - TH - 1) * NT
                        for k in range(KC):
                            dma("sp", None, outT[:, k, o0:o0 + NT], XA[k], XA[k].ap)
        try:
            ckpt(1)
            xattn_kv()
            ckpt(2)
            stream()
        except _Stop:
            pass
        emit_program(nc, P, es)
    return nc


def _tile_w(W):
    K, M = W.shape
    return np.ascontiguousarray(W.reshape(K // 128, 128, M // 128, 128).transpose(2, 1, 0, 3))


def _fm(v, nch):
    return np.ascontiguousarray(np.asarray(v, np.float32).reshape(nch, 128).T)


def make_consts():
    c = np.zeros((128, NCONST), np.float32)
    i = np.arange(128)
    blk = (i[:, None] // 64) == (i[None, :] // 64)
    for r in range(4):
        c[:, C_ID + 128 * r:C_ID + 128 * r + 128] = np.eye(128)
        c[:, C_UBD + 128 * r:C_UBD + 128 * r + 128] = blk & (i[:, None] <= i[None, :])
        c[:, C_MSTR + 128 * r:C_MSTR + 128 * r + 128] = blk & (i[:, None] > i[None, :])
    c[:, C_ONES:C_ONES + 128] = 1.0
    c[:, C_OBD:C_OBD + 128] = blk
    c[:, C_BSEL] = i < 64
    c[:, C_BSEL + 1] = i >= 64
    c[:, C_EPS] = EPS
    c[:, C_ONE] = 1.0
    return c


def prep_inputs(cfg, inp):
    SEQ, OWN = cfg.SEQ, cfg.OWN
    f = lambda a: np.asarray(a, np.float32)
    w_in_full = f(inp["w_in"])[0]
    cols = np.concatenate([np.arange(0, 8192), np.arange(8224, 13344)])
    shared = {
        "consts": make_consts(),
        "w_in": _tile_w(w_in_full[:, cols]),
        "wba": np.ascontiguousarray(w_in_full[:, 8192:8224].reshape(KC, 128, 32).transpose(1, 0, 2)),
        "poolw": np.ascontiguousarray(f(inp["pool_w"])[0].reshape(4, 2, 128, 256).transpose(2, 0, 1, 3)),
        "w_a": _tile_w(f(inp["w_branch_a"])[0]),
        "w_b": _tile_w(f(inp["w_branch_b"])[0]),
        "w_mix": _tile_w(f(inp["w_mix_out"])[0]),
        "w_xq": _tile_w(f(inp["w_xq"])[0]),
        "w_xkv": _tile_w(f(inp["w_xkv"])[0]),
        "w_xo": _tile_w(f(inp["w_xo"])[0]),
        "w_up": _tile_w(f(inp["w_up"])[0]),
        "w_down": _tile_w(f(inp["w_down"])[0]),
    }
    prm = np.zeros((128, NPRM), np.float32)
    for gi, nm in enumerate(["mix_pre_norm", "mix_post_norm", "xa_pre_norm", "mem_norm", "xa_post_norm",
                             "ffn_pre_norm", "ffn_post_norm"]):
        prm[:, P_GAIN + 16 * gi:P_GAIN + 16 * gi + 16] = _fm(f(inp[nm])[0], 16)
    cq = f(inp["conv_qkv"])[0]
    prm[:, P_CONVQ:P_CONVQ + 192] = cq.reshape(4, 48, 128).transpose(2, 1, 0).reshape(128, 192)
    fw = f(inp["ffn_conv_w"])[0]
    prm[:, P_FCW:P_FCW + 258] = fw.reshape(3, 86, 128).transpose(2, 1, 0).reshape(128, 258)
    prm[:, P_FCB:P_FCB + 86] = _fm(f(inp["ffn_conv_b"])[0], 86)
    prm[:, P_PSC:P_PSC + 8] = _fm(f(inp["pool_scale"])[0], 8)
    prm[:, P_GN] = f(inp["gdn_norm"])[0]
    prm[:, P_ALOG:P_ALOG + 16] = f(inp["a_log"])[0][None, :]
    prm[:, P_DTB:P_DTB + 16] = f(inp["dt_bias"])[0][None, :]
    x = f(inp["x"])
    mem = f(inp["mem"])
    in_maps = []
    for c in range(8):
        b, j = c // 4, c % 4
        end = OWN * (j + 1)
        start = end - SEQ
        xs = np.zeros((SEQ, D), np.float32)
        if start < 0:
            xs[-start:] = x[b, 0:end]
        else:
            xs[:] = x[b, start:end]
        xTc = np.ascontiguousarray(xs.T.reshape(KC, 128, SEQ).transpose(1, 0, 2))
        memTc = np.ascontiguousarray(mem[b].T.reshape(KC, 128, MEM).transpose(1, 0, 2))
        p = prm.copy()
        p[:, P_FLAG] = 0.0 if j == 0 else 1.0
        own_start = OWN * j
        t = own_start + np.arange(16)
        for g, win in enumerate((2, 4, 8, 16)):
            p[:, P_INVC + 16 * g:P_INVC + 16 * g + 16] = (1.0 / np.minimum(t + 1, win))[None, :]
        m = dict(shared)
        m.update({"xT": xTc, "memT": memTc, "prm": p})
        in_maps.append(m)
    return in_maps


_NC_CACHE = {}


def run(cfg, inp, dbg=()):
    key = (cfg.SEQ, cfg.NT, tuple(dbg))
    if key not in _NC_CACHE:
        _NC_CACHE[key] = build(cfg, dbg)
    nc = _NC_CACHE[key]
    in_maps = prep_inputs(cfg, inp)
    res = run_bass_kernel_spmd(nc, in_maps, core_ids=list(range(8)))
    B = 2
    out = np.zeros((B, cfg.SEQ, D), np.float32)
    for c in range(8):
        b, j = c // 4, c % 4
        o = res.results[c]["outT"]
        out[b, cfg.OWN * j:cfg.OWN * (j + 1), :] = o.transpose(2, 1, 0).reshape(cfg.OWN, D)
    return out, res


def kernel(**inputs):
    cfg = Cfg(SEQ=8192, NT=256)
    out, _ = run(cfg, inputs)
    return out
```
